# Optimizing a Trainium2 kernel written in Bass

```python
import math
import jax, jax.numpy as jnp
from jax import lax
import numpy as np

D_MODEL = 1024
BATCH = 16
SEQ = 4096
DEPTH = 4

GRID_W = 64
CTX_LEN = 256
Q_BLOCK = 128
ROPE_BASE = 10000.0
LN_EPS = 1e-6

MLA_HEADS = 6
MLA_NOPE = 64
MLA_ROPE = 32
MLA_V = 64
MLA_Q_RANK = 256
MLA_KV_RANK = 128

DIFF_HEADS = 4
DIFF_DIM = 48
DIFF_V = 2 * DIFF_DIM

HY_CH = 256
HY_ORDER = 2
HY_BANDS = 16
HY_EMB = 1 + 2 * HY_BANDS
HY_FILTER_HIDDEN = 64
HY_TARGET = 1e-2
HY_FAST_DECAY_PCT = 0.3
HY_SLOW_DECAY_PCT = 1.5
HY_MIN_RATE = -math.log(HY_TARGET) / HY_SLOW_DECAY_PCT
HY_MAX_RATE = -math.log(HY_TARGET) / HY_FAST_DECAY_PCT

D_FF = 4 * D_MODEL
N_MOD = 6

D_MIX = MLA_HEADS * MLA_V + DIFF_HEADS * DIFF_V + HY_CH
IN_MLA = MLA_Q_RANK + MLA_KV_RANK + MLA_ROPE
IN_DIFF = 2 * DIFF_HEADS * 2 * DIFF_DIM + DIFF_HEADS * DIFF_V
IN_HY = (HY_ORDER + 1) * HY_CH
D_IN = IN_MLA + IN_DIFF + IN_HY

kernel_name = 'hybrid_mla_diffattn_hyena_dit'

F32 = jnp.float32


def _layer_norm(x):
    xf = x.astype(F32)
    mu = jnp.mean(xf, -1, keepdims=True)
    var = jnp.mean(jnp.square(xf - mu), -1, keepdims=True)
    return ((xf - mu) * lax.rsqrt(var + LN_EPS)).astype(x.dtype)


def _rms_norm(x, g):
    xf = x.astype(F32)
    ms = jnp.mean(jnp.square(xf), -1, keepdims=True)
    return (xf * lax.rsqrt(ms + LN_EPS)).astype(x.dtype) * g


def _modulate(x, shift, scale):
    return _layer_norm(x) * (1 + scale) + shift


def _post_norm(x, g, b):
    return _layer_norm(x) * g + b


def _rope_1d(x, pos):
    h = x.shape[-1]
    inv = ROPE_BASE ** (-(jnp.arange(h // 2, dtype=F32) * (2.0 / h)))
    ang = pos.astype(F32)[:, None] * inv[None, :]
    cos = jnp.cos(ang).astype(x.dtype)
    sin = jnp.sin(ang).astype(x.dtype)
    x1, x2 = x[..., : h // 2], x[..., h // 2:]
    return jnp.concatenate([x1 * cos - x2 * sin, x2 * cos + x1 * sin], -1)


def _rope_2d(x, row, col):
    d = x.shape[-1]
    return jnp.concatenate([_rope_1d(x[..., : d // 2], row), _rope_1d(x[..., d // 2:], col)], -1)


def _merge_heads(o):
    b, h, l, d = o.shape
    return o.transpose(0, 2, 1, 3).reshape(b, l, h * d)


def _blocked_attention(q, k, v, scale):
    b, h, l, dk = q.shape
    nb = l // Q_BLOCK
    qb = q.reshape(b, h, nb, Q_BLOCK, dk).transpose(2, 0, 1, 3, 4)

    def one(qi):
        s = jnp.einsum('bhqd,bhkd->bhqk', qi, k).astype(F32) * scale
        p = jax.nn.softmax(s, axis=-1).astype(v.dtype)
        return jnp.einsum('bhqk,bhkd->bhqd', p, v)

    o = lax.map(one, qb)
    return o.transpose(1, 2, 0, 3, 4).reshape(b, h, l, v.shape[-1])


def _blocked_diff_attention(q, k, v, lam, scale):
    b, h, m, l, d = q.shape
    nb = l // Q_BLOCK
    qb = q.reshape(b, h, m, nb, Q_BLOCK, d).transpose(3, 0, 1, 2, 4, 5)

    def one(qi):
        s = jnp.einsum('bhmqd,bhmkd->bhmqk', qi, k).astype(F32) * scale
        p = jax.nn.softmax(s, axis=-1)
        a = (p[:, :, 0] - lam * p[:, :, 1]).astype(v.dtype)
        return jnp.einsum('bhqk,bhkd->bhqd', a, v)

    o = lax.map(one, qb)
    return o.transpose(1, 2, 0, 3, 4).reshape(b, h, l, v.shape[-1])


def _mla_project(p, q_norm, w_q_up, kv_norm, w_kv_up):
    b, l, _ = p.shape
    cq = p[..., :MLA_Q_RANK]
    ckv = p[..., MLA_Q_RANK:MLA_Q_RANK + MLA_KV_RANK]
    k_rope = p[..., MLA_Q_RANK + MLA_KV_RANK:IN_MLA]
    q = (_rms_norm(cq, q_norm) @ w_q_up).reshape(b, l, MLA_HEADS, MLA_NOPE + MLA_ROPE).transpose(0, 2, 1, 3)
    kv = (_rms_norm(ckv, kv_norm) @ w_kv_up).reshape(b, l, MLA_HEADS, MLA_NOPE + MLA_V).transpose(0, 2, 1, 3)
    return q, kv[..., :MLA_NOPE], kv[..., MLA_NOPE:], k_rope


def _mla_keys(k_nope, k_rope):
    b, h, l, _ = k_nope.shape
    return jnp.concatenate([k_nope, jnp.broadcast_to(k_rope[:, None], (b, h, l, MLA_ROPE))], -1)


def _mla(px, pc, row, col, need_ctx, q_norm, w_q_up, kv_norm, w_kv_up):
    qx, knx, vx, krx = _mla_project(px, q_norm, w_q_up, kv_norm, w_kv_up)
    qc, knc, vc, krc = _mla_project(pc, q_norm, w_q_up, kv_norm, w_kv_up)
    qx = jnp.concatenate([qx[..., :MLA_NOPE], _rope_2d(qx[..., MLA_NOPE:], row, col)], -1)
    kx = _mla_keys(knx, _rope_2d(krx, row, col))
    kc = _mla_keys(knc, krc)
    scale = (MLA_NOPE + MLA_ROPE) ** -0.5
    ox = _merge_heads(_blocked_attention(qx, jnp.concatenate([kc, kx], 2), jnp.concatenate([vc, vx], 2), scale))
    oc = _merge_heads(_blocked_attention(qc, kc, vc, scale)) if need_ctx else None
    return ox, oc


def _diff_project(p):
    b, l, _ = p.shape
    n = DIFF_HEADS * 2 * DIFF_DIM
    q = p[..., :n].reshape(b, l, DIFF_HEADS, 2, DIFF_DIM).transpose(0, 2, 3, 1, 4)
    k = p[..., n:2 * n].reshape(b, l, DIFF_HEADS, 2, DIFF_DIM).transpose(0, 2, 3, 1, 4)
    v = p[..., 2 * n:].reshape(b, l, DIFF_HEADS, DIFF_V).transpose(0, 2, 1, 3)
    return q, k, v


def _diff(px, pc, row, col, lam_init, need_ctx, lam_p, subln):
    qx, kx, vx = _diff_project(px)
    qc, kc, vc = _diff_project(pc)
    qx = _rope_2d(qx, row, col)
    kx = _rope_2d(kx, row, col)
    lp = lam_p.astype(F32)
    lam = jnp.exp(jnp.sum(lp[0] * lp[1])) - jnp.exp(jnp.sum(lp[2] * lp[3])) + lam_init
    scale = DIFF_DIM ** -0.5

    def finish(o):
        return _merge_heads(_rms_norm(o, subln) * (1.0 - lam_init))

    ox = finish(_blocked_diff_attention(qx, jnp.concatenate([kc, kx], 3), jnp.concatenate([vc, vx], 2), lam, scale))
    oc = finish(_blocked_diff_attention(qc, kc, vc, lam, scale)) if need_ctx else None
    return ox, oc


def _short_conv(u, w, b):
    l = u.shape[1]
    up = jnp.pad(u, ((0, 0), (1, 1), (0, 0)))
    return up[:, :l] * w[0] + up[:, 1:l + 1] * w[1] + up[:, 2:] * w[2] + b


def _hyena_filters(l, fw1, fb1, fw2, fb2, fw3, fb3):
    t = jnp.arange(l, dtype=F32)
    bands = jnp.arange(1, HY_BANDS + 1, dtype=F32)
    ang = (2.0 * math.pi / l) * t[:, None] * bands[None, :]
    emb = jnp.concatenate([(t / l)[:, None], jnp.cos(ang), jnp.sin(ang)], -1)
    h = jnp.sin(emb @ fw1.astype(F32) + fb1.astype(F32))
    h = jnp.sin(h @ fw2.astype(F32) + fb2.astype(F32))
    h = (h @ fw3.astype(F32) + fb3.astype(F32)).reshape(l, HY_ORDER, 2, HY_CH)
    rates = jnp.linspace(HY_MIN_RATE, HY_MAX_RATE, HY_CH, dtype=F32)
    window = jnp.exp(-(t / l)[:, None] * rates[None, :])
    h = h * window[:, None, None, :]
    fwd, bwd = h[:, :, 0], h[:, :, 1]
    kern = jnp.concatenate([fwd, jnp.zeros((1, HY_ORDER, HY_CH), F32), bwd[1:][::-1]], 0)
    kern = kern / jnp.sum(jnp.abs(kern), axis=0, keepdims=True)
    return jnp.fft.rfft(kern, axis=0)


def _fftconv(z, kf, bias):
    l = z.shape[1]
    zf = z.astype(F32)
    y = jnp.fft.irfft(jnp.fft.rfft(zf, n=2 * l, axis=1) * kf[None], n=2 * l, axis=1)[:, :l]
    return (y + zf * bias.astype(F32)).astype(z.dtype)


def _hyena(u, conv_w, conv_b, fw1, fb1, fw2, fb2, fw3, fb3, hbias):
    l = u.shape[1]
    u = _short_conv(u, conv_w, conv_b)
    v, x1, x2 = jnp.split(u, 3, axis=-1)
    kf = _hyena_filters(l, fw1, fb1, fw2, fb2, fw3, fb3)
    z = x1 * _fftconv(v, kf[:, 0], hbias[0])
    return x2 * _fftconv(z, kf[:, 1], hbias[1])


def _mixer(px, pc, row, col, lam_init, need_ctx, q_norm, w_q_up, kv_norm, w_kv_up, lam_p, subln,
           conv_w, conv_b, fw1, fb1, fw2, fb2, fw3, fb3, hbias):
    s1 = IN_MLA
    s2 = IN_MLA + IN_DIFF
    mla_x, mla_c = _mla(px[..., :s1], pc[..., :s1], row, col, need_ctx, q_norm, w_q_up, kv_norm, w_kv_up)
    dif_x, dif_c = _diff(px[..., s1:s2], pc[..., s1:s2], row, col, lam_init, need_ctx, lam_p, subln)
    hy_x = _hyena(px[..., s2:], conv_w, conv_b, fw1, fb1, fw2, fb2, fw3, fb3, hbias)
    ox = jnp.concatenate([mla_x, dif_x, hy_x], -1)
    oc = None
    if need_ctx:
        hy_c = _hyena(pc[..., s2:], conv_w, conv_b, fw1, fb1, fw2, fb2, fw3, fb3, hbias)
        oc = jnp.concatenate([mla_c, dif_c, hy_c], -1)
    return ox, oc


def _ffn(h, w1, b1, w2, b2):
    return jnp.square(jax.nn.relu(h @ w1 + b1)) @ w2 + b2


def setup_inputs(seed: int = 0) -> dict:
    key = jax.random.key(seed)
    ks = iter(jax.random.split(key, 40))
    L = DEPTH
    beta = (8.0 * DEPTH) ** -0.25

    def nrm(shape, scale):
        return jax.random.normal(next(ks), shape, F32) * scale

    def gain(shape):
        return 1.0 + nrm(shape, 0.02)

    return {
        'x': nrm((BATCH, SEQ, D_MODEL), 1.0),
        'c': nrm((BATCH, D_MODEL), 1.0),
        'ctx': nrm((BATCH, CTX_LEN, D_MODEL), 1.0),
        'c_ctx': nrm((D_MODEL,), 1.0),
        'w_mod': nrm((L, D_MODEL, N_MOD * D_MODEL), D_MODEL ** -0.5),
        'b_mod': nrm((L, N_MOD * D_MODEL), 0.02),
        'w_in': nrm((L, D_MODEL, D_IN), D_MODEL ** -0.5),
        'mla_q_norm': gain((L, MLA_Q_RANK)),
        'w_q_up': nrm((L, MLA_Q_RANK, MLA_HEADS * (MLA_NOPE + MLA_ROPE)), MLA_Q_RANK ** -0.5),
        'mla_kv_norm': gain((L, MLA_KV_RANK)),
        'w_kv_up': nrm((L, MLA_KV_RANK, MLA_HEADS * (MLA_NOPE + MLA_V)), MLA_KV_RANK ** -0.5),
        'diff_lambda': nrm((L, 4, DIFF_DIM), 0.1),
        'diff_subln': gain((L, DIFF_V)),
        'hy_conv_w': nrm((L, 3, IN_HY), 3 ** -0.5),
        'hy_conv_b': nrm((L, IN_HY), 0.02),
        'hy_fw1': nrm((L, HY_EMB, HY_FILTER_HIDDEN), HY_EMB ** -0.5),
        'hy_fb1': nrm((L, HY_FILTER_HIDDEN), 0.02),
        'hy_fw2': nrm((L, HY_FILTER_HIDDEN, HY_FILTER_HIDDEN), HY_FILTER_HIDDEN ** -0.5),
        'hy_fb2': nrm((L, HY_FILTER_HIDDEN), 0.02),
        'hy_fw3': nrm((L, HY_FILTER_HIDDEN, HY_ORDER * 2 * HY_CH), HY_FILTER_HIDDEN ** -0.5),
        'hy_fb3': nrm((L, HY_ORDER * 2 * HY_CH), 0.02),
        'hy_bias': nrm((L, HY_ORDER, HY_CH), 0.1),
        'w_out': nrm((L, D_MIX, D_MODEL), beta * D_MIX ** -0.5),
        'b_out': nrm((L, D_MODEL), 0.02),
        'ln1_g': gain((L, D_MODEL)),
        'ln1_b': nrm((L, D_MODEL), 0.02),
        'w_ff1': nrm((L, D_MODEL, D_FF), D_MODEL ** -0.5),
        'b_ff1': nrm((L, D_FF), 0.02),
        'w_ff2': nrm((L, D_FF, D_MODEL), beta * D_FF ** -0.5),
        'b_ff2': nrm((L, D_MODEL), 0.02),
        'ln2_g': gain((L, D_MODEL)),
        'ln2_b': nrm((L, D_MODEL), 0.02),
    }


def reference(x, c, ctx, c_ctx, w_mod, b_mod, w_in, mla_q_norm, w_q_up, mla_kv_norm, w_kv_up,
              diff_lambda, diff_subln, hy_conv_w, hy_conv_b, hy_fw1, hy_fb1, hy_fw2, hy_fb2, hy_fw3, hy_fb3,
              hy_bias, w_out, b_out, ln1_g, ln1_b, w_ff1, b_ff1, w_ff2, b_ff2, ln2_g, ln2_b):
    seq = x.shape[1]
    rows = seq // GRID_W
    row = jnp.repeat(jnp.arange(rows), GRID_W)
    col = jnp.tile(jnp.arange(GRID_W), rows)
    alpha = (2.0 * DEPTH) ** 0.25
    silu_c = jax.nn.silu(c)
    silu_cc = jax.nn.silu(c_ctx)
    for layer in range(DEPTH):
        need_ctx = layer < DEPTH - 1
        lam_init = 0.8 - 0.6 * math.exp(-0.3 * layer)
        mod = silu_c @ w_mod[layer] + b_mod[layer]
        mod_c = silu_cc @ w_mod[layer] + b_mod[layer]
        sh1, sc1, g1, sh2, sc2, g2 = jnp.split(mod[:, None, :], N_MOD, axis=-1)
        csh1, csc1, cg1, csh2, csc2, cg2 = jnp.split(mod_c, N_MOD, axis=-1)

        px = _modulate(x, sh1, sc1) @ w_in[layer]
        pc = _modulate(ctx, csh1, csc1) @ w_in[layer]
        ox, oc = _mixer(px, pc, row, col, lam_init, need_ctx,
                        mla_q_norm[layer], w_q_up[layer], mla_kv_norm[layer], w_kv_up[layer],
                        diff_lambda[layer], diff_subln[layer],
                        hy_conv_w[layer], hy_conv_b[layer], hy_fw1[layer], hy_fb1[layer],
                        hy_fw2[layer], hy_fb2[layer], hy_fw3[layer], hy_fb3[layer], hy_bias[layer])
        x = _post_norm(alpha * x + g1 * (ox @ w_out[layer] + b_out[layer]), ln1_g[layer], ln1_b[layer])
        x = _post_norm(alpha * x + g2 * _ffn(_modulate(x, sh2, sc2), w_ff1[layer], b_ff1[layer], w_ff2[layer], b_ff2[layer]),
                       ln2_g[layer], ln2_b[layer])
        if need_ctx:
            ctx = _post_norm(alpha * ctx + cg1 * (oc @ w_out[layer] + b_out[layer]), ln1_g[layer], ln1_b[layer])
            ctx = _post_norm(alpha * ctx + cg2 * _ffn(_modulate(ctx, csh2, csc2), w_ff1[layer], b_ff1[layer], w_ff2[layer], b_ff2[layer]),
                             ln2_g[layer], ln2_b[layer])
    return x
```

```python
import math
from contextlib import ExitStack

import numpy as np
import concourse.bass as bass
import concourse.mybir as mybir
from concourse.bass_utils import run_bass_kernel_spmd

F32 = mybir.dt.float32
BF16 = mybir.dt.bfloat16
AF = mybir.ActivationFunctionType
ALU = mybir.AluOpType

ENGS = ("pe", "act", "dve", "pool", "sp")
DMAQ = ("sp", "pool", "act")

LN_EPS = 1e-6
ROPE_BASE = 10000.0
GRID_W = 64
D = 1024
DIN = 2336
DFF = 4096
MAGIC = 12582912.0
SAME_ENGINE_SYNC = True
HOIST_SP = True
HOIST_SET = ("init", "mod", "A", "B", "C", "H", "DE1", "E2")


class _Rec:
    def __init__(self):
        self.call = None

    def __getattr__(self, name):
        def f(*a, **k):
            self.call = (name, a, k)
            return self
        return f


class Prog:
    RING = 12

    def __init__(self, nc, es, same_engine_sync=True):
        self.nc = nc
        self.same_engine_sync = same_engine_sync
        self.esem = {e: es.enter_context(nc.semaphore("s_" + e)) for e in ENGS}
        self.rings = {e: [es.enter_context(nc.semaphore("d_%s_%d" % (e, j))) for j in range(self.RING)]
                      for e in DMAQ}
        self.fsem = es.enter_context(nc.semaphore("fence"))
        self.cnt = {e: 0 for e in ENGS}
        self.dcnt = {e: 0 for e in DMAQ}
        self.waited = {e: {} for e in ENGS}
        self.nfence = 0
        self.n_emitted = 0
        self.hoist_sp = HOIST_SP
        self.defer = None
        self._reset()

    def _reset(self):
        self.ops = []
        self.last_writer = {}
        self.readers = {}

    def op(self, eng, fn, reads=(), writes=(), dma=False):
        rec = _Rec()
        fn(rec)
        assert rec.call is not None
        if self.defer is not None:
            self.defer.append((eng, rec.call, tuple(reads), tuple(writes), dma))
            return None
        return self.add(eng, rec.call, reads, writes, dma)

    def add(self, eng, call, reads=(), writes=(), dma=False):
        i = len(self.ops)
        deps = set()
        for k in reads:
            lw = self.last_writer.get(k)
            if lw is not None:
                deps.add(lw)
        for k in writes:
            lw = self.last_writer.get(k)
            if lw is not None:
                deps.add(lw)
            r = self.readers.get(k)
            if r:
                deps.update(r["eng"].values())
                deps.update(r["dma"])
        for k in reads:
            r = self.readers.setdefault(k, {"eng": {}, "dma": []})
            if dma:
                r["dma"].append(i)
            else:
                r["eng"][eng] = i
        for k in writes:
            self.last_writer[k] = i
            self.readers[k] = {"eng": {}, "dma": []}
        deps.discard(i)
        self.ops.append(dict(eng=eng, call=call, deps=deps, dma=dma))
        return i

    def _skip(self, od, o):
        if od["dma"] or o["dma"]:
            return False
        if od["eng"] == o["eng"]:
            if o["eng"] == "pe":
                return True
            if not self.same_engine_sync:
                return True
        return False

    def flush(self):
        nc = self.nc
        ops = self.ops
        if not ops:
            return
        need = [False] * len(ops)
        for o in ops:
            for d in o["deps"]:
                od = ops[d]
                if od["dma"] or self._skip(od, o):
                    continue
                need[d] = True
        per = {e: [i for i, o in enumerate(ops) if o["eng"] == e] for e in ENGS}
        if self.hoist_sp:
            per["sp"].sort(key=lambda i: (max(ops[i]["deps"]) if ops[i]["deps"] else -1, i))
        for e in ENGS:
            for i in reversed(per[e]):
                if not ops[i]["dma"]:
                    need[i] = True
                    break
        for e in ENGS:
            for i in per[e]:
                o = ops[i]
                if o["dma"]:
                    j = self.dcnt[e]
                    self.dcnt[e] += 1
                    o["sem"] = self.rings[e][j % self.RING]
                    o["val"] = 16 * (j // self.RING + 1)
                    o["ringprev"] = (o["sem"], o["val"] - 16) if j >= self.RING else None
                elif need[i]:
                    self.cnt[e] += 1
                    o["sem"] = self.esem[e]
                    o["val"] = self.cnt[e]
        final = {}
        for e in DMAQ:
            n = self.dcnt[e]
            for j in range(min(n, self.RING)):
                final[self.rings[e][j]] = 16 * ((n - 1 - j) // self.RING + 1)
        for e in ENGS:
            if self.cnt[e] > 0:
                final[self.esem[e]] = self.cnt[e]
        fence_in = self.nfence
        self.nfence += 1

        def run(e, engobj):
            waited = self.waited[e]
            if fence_in > 0 and e != "sp":
                engobj.wait_ge(self.fsem, fence_in)
            for i in per[e]:
                o = ops[i]
                want = {}
                for d in o["deps"]:
                    od = ops[d]
                    if self._skip(od, o):
                        continue
                    s, v = od["sem"], od["val"]
                    if want.get(s, 0) < v:
                        want[s] = v
                if o["dma"] and o["ringprev"] is not None:
                    s, v = o["ringprev"]
                    if want.get(s, 0) < v:
                        want[s] = v
                for s, v in want.items():
                    if waited.get(s, 0) < v:
                        engobj.wait_ge(s, v)
                        waited[s] = v
                name, a, k = o["call"]
                inst = getattr(engobj, name)(*a, **k)
                self.n_emitted += 1
                if o["dma"]:
                    inst.then_inc(o["sem"], 16)
                elif need[i]:
                    inst.then_inc(o["sem"], 1)
            if e == "sp":
                for s, v in final.items():
                    if waited.get(s, 0) < v:
                        engobj.wait_ge(s, v)
                        waited[s] = v
                engobj.sem_inc(self.fsem, 1)

        with nc.Block() as block:
            @block.tensor
            def _(t):
                run("pe", t)

            @block.scalar
            def _(t):
                run("act", t)

            @block.vector
            def _(t):
                run("dve", t)

            @block.gpsimd
            def _(t):
                run("pool", t)

            @block.sync
            def _(t):
                run("sp", t)
        self._reset()


class Cfg:
    def __init__(self, SEQ=4096, CTX=256, DEPTH=4, NB=2, debug=False, stop_after=None):
        self.SEQ, self.CTX, self.DEPTH, self.NB = SEQ, CTX, DEPTH, NB
        self.T = SEQ + CTX
        self.NT = self.T // 128
        self.NG = self.T // 256
        self.debug = debug
        self.stop_after = stop_after


WSPEC = [
    ("w_mod", (D, 6 * D)), ("b_mod", (6 * D,)), ("w_in", (D, DIN)), ("mla_q_norm", (256,)),
    ("w_q_up", (256, 576)), ("mla_kv_norm", (128,)), ("w_kv_up", (128, 768)), ("diff_lambda", (4, 48)),
    ("diff_subln", (96,)), ("hy_conv_w", (3, 768)), ("hy_conv_b", (768,)), ("hy_fw1", (33, 64)),
    ("hy_fb1", (64,)), ("hy_fw2", (64, 64)), ("hy_fb2", (64,)), ("hy_fw3", (64, 1024)), ("hy_fb3", (1024,)),
    ("hy_bias", (2, 256)), ("w_out", (D, D)), ("b_out", (D,)), ("ln1_g", (D,)), ("ln1_b", (D,)),
    ("w_ff1", (D, DFF)), ("b_ff1", (DFF,)), ("w_ff2", (DFF, D)), ("b_ff2", (D,)), ("ln2_g", (D,)), ("ln2_b", (D,)),
]


def host_consts(cfg):
    out = {}
    out["idn"] = np.eye(128, dtype=np.float32)
    T, CTX, SEQ = cfg.T, cfg.CTX, cfg.SEQ
    t = np.arange(SEQ)
    row = (t // GRID_W).astype(np.float32)
    col = (t % GRID_W).astype(np.float32)

    def rope_tab(h):
        inv = (ROPE_BASE ** (-(np.arange(h // 2, dtype=np.float32) * (2.0 / h)))).astype(np.float32)
        tab = np.zeros((T, 2, 2, h // 2), np.float32)
        tab[:, 0] = 1.0
        for a, pos in enumerate((row, col)):
            ang = (pos[:, None] * inv[None, :]).astype(np.float32)
            tab[CTX:, 0, a] = np.cos(ang)
            tab[CTX:, 1, a] = np.sin(ang)
        return tab.reshape(T, 2 * 2 * (h // 2))

    out["rope_m"] = rope_tab(16)
    out["rope_d"] = rope_tab(24)
    rates = np.linspace(-math.log(1e-2) / 1.5, -math.log(1e-2) / 0.3, 256, dtype=np.float32)
    out["negrates"] = (-rates).astype(np.float32)
    for l in sorted({cfg.SEQ, cfg.CTX}):
        tt = np.arange(l, dtype=np.float32)
        bands = np.arange(1, 17, dtype=np.float32)
        ang = (np.float32(2.0 * math.pi / l) * tt[:, None] * bands[None, :]).astype(np.float32)
        emb = np.concatenate([(tt / np.float32(l))[:, None], np.cos(ang), np.sin(ang)], -1).astype(np.float32)
        out["embT_%d" % l] = np.ascontiguousarray(emb.T)
        out["tau_%d" % l] = np.ascontiguousarray((tt / np.float32(l)).reshape(l // 128, 128).T)
        f = np.arange(l, dtype=np.float64)
        a = np.arange(l // 128, dtype=np.float64)
        b = np.arange(128, dtype=np.float64)
        w = 2.0 * np.pi / (4.0 * l)
        A = w * (2 * f[None, :] + 1) * (128.0 * a[:, None])
        B = w * (2 * f[None, :] + 1) * b[:, None]
        out["fa_%d" % l] = np.stack([np.cos(A), np.sin(A)], 1).astype(np.float32)
        out["fb_%d" % l] = np.stack([np.cos(B), np.sin(B)], 1).astype(np.float32)
        A2 = w * (256.0 * a[:, None]) * f[None, :]
        B2 = w * (2 * b[:, None] + 1) * f[None, :]
        out["ia_%d" % l] = np.stack([np.cos(A2), np.sin(A2)], 1).astype(np.float32)
        out["ib_%d" % l] = np.stack([np.cos(B2), np.sin(B2)], 1).astype(np.float32)
    return out


class Builder:
    def __init__(self, cfg):
        self.cfg = cfg
        self.nc = bass.Bass("TRN2", target_bir_lowering=False)
        self.dbg = []

    def din(self, name, shape, dt=F32):
        return self.nc.dram_tensor(name, list(shape), dt, kind="ExternalInput").ap()

    def scr(self, name, shape, dt=F32, dbg=False):
        if dbg and self.cfg.debug:
            self.dbg.append(name)
            return self.nc.dram_tensor(name, list(shape), dt, kind="ExternalOutput").ap()
        return self.nc.dram_tensor(name, list(shape), dt).ap()

    def _uniq(self, name):
        self._uid = getattr(self, "_uid", 0) + 1
        return "%s_u%d" % (name, self._uid)

    def sb(self, es, name, shape, dt=F32):
        return es.enter_context(self.nc.sbuf_tensor(self._uniq(name), list(shape), dt))

    def ps(self, es, name, shape, dt=F32):
        return es.enter_context(self.nc.psum_tensor(self._uniq(name), list(shape), dt))

    def dma(self, q, out, in_, reads=(), writes=(), **kw):
        self.P.op(q, lambda e: e.dma_start(out=out, in_=in_, **kw), reads, writes, dma=True)

    def build(self):
        cfg, nc = self.cfg, self.nc
        NB, SEQ, CTX, T, L = cfg.NB, cfg.SEQ, cfg.CTX, cfg.T, cfg.DEPTH
        I = self.I = {}
        I["x"] = self.din("x", (NB, SEQ, D))
        I["ctx"] = self.din("ctx", (NB, CTX, D))
        I["c"] = self.din("c", (NB, D))
        I["c_ctx"] = self.din("c_ctx", (D,))
        for n, s in WSPEC:
            I[n] = self.din(n, (L,) + s)
        hc = host_consts(cfg)
        for n, v in hc.items():
            I[n] = self.din(n, v.shape)
        self.out = nc.dram_tensor("out", [NB, SEQ, D], F32, kind="ExternalOutput").ap()
        S = self.S = {}
        S["xs"] = self.scr("xs", (NB, T, D), dbg=True)
        S["mod"] = self.scr("mod", (3, 6 * D), dbg=True)
        S["px"] = self.scr("px", (NB, T, DIN), dbg=True)
        S["qtm"] = self.scr("qtm", (NB, 6, 96, T), BF16, dbg=True)
        S["ktm"] = self.scr("ktm", (NB, 6, 96, T), BF16, dbg=True)
        S["vm"] = self.scr("vm", (NB, T, 6 * 65), BF16, dbg=True)
        S["qtd"] = self.scr("qtd", (NB, 8, 48, T), BF16, dbg=True)
        S["ktd"] = self.scr("ktd", (NB, 8, 48, T), BF16, dbg=True)
        S["vd"] = self.scr("vd", (NB, T, 4 * 97), BF16, dbg=True)
        S["ox"] = self.scr("ox", (NB, T, D), BF16, dbg=True)
        S["dbg_qb"] = self.scr("dbg_qb", (NB, T, 576), BF16, dbg=True)
        S["hx"] = self.scr("hx", (NB, T, 512), F32, dbg=True)
        S["ht"] = self.scr("ht", (NB, 32, 128, T), BF16)
        for l in sorted({SEQ, CTX}):
            for nm in ("ctf", "stf", "cft", "sft"):
                S["%s_%d" % (nm, l)] = self.scr("%s_%d" % (nm, l), (l, l), BF16)
            S["kf_%d" % l] = self.scr("kf_%d" % l, (l // 128, 128, 2, 2, 256), F32, dbg=True)
        with ExitStack() as es:
            self.P = Prog(nc, es, same_engine_sync=SAME_ENGINE_SYNC)
            self.g = ExitStack()
            es.enter_context(self.g)
            self.idb = self.sb(self.g, "idb", [128, 128], BF16)
            self.cS = self.sb(self.g, "cS", [128, 8, 3], F32)
            self.phase_init()
            stop = cfg.stop_after
            for layer in range(L):
                last = layer == L - 1
                self.P.hoist_sp = HOIST_SP and ("mod" in HOIST_SET)
                self.phase_mod(layer)
                if stop == "mod":
                    break
                self.P.hoist_sp = HOIST_SP and ("A" in HOIST_SET)
                self.phase_A(layer)
                if stop == "A":
                    break
                self.P.hoist_sp = HOIST_SP and ("B" in HOIST_SET)
                self.phase_B(layer)
                if stop == "B":
                    break
                self.P.hoist_sp = HOIST_SP and ("C" in HOIST_SET)
                self.phase_C(layer, "mla", last)
                self.phase_C(layer, "diff", last)
                if stop == "C":
                    break
                self.P.hoist_sp = HOIST_SP and ("H" in HOIST_SET)
                self.phase_H(layer, last)
                if stop == "H":
                    break
                self.P.hoist_sp = HOIST_SP and ("DE1" in HOIST_SET)
                self.phase_DE1(layer, last)
                if stop == "DE1":
                    break
                self.P.hoist_sp = HOIST_SP and ("E2" in HOIST_SET)
                self.phase_E2(layer, last)
                if stop == "E2":
                    break
        return nc

    def phase_init(self):
        cfg, P, I = self.cfg, self.P, self.I
        with ExitStack() as es:
            idf = self.sb(es, "idf", [128, 128])
            ct = self.sb(es, "ct", [128, 8, 3])
            self.dma("sp", idf[:], I["idn"], writes=["idf"])
            P.op("dve", lambda e: e.tensor_copy(out=self.idb[:], in_=idf[:]), ["idf"], ["idb"])
            for b in range(cfg.NB):
                self.dma("sp", ct[:, :, b], I["c"][b].rearrange("(k p) -> p k", p=128), writes=["ct"],
                         allow_slow_non_contiguous=True)
            self.dma("sp", ct[:, :, 2], I["c_ctx"].rearrange("(k p) -> p k", p=128), writes=["ct"],
                     allow_slow_non_contiguous=True)
            P.op("act", lambda e: e.activation(out=self.cS[:], in_=ct[:], func=AF.Silu), ["ct"], ["cS"])
            P.flush()
        for l in sorted({cfg.SEQ, cfg.CTX}):
            self.gen_tables(l)

    def gen_tables(self, l):
        P, I, S = self.P, self.I, self.S
        na = l // 128
        for (an, bn, cn, sn) in (("fa_%d", "fb_%d", "ctf_%d", "stf_%d"), ("ia_%d", "ib_%d", "cft_%d", "sft_%d")):
            At, Bt = I[an % l], I[bn % l]
            Ct, St = S[cn % l], S[sn % l]
            with ExitStack() as es:
                cB = self.sb(es, "cB", [128, l])
                sB = self.sb(es, "sB", [128, l])
                self.dma("sp", cB[:], Bt[:, 0, :], writes=["cB"])
                self.dma("sp", sB[:], Bt[:, 1, :], writes=["sB"])
                bufs = []
                for s in range(2):
                    bufs.append(dict(
                        cA=self.sb(es, "cA%d" % s, [128, l]), sA=self.sb(es, "sA%d" % s, [128, l]),
                        t1=self.sb(es, "t1%d" % s, [128, l]), t2=self.sb(es, "t2%d" % s, [128, l]),
                        oc=self.sb(es, "oc%d" % s, [128, l], BF16), os=self.sb(es, "os%d" % s, [128, l], BF16)))
                for a in range(na):
                    s = a % 2
                    bf = bufs[s]
                    k = lambda n: "%s%d" % (n, s)
                    self.dma("sp", bf["cA"][:], At[a, 0, :].partition_broadcast(128), writes=[k("cA")])
                    self.dma("sp", bf["sA"][:], At[a, 1, :].partition_broadcast(128), writes=[k("sA")])
                    ve = "dve" if s == 0 else "pool"
                    P.op(ve, lambda e, bf=bf: e.tensor_tensor(out=bf["t1"][:], in0=cB[:], in1=bf["cA"][:], op=ALU.mult),
                         ["cB", k("cA")], [k("t1")])
                    P.op(ve, lambda e, bf=bf: e.tensor_tensor(out=bf["t2"][:], in0=sB[:], in1=bf["sA"][:], op=ALU.mult),
                         ["sB", k("sA")], [k("t2")])
                    P.op(ve, lambda e, bf=bf: e.tensor_tensor(out=bf["oc"][:], in0=bf["t1"][:], in1=bf["t2"][:], op=ALU.subtract),
                         [k("t1"), k("t2")], [k("oc")])
                    P.op(ve, lambda e, bf=bf: e.tensor_tensor(out=bf["t1"][:], in0=cB[:], in1=bf["sA"][:], op=ALU.mult),
                         ["cB", k("sA"), k("oc")], [k("t1")])
                    P.op(ve, lambda e, bf=bf: e.tensor_tensor(out=bf["t2"][:], in0=sB[:], in1=bf["cA"][:], op=ALU.mult),
                         ["sB", k("cA"), k("oc")], [k("t2")])
                    P.op(ve, lambda e, bf=bf: e.tensor_tensor(out=bf["os"][:], in0=bf["t1"][:], in1=bf["t2"][:], op=ALU.add),
                         [k("t1"), k("t2")], [k("os")])
                    self.dma("sp", Ct[a * 128:(a + 1) * 128, :], bf["oc"][:], reads=[k("oc")])
                    self.dma("sp", St[a * 128:(a + 1) * 128, :], bf["os"][:], reads=[k("os")])
                P.flush()

    def phase_mod(self, layer):
        P, I, S = self.P, self.I, self.S
        with ExitStack() as es:
            wm = [self.sb(es, "wm%d" % s, [128, 8, 512]) for s in range(2)]
            bm = self.sb(es, "bm", [3, 6 * D])
            mr = self.sb(es, "mr", [3, 6 * D])
            pm = [self.ps(es, "pm%d" % s, [128, 512]) for s in range(2)]
            self.dma("sp", bm[:], I["b_mod"][layer].partition_broadcast(3), writes=["bm"])
            wv = I["w_mod"][layer].rearrange("(k p) n -> p k n", p=128)
            for n in range(12):
                s = n % 2
                self.dma("sp", wm[s][:], wv[:, :, n * 512:(n + 1) * 512], writes=["wm%d" % s])
                for k in range(8):
                    P.op("pe", lambda e, s=s, k=k: e.matmul(pm[s][0:3, :], lhsT=self.cS[:, k, :], rhs=wm[s][:, k, :],
                                                           start=(k == 0), stop=(k == 7)),
                         ["cS", "wm%d" % s], ["pm%d" % s])
                P.op("dve", lambda e, s=s, n=n: e.tensor_tensor(out=mr[:, n * 512:(n + 1) * 512], in0=pm[s][0:3, :],
                                                                in1=bm[:, n * 512:(n + 1) * 512], op=ALU.add),
                     ["pm%d" % s, "bm"], ["mr"])
            self.dma("sp", S["mod"], mr[:], reads=["mr"])
            P.flush()

    def load_mod_fm(self, es, name, slot_shift, slot_scale):
        P, S = self.P, self.S
        t = self.sb(es, name, [128, 3, 2, 8])
        for j in range(3):
            for q, sl in enumerate((slot_shift, slot_scale)):
                self.dma("sp", t[:, j, q, :], S["mod"][j, sl * D:(sl + 1) * D].rearrange("(k p) -> p k", p=128),
                         writes=[name], allow_slow_non_contiguous=True)
        P.op("dve", lambda e: e.tensor_scalar_add(out=t[:, :, 1, :], in0=t[:, :, 1, :], scalar1=1.0), [name], [name])
        return t

    def ln_mod_T(self, xt, xkey, W, slot, modt, j, outT, outkey, col0):
        self.ln_chain(xt, xkey, W, slot)
        self.ln_T(W, slot, slot % len(W["pT"]), modt, j, outT, outkey, col0)

    def ln_chain(self, xt, xkey, W, slot):
        P = self.P
        st, mv, rs, xn = W["st"][slot], W["mv"][slot], W["rs"][slot], W["xn"][slot]
        sk = "_%s%d" % (W["tag"], slot)
        for h in range(2):
            P.op("dve", lambda e: e.bn_stats(out=st[:, h, :], in_=xt[:, h * 512:(h + 1) * 512]), [xkey], ["st" + sk + str(h)])
        P.op("dve", lambda e: e.bn_aggr(out=mv[:], in_=st[:].rearrange("p a b -> p (a b)")),
             ["st" + sk + "0", "st" + sk + "1"], ["mv" + sk])
        P.op("act", lambda e: e.activation(out=rs[:], in_=mv[:, 1:2], func=AF.Sqrt, bias=LN_EPS), ["mv" + sk], ["rs" + sk])
        P.op("dve", lambda e: e.reciprocal(out=rs[:], in_=rs[:]), ["rs" + sk], ["rs" + sk])
        P.op("dve", lambda e: e.tensor_scalar(out=xn[:], in0=xt, scalar1=mv[:, 0:1], scalar2=rs[:], op0=ALU.subtract, op1=ALU.mult),
             [xkey, "mv" + sk, "rs" + sk], ["xn" + sk])

    def ln_T(self, W, slot, pslot, modt, j, outT, outkey, col0):
        P = self.P
        xn, pT = W["xn"][slot], W["pT"][pslot]
        sk = "_%s%d" % (W["tag"], slot)
        pk = "pT_%s%d" % (W["tag"], pslot)
        for k in range(8):
            P.op("pe", lambda e: e.transpose(out=pT[:, k, :], in_=xn[:, k * 128:(k + 1) * 128], identity=self.idb[:]),
                 ["xn" + sk, "idb"], [pk])
        for k in range(8):
            P.op("act", lambda e: e.activation(out=outT[:, k, col0:col0 + 128], in_=pT[:, k, :], func=AF.Identity,
                                               scale=modt[:, j, 1, k:k + 1], bias=modt[:, j, 0, k:k + 1]),
                 [pk, W["modkey"]], [outkey])

    def ln_ws(self, es, tag, modkey, nslot=2, npslot=2):
        W = dict(tag=tag, modkey=modkey)
        W["st"] = [self.sb(es, "st%s%d" % (tag, s), [128, 2, 6]) for s in range(nslot)]
        W["mv"] = [self.sb(es, "mv%s%d" % (tag, s), [128, 2]) for s in range(nslot)]
        W["rs"] = [self.sb(es, "rs%s%d" % (tag, s), [128, 1]) for s in range(nslot)]
        W["xn"] = [self.sb(es, "xn%s%d" % (tag, s), [128, 1024], BF16) for s in range(nslot)]
        W["pT"] = [self.ps(es, "pT%s%d" % (tag, s), [128, 8, 128], BF16) for s in range(npslot)]
        return W

    def x_src(self, layer, b, i):
        cfg = self.cfg
        if layer == 0:
            nct = cfg.CTX // 128
            if i < nct:
                return self.I["ctx"][b, i * 128:(i + 1) * 128, :]
            return self.I["x"][b, (i - nct) * 128:(i - nct + 1) * 128, :]
        return self.S["xs"][b, i * 128:(i + 1) * 128, :]

    def phase_A(self, layer):
        cfg, P, I, S = self.cfg, self.P, self.I, self.S
        with ExitStack() as es:
            win = self.sb(es, "win", [128, 8, DIN], BF16)
            self.dma("pool", win[:], I["w_in"][layer].rearrange("(k p) n -> p k n", p=128), writes=["win"])
            modt = self.load_mod_fm(es, "modA", 0, 1)
            W = self.ln_ws(es, "A", "modA")
            xt = [self.sb(es, "xtA%d" % s, [128, 1024]) for s in range(2)]
            xmT = [self.sb(es, "xmTA%d" % s, [128, 8, 128], BF16) for s in range(2)]
            pxs = [self.sb(es, "pxsA%d" % s, [128, DIN]) for s in range(2)]
            ppx = [self.ps(es, "ppxA%d" % s, [128, 512]) for s in range(3)]
            chunks = [(0, 512), (512, 512), (1024, 512), (1536, 512), (2048, DIN - 2048)]
            nct = cfg.CTX // 128
            seq = [(b, i) for b in range(cfg.NB) for i in range(cfg.NT)]
            cc = [0]

            def front(n):
                b, i = seq[n]
                s = n % 2
                j = 2 if i < nct else b
                self.dma("sp", xt[s][:], self.x_src(layer, b, i), writes=["xtA%d" % s])
                self.ln_mod_T(xt[s][:], "xtA%d" % s, W, s, modt, j, xmT[s], "xmTA%d" % s, 0)

            def back(n):
                b, i = seq[n]
                s = n % 2
                for (c0, cw) in chunks:
                    pb = cc[0] % 3
                    cc[0] += 1
                    for k in range(8):
                        P.op("pe", lambda e: e.matmul(ppx[pb][:, 0:cw], lhsT=xmT[s][:, k, :], rhs=win[:, k, c0:c0 + cw], start=(k == 0),
                                                      stop=(k == 7)), ["xmTA%d" % s, "win"], ["ppxA%d" % pb])
                    if cc[0] % 2 == 0:
                        P.op("dve", lambda e: e.tensor_copy(out=pxs[s][:, c0:c0 + cw], in_=ppx[pb][:, 0:cw]), ["ppxA%d" % pb], ["pxsA%d" % s])
                    else:
                        P.op("act", lambda e: e.copy(out=pxs[s][:, c0:c0 + cw], in_=ppx[pb][:, 0:cw]), ["ppxA%d" % pb], ["pxsA%d" % s])
                self.dma("sp", S["px"][b, i * 128:(i + 1) * 128, :], pxs[s][:], reads=["pxsA%d" % s])

            front(0)
            for n in range(len(seq)):
                if n + 1 < len(seq):
                    front(n + 1)
                back(n)
            P.flush()

    def rope(self, eng, src, dst, cosv, sinv, ta, tb, rkeys, wkey, tkey):
        P = self.P
        s0, s1 = src[:, :, :, 0, :], src[:, :, :, 1, :]
        d0, d1 = dst[:, :, :, 0, :], dst[:, :, :, 1, :]
        P.op(eng, lambda e: e.tensor_tensor(out=ta, in0=s0, in1=cosv, op=ALU.mult), rkeys, [tkey + "a"])
        P.op(eng, lambda e: e.tensor_tensor(out=tb, in0=s1, in1=sinv, op=ALU.mult), rkeys, [tkey + "b"])
        P.op(eng, lambda e: e.tensor_tensor(out=d0, in0=ta, in1=tb, op=ALU.subtract), [tkey + "a", tkey + "b"], [wkey])
        P.op(eng, lambda e: e.tensor_tensor(out=ta, in0=s1, in1=cosv, op=ALU.mult), rkeys + [wkey], [tkey + "a"])
        P.op(eng, lambda e: e.tensor_tensor(out=tb, in0=s0, in1=sinv, op=ALU.mult), rkeys + [wkey], [tkey + "b"])
        P.op(eng, lambda e: e.tensor_tensor(out=d1, in0=ta, in1=tb, op=ALU.add), [tkey + "a", tkey + "b"], [wkey])

    def phase_B(self, layer):
        cfg, P, I, S = self.cfg, self.P, self.I, self.S
        with ExitStack() as es:
            wqf = self.sb(es, "wqf", [128, 2, 576])
            wq = self.sb(es, "wq", [128, 2, 576], BF16)
            gq = self.sb(es, "gq", [128, 2])
            wkvf = self.sb(es, "wkvf", [128, 768])
            wkv = self.sb(es, "wkv", [128, 768], BF16)
            gkv = self.sb(es, "gkv", [128, 1])
            self.dma("sp", wqf[:], I["w_q_up"][layer].rearrange("(k p) n -> p k n", p=128), writes=["wqf"])
            self.dma("sp", gq[:], I["mla_q_norm"][layer].rearrange("(k p) -> p k", p=128), writes=["gq"],
                     allow_slow_non_contiguous=True)
            self.dma("sp", wkvf[:], I["w_kv_up"][layer], writes=["wkvf"])
            self.dma("sp", gkv[:], I["mla_kv_norm"][layer].rearrange("(p o) -> p o", o=1), writes=["gkv"],
                     allow_slow_non_contiguous=True)
            for k in range(2):
                P.op("dve", lambda e, k=k: e.tensor_scalar(out=wq[:, k, :], in0=wqf[:, k, :], scalar1=gq[:, k:k + 1], scalar2=None,
                                                          op0=ALU.mult), ["wqf", "gq"], ["wq"])
            P.op("dve", lambda e: e.tensor_scalar(out=wkv[:], in0=wkvf[:], scalar1=gkv[:, 0:1], scalar2=None, op0=ALU.mult),
                 ["wkvf", "gkv"], ["wkv"])
            NS = 2
            pxt = [self.sb(es, "pxtB%d" % s, [128, 1568]) for s in range(NS)]
            rm = [self.sb(es, "rmB%d" % s, [128, 32]) for s in range(NS)]
            rd = [self.sb(es, "rdB%d" % s, [128, 48]) for s in range(NS)]
            junk = self.sb(es, "junkB", [128, 256])
            ssq = [self.sb(es, "ssqB%d" % s, [128, 2]) for s in range(NS)]
            rr = [self.sb(es, "rrB%d" % s, [128, 2]) for s in range(NS)]
            cn = [self.sb(es, "cnB%d" % s, [128, 384], BF16) for s in range(NS)]
            cT = [self.sb(es, "cTB%d" % s, [128, 3, 128], BF16) for s in range(NS)]
            qf = [self.sb(es, "qfB%d" % s, [128, 576]) for s in range(NS)]
            kvf = [self.sb(es, "kvfB%d" % s, [128, 768]) for s in range(NS)]
            kr = [self.sb(es, "krB%d" % s, [128, 32]) for s in range(NS)]
            qb = [self.sb(es, "qbB%d" % s, [128, 6, 96], BF16) for s in range(NS)]
            kb = [self.sb(es, "kbB%d" % s, [128, 6, 96], BF16) for s in range(NS)]
            vb = [self.sb(es, "vbB%d" % s, [128, 6, 65], BF16) for s in range(NS)]
            qdb = [self.sb(es, "qdbB%d" % s, [128, 8, 48], BF16) for s in range(NS)]
            kdb = [self.sb(es, "kdbB%d" % s, [128, 8, 48], BF16) for s in range(NS)]
            vdb = [self.sb(es, "vdbB%d" % s, [128, 4, 97], BF16) for s in range(NS)]
            tas = [self.sb(es, "taB%d" % q, [128, 8, 2, 12]) for q in range(4)]
            tbs = [self.sb(es, "tbB%d" % q, [128, 8, 2, 12]) for q in range(4)]
            qTs = [self.sb(es, "qTsB%d" % s, [96, 6, 256], BF16) for s in range(2)]
            kTs = [self.sb(es, "kTsB%d" % s, [96, 6, 256], BF16) for s in range(2)]
            qdTs = [self.sb(es, "qdTsB%d" % s, [48, 8, 256], BF16) for s in range(2)]
            kdTs = [self.sb(es, "kdTsB%d" % s, [48, 8, 256], BF16) for s in range(2)]
            ptr = [self.ps(es, "ptrB%d" % s, [128, 8, 128], BF16) for s in range(3)]
            pq0 = self.ps(es, "pq0B", [128, 512])
            pq1 = self.ps(es, "pq1B", [128, 512])
            pkv0 = self.ps(es, "pkv0B", [128, 512])
            pkv1 = self.ps(es, "pkv1B", [128, 512])
            for s in range(NS):
                P.op("pool", lambda e, s=s: e.memset(vb[s][:, :, 64:65], 1.0), [], ["vbB%d" % s])
                P.op("pool", lambda e, s=s: e.memset(vdb[s][:, :, 96:97], 1.0), [], ["vdbB%d" % s])
            trc = [0]

            def next_tr():
                t = trc[0] % 3
                trc[0] += 1
                return t

            it = 0
            for b in range(cfg.NB):
                for i in range(cfg.NT):
                    s = it % NS
                    it += 1
                    g, gs, off = i // 2, (i // 2) % 2, (i % 2) * 128
                    r0 = i * 128
                    K = lambda n: "%sB%d" % (n, s)
                    self.dma("sp", pxt[s][:], S["px"][b, r0:r0 + 128, 0:1568], writes=[K("pxt")])
                    self.dma("sp", rm[s][:], I["rope_m"][r0:r0 + 128, :], writes=[K("rm")])
                    self.dma("sp", rd[s][:], I["rope_d"][r0:r0 + 128, :], writes=[K("rd")])
                    P.op("act", lambda e, s=s: e.activation(out=junk[:, 0:256], in_=pxt[s][:, 0:256], func=AF.Square,
                                                            accum_out=ssq[s][:, 0:1]), [K("pxt")], ["junkB", K("ssq")])
                    P.op("act", lambda e, s=s: e.activation(out=junk[:, 0:128], in_=pxt[s][:, 256:384], func=AF.Square,
                                                            accum_out=ssq[s][:, 1:2]), [K("pxt")], ["junkB", K("ssq")])
                    P.op("act", lambda e, s=s: e.activation(out=rr[s][:, 0:1], in_=ssq[s][:, 0:1], func=AF.Sqrt, scale=1.0 / 256,
                                                            bias=LN_EPS), [K("ssq")], [K("rr")])
                    P.op("act", lambda e, s=s: e.activation(out=rr[s][:, 1:2], in_=ssq[s][:, 1:2], func=AF.Sqrt, scale=1.0 / 128,
                                                            bias=LN_EPS), [K("ssq")], [K("rr")])
                    P.op("dve", lambda e, s=s: e.reciprocal(out=rr[s][:], in_=rr[s][:]), [K("rr")], [K("rr")])
                    P.op("dve", lambda e, s=s: e.tensor_scalar(out=cn[s][:, 0:256], in0=pxt[s][:, 0:256], scalar1=rr[s][:, 0:1],
                                                               scalar2=None, op0=ALU.mult), [K("pxt"), K("rr")], [K("cn")])
                    P.op("dve", lambda e, s=s: e.tensor_scalar(out=cn[s][:, 256:384], in0=pxt[s][:, 256:384], scalar1=rr[s][:, 1:2],
                                                               scalar2=None, op0=ALU.mult), [K("pxt"), K("rr")], [K("cn")])
                    t = next_tr()
                    for k in range(3):
                        P.op("pe", lambda e, s=s, k=k, t=t: e.transpose(out=ptr[t][:, k, :], in_=cn[s][:, k * 128:(k + 1) * 128],
                                                                        identity=self.idb[:]), [K("cn"), "idb"], ["ptrB%d" % t])
                    P.op("act", lambda e, s=s, t=t: e.copy(out=cT[s][:], in_=ptr[t][:, 0:3, :]), ["ptrB%d" % t], [K("cT")])
                    for k in range(2):
                        P.op("pe", lambda e, s=s, k=k: e.matmul(pq0[:], lhsT=cT[s][:, k, :], rhs=wq[:, k, 0:512], start=(k == 0),
                                                                stop=(k == 1)), [K("cT"), "wq"], ["pq0B"])
                    for k in range(2):
                        P.op("pe", lambda e, s=s, k=k: e.matmul(pq1[:, 0:64], lhsT=cT[s][:, k, :], rhs=wq[:, k, 512:576], start=(k == 0),
                                                                stop=(k == 1)), [K("cT"), "wq"], ["pq1B"])
                    P.op("pe", lambda e, s=s: e.matmul(pkv0[:], lhsT=cT[s][:, 2, :], rhs=wkv[:, 0:512], start=True, stop=True),
                         [K("cT"), "wkv"], ["pkv0B"])
                    P.op("pe", lambda e, s=s: e.matmul(pkv1[:, 0:256], lhsT=cT[s][:, 2, :], rhs=wkv[:, 512:768], start=True, stop=True),
                         [K("cT"), "wkv"], ["pkv1B"])
                    P.op("act", lambda e, s=s: e.copy(out=qf[s][:, 0:512], in_=pq0[:]), ["pq0B"], [K("qf")])
                    P.op("dve", lambda e, s=s: e.tensor_copy(out=qf[s][:, 512:576], in_=pq1[:, 0:64]), ["pq1B"], [K("qf")])
                    P.op("dve", lambda e, s=s: e.tensor_copy(out=kvf[s][:, 0:512], in_=pkv0[:]), ["pkv0B"], [K("kvf")])
                    P.op("act", lambda e, s=s: e.copy(out=kvf[s][:, 512:768], in_=pkv1[:, 0:256]), ["pkv1B"], [K("kvf")])
                    qv = qf[s][:].rearrange("p (h d) -> p h d", h=6)
                    kvv = kvf[s][:].rearrange("p (h d) -> p h d", h=6)
                    cosm = rm[s][:, 0:16].rearrange("p (a m) -> p a m", a=2).unsqueeze(1)
                    sinm = rm[s][:, 16:32].rearrange("p (a m) -> p a m", a=2).unsqueeze(1)
                    P.op("pool", lambda e, s=s, qv=qv: e.tensor_copy(out=qb[s][:, :, 0:64], in_=qv[:, :, 0:64]), [K("qf")], [K("qb")])
                    self.rope("dve", qv[:, :, 64:96].rearrange("p h (a s m) -> p h a s m", a=2, s=2),
                              qb[s][:, :, 64:96].rearrange("p h (a s m) -> p h a s m", a=2, s=2),
                              cosm.to_broadcast([128, 6, 2, 8]), sinm.to_broadcast([128, 6, 2, 8]),
                              tas[0][:, 0:6, :, 0:8], tbs[0][:, 0:6, :, 0:8], [K("qf"), K("rm")], K("qb"), "ropeB0")
                    self.rope("dve", pxt[s][:, 384:416].rearrange("p (h a s m) -> p h a s m", h=1, a=2, s=2),
                              kr[s][:].rearrange("p (h a s m) -> p h a s m", h=1, a=2, s=2),
                              cosm.to_broadcast([128, 1, 2, 8]), sinm.to_broadcast([128, 1, 2, 8]),
                              tas[1][:, 0:1, :, 0:8], tbs[1][:, 0:1, :, 0:8], [K("pxt"), K("rm")], K("kr"), "ropeB1")
                    P.op("pool", lambda e, s=s, kvv=kvv: e.tensor_copy(out=kb[s][:, :, 0:64], in_=kvv[:, :, 0:64]), [K("kvf")], [K("kb")])
                    P.op("dve", lambda e, s=s: e.tensor_copy(out=kb[s][:, :, 64:96], in_=kr[s][:].unsqueeze(1).to_broadcast([128, 6, 32])),
                         [K("kr")], [K("kb")])
                    P.op("pool", lambda e, s=s, kvv=kvv: e.tensor_copy(out=vb[s][:, :, 0:64], in_=kvv[:, :, 64:128]), [K("kvf")], [K("vb")])
                    self.dma("sp", S["vm"][b, r0:r0 + 128, :], vb[s][:].rearrange("p h d -> p (h d)"), reads=[K("vb")])
                    cosd = rd[s][:, 0:24].rearrange("p (a m) -> p a m", a=2).unsqueeze(1).to_broadcast([128, 8, 2, 12])
                    sind = rd[s][:, 24:48].rearrange("p (a m) -> p a m", a=2).unsqueeze(1).to_broadcast([128, 8, 2, 12])
                    self.rope("pool", pxt[s][:, 416:800].rearrange("p (g a s m) -> p g a s m", g=8, a=2, s=2),
                              qdb[s][:].rearrange("p g (a s m) -> p g a s m", a=2, s=2), cosd, sind, tas[2][:], tbs[2][:],
                              [K("pxt"), K("rd")], K("qdb"), "ropeB2")
                    self.rope("pool", pxt[s][:, 800:1184].rearrange("p (g a s m) -> p g a s m", g=8, a=2, s=2),
                              kdb[s][:].rearrange("p g (a s m) -> p g a s m", a=2, s=2), cosd, sind, tas[3][:], tbs[3][:],
                              [K("pxt"), K("rd")], K("kdb"), "ropeB3")
                    P.op("pool", lambda e, s=s: e.tensor_copy(out=vdb[s][:, :, 0:96],
                                                              in_=pxt[s][:, 1184:1568].rearrange("p (h d) -> p h d", h=4)),
                         [K("pxt")], [K("vdb")])
                    self.dma("sp", S["vd"][b, r0:r0 + 128, :], vdb[s][:].rearrange("p h d -> p (h d)"), reads=[K("vdb")])
                    if cfg.debug:
                        self.dma("sp", S["dbg_qb"][b, r0:r0 + 128, :], qb[s][:].rearrange("p h d -> p (h d)"), reads=[K("qb")])
                    for (srcb, skey, nh, dd, stg, stkey) in ((qb, "qb", 6, 96, qTs, "qTs"), (kb, "kb", 6, 96, kTs, "kTs"),
                                                             (qdb, "qdb", 8, 48, qdTs, "qdTs"), (kdb, "kdb", 8, 48, kdTs, "kdTs")):
                        t = next_tr()
                        for h in range(nh):
                            P.op("pe", lambda e, s=s, h=h, t=t, srcb=srcb, dd=dd: e.transpose(
                                out=ptr[t][0:dd, h, :], in_=srcb[s][:, h, :], identity=self.idb[:]),
                                 [K(skey), "idb"], ["ptrB%d" % t])
                        eng = "act" if skey in ("qb", "qdb") else "dve"
                        if eng == "act":
                            P.op("act", lambda e, t=t, nh=nh, dd=dd, stg=stg: e.copy(out=stg[gs][:, :, off:off + 128], in_=ptr[t][0:dd, 0:nh, :]),
                                 ["ptrB%d" % t], ["%sB%d" % (stkey, gs)])
                        else:
                            P.op("dve", lambda e, t=t, nh=nh, dd=dd, stg=stg: e.tensor_copy(out=stg[gs][:, :, off:off + 128],
                                                                                           in_=ptr[t][0:dd, 0:nh, :]),
                                 ["ptrB%d" % t], ["%sB%d" % (stkey, gs)])
                    if i % 2 == 1:
                        c0 = g * 256
                        for (stg, stkey, dst) in ((qTs, "qTs", "qtm"), (kTs, "kTs", "ktm"), (qdTs, "qdTs", "qtd"), (kdTs, "kdTs", "ktd")):
                            self.dma("sp", S[dst][b].rearrange("h d t -> d h t")[:, :, c0:c0 + 256], stg[gs][:],
                                     reads=["%sB%d" % (stkey, gs)])
            P.flush()

    def phase_C(self, layer, fam, last):
        cfg, P, I, S = self.cfg, self.P, self.I, self.S
        T, CTX, SEQ, NT = cfg.T, cfg.CTX, cfg.SEQ, cfg.NT
        if fam == "mla":
            d, dv, nu, qsrc, ksrc, vsrc, col0 = 96, 64, 6, "qtm", "ktm", "vm", 0
        else:
            d, dv, nu, qsrc, ksrc, vsrc, col0 = 48, 96, 8, "qtd", "ktd", "vd", 384
        nh = 6 if fam == "mla" else 4
        scale = float(d) ** -0.5
        lam_init = 0.8 - 0.6 * math.exp(-0.3 * layer)
        with ExitStack() as es:
            KP = 128
            kt = self.sb(es, "ktC", [KP, nu, T], BF16)
            vt = self.sb(es, "vtC", [128, NT, nh * (dv + 1)], BF16)
            qt = [self.sb(es, "qtC%d" % s, [KP, nu, 512], BF16) for s in range(2)]
            if d < 96:
                P.op("pool", lambda e: e.memset(kt[:], 0.0), [], ["ktC"])
                for s_ in range(2):
                    P.op("dve", lambda e: e.memset(qt[s_][:], 0.0), [], ["qtC%d" % s_])
            pt = [self.sb(es, "ptC%d" % s, [128, 512], BF16) for s in range(3)]
            oxs = [self.sb(es, "oxsC%d" % s, [128, 4, 384], BF16) for s in range(2)]
            rinv = [self.sb(es, "rinvC%d" % s, [128, 4]) for s in range(4)]
            pS = [self.ps(es, "pSC%d" % s, [128, 512]) for s in range(3)]
            pO = [self.ps(es, "pOC%d" % s, [128, 4, 128]) for s in range(2)]
            pOT = [self.ps(es, "pOTC%d" % s, [128, 512]) for s in range(2)]
            oT = [self.sb(es, "oTC%d" % s, [128, 512]) for s in range(2)]
            idf = self.sb(es, "idfC", [128, 128])
            self.dma("sp", idf[:], I["idn"], writes=["idfC"])
            if fam == "diff":
                dl = self.sb(es, "dlC", [128, 192])
                tmp = self.sb(es, "tmpC", [128, 48])
                ss = self.sb(es, "ssC", [128, 2])
                nl = self.sb(es, "nlC", [128, 1])
                sg = self.sb(es, "sgC", [128, 96])
                t1 = self.sb(es, "t1C", [128, 4, 96])
                t2 = self.sb(es, "t2C", [128, 4, 96])
                oo = self.sb(es, "ooC", [128, 4, 96])
                junk = self.sb(es, "junkC", [128, 4, 96])
                sq = self.sb(es, "sqC", [128, 4])
                self.dma("sp", dl[:], I["diff_lambda"][layer].rearrange("a b -> (a b)").partition_broadcast(128), writes=["dl"])
                self.dma("sp", sg[:], I["diff_subln"][layer].partition_broadcast(128), writes=["sg"])
                for q in range(2):
                    P.op("dve", lambda e, q=q: e.tensor_tensor(out=tmp[:], in0=dl[:, q * 96:q * 96 + 48], in1=dl[:, q * 96 + 48:q * 96 + 96],
                                                               op=ALU.mult), ["dl"], ["tmpC"])
                    P.op("dve", lambda e, q=q: e.reduce_sum(out=ss[:, q:q + 1], in_=tmp[:], axis=mybir.AxisListType.X), ["tmpC"], ["ssC"])
                P.op("act", lambda e: e.activation(out=ss[:], in_=ss[:], func=AF.Exp), ["ssC"], ["ssC"])
                P.op("dve", lambda e: e.tensor_tensor(out=nl[:], in0=ss[:, 1:2], in1=ss[:, 0:1], op=ALU.subtract), ["ssC"], ["nlC"])
                P.op("dve", lambda e: e.tensor_scalar_add(out=nl[:], in0=nl[:], scalar1=-lam_init), ["nlC"], ["nlC"])
                P.op("dve", lambda e: e.tensor_scalar(out=sg[:], in0=sg[:], scalar1=1.0 - lam_init, scalar2=None, op0=ALU.mult), ["sg"], ["sg"])
            tiles = []
            qci, uci = 0, 0
            for b in range(cfg.NB):
                qchunks = [] if last else [(0, CTX, CTX // 128)]
                qchunks += [(CTX + j * 512, 512, NT) for j in range(SEQ // 512)]
                for ci, (tok0, Wd, nkc) in enumerate(qchunks):
                    for u in range(nu):
                        for kc in range(nkc):
                            tiles.append(dict(b=b, tok0=tok0, Wd=Wd, nkc=nkc, u=u, kc=kc, qs=qci % 2, po=uci % 2,
                                              first_b=(ci == 0 and u == 0 and kc == 0), first_c=(u == 0 and kc == 0)))
                        uci += 1
                    qci += 1
            for idx, t in enumerate(tiles):
                t["si"] = idx % 3
                t["pi"] = idx % 3

            def emit_qk(t):
                b, qs, Wd, u, kc, si = t["b"], t["qs"], t["Wd"], t["u"], t["kc"], t["si"]
                if t["first_b"]:
                    self.dma("sp", kt[0:d], S[ksrc][b].rearrange("h d t -> d h t"), writes=["ktC"])
                if t["first_c"]:
                    self.dma("sp", qt[qs][0:d, :, 0:Wd], S[qsrc][b].rearrange("h d t -> d h t")[:, :, t["tok0"]:t["tok0"] + Wd],
                             writes=["qtC%d" % qs])
                kd = d if d >= 96 else KP
                P.op("pe", lambda e: e.matmul(pS[si][:, 0:Wd], lhsT=kt[0:kd, u, kc * 128:(kc + 1) * 128], rhs=qt[qs][0:kd, u, 0:Wd],
                                              start=True, stop=True), ["ktC", "qtC%d" % qs], ["pSC%d" % si])

            def emit_rest(t):
                b, qs, Wd, u, kc, si, pi, po, nkc = t["b"], t["qs"], t["Wd"], t["u"], t["kc"], t["si"], t["pi"], t["po"], t["nkc"]
                nqb = Wd // 128
                oxk = "oxsC%d" % qs
                hv = u if fam == "mla" else u // 2
                if t["first_b"]:
                    self.dma("sp", vt[:], S[vsrc][b].rearrange("(n p) c -> p n c", p=128), writes=["vtC"])
                P.op("act", lambda e: e.activation(out=pt[pi][:, 0:Wd], in_=pS[si][:, 0:Wd], func=AF.Exp, scale=scale),
                     ["pSC%d" % si], ["ptC%d" % pi])
                P.op("pe", lambda e: e.matmul(pOT[po][0:dv + 1, 0:Wd], lhsT=vt[:, kc, hv * (dv + 1):(hv + 1) * (dv + 1)], rhs=pt[pi][:, 0:Wd],
                                              start=(kc == 0), stop=(kc == nkc - 1)), ["ptC%d" % pi, "vtC"], ["pOTC%d" % po])
                if kc != nkc - 1:
                    return
                P.op("dve", lambda e: e.tensor_copy(out=oT[po][0:dv + 1, 0:Wd], in_=pOT[po][0:dv + 1, 0:Wd]), ["pOTC%d" % po], ["oTC%d" % po])
                for qb in range(nqb):
                    P.op("pe", lambda e: e.transpose(out=pO[po][:, qb, 0:dv + 1], in_=oT[po][0:dv + 1, qb * 128:(qb + 1) * 128],
                                                     identity=idf[0:dv + 1, 0:dv + 1]), ["oTC%d" % po, "idfC"], ["pOC%d" % po])
                rk = "rinvC%d" % po
                P.op("dve", lambda e: e.reciprocal(out=rinv[po][:, 0:nqb], in_=pO[po][:, 0:nqb, dv]), ["pOC%d" % po], [rk])
                rb = rinv[po][:, 0:nqb].unsqueeze(2).to_broadcast([128, nqb, dv])
                if fam == "mla":
                    P.op("dve", lambda e: e.tensor_tensor(out=oxs[qs][:, 0:nqb, u * 64:(u + 1) * 64], in0=pO[po][:, 0:nqb, 0:dv], in1=rb,
                                                          op=ALU.mult), ["pOC%d" % po, rk], [oxk])
                else:
                    h, m = u // 2, u % 2
                    tt = t1 if m == 0 else t2
                    tk = "t1C" if m == 0 else "t2C"
                    P.op("dve", lambda e: e.tensor_tensor(out=tt[:, 0:nqb, :], in0=pO[po][:, 0:nqb, 0:dv], in1=rb, op=ALU.mult),
                         ["pOC%d" % po, rk], [tk])
                    if m == 1:
                        P.op("dve", lambda e: e.scalar_tensor_tensor(out=oo[:, 0:nqb, :], in0=t2[:, 0:nqb, :], scalar=nl[:, 0:1],
                                                                     in1=t1[:, 0:nqb, :], op0=ALU.mult, op1=ALU.add),
                             ["t1C", "t2C", "nlC"], ["ooC"])
                        for qb in range(nqb):
                            P.op("pool", lambda e: e.tensor_tensor(out=junk[:, qb, :], in0=oo[:, qb, :], in1=oo[:, qb, :], op=ALU.mult),
                                 ["ooC"], ["junkC"])
                        P.op("dve", lambda e: e.reduce_sum(out=sq[:, 0:nqb], in_=junk[:, 0:nqb, :], axis=mybir.AxisListType.X), ["junkC"], ["sqC"])
                        P.op("act", lambda e: e.activation(out=sq[:, 0:nqb], in_=sq[:, 0:nqb], func=AF.Sqrt, scale=1.0 / 96, bias=LN_EPS),
                             ["sqC"], ["sqC"])
                        P.op("dve", lambda e: e.reciprocal(out=sq[:, 0:nqb], in_=sq[:, 0:nqb]), ["sqC"], ["sqC"])
                        P.op("dve", lambda e: e.tensor_tensor(out=oo[:, 0:nqb, :], in0=oo[:, 0:nqb, :],
                                                              in1=sq[:, 0:nqb].unsqueeze(2).to_broadcast([128, nqb, 96]), op=ALU.mult),
                             ["ooC", "sqC"], ["ooC"])
                        P.op("dve", lambda e: e.tensor_tensor(out=oxs[qs][:, 0:nqb, h * 96:(h + 1) * 96], in0=oo[:, 0:nqb, :],
                                                              in1=sg[:].unsqueeze(1).to_broadcast([128, nqb, 96]), op=ALU.mult),
                             ["ooC", "sg"], [oxk])
                if u == nu - 1:
                    self.dma("sp", S["ox"][b, t["tok0"]:t["tok0"] + Wd, col0:col0 + 384].rearrange("(q p) c -> p q c", p=128),
                             oxs[qs][:, 0:nqb, :], reads=[oxk])

            LA = 2
            for idx in range(min(LA, len(tiles))):
                emit_qk(tiles[idx])
            for idx in range(len(tiles)):
                if idx + LA < len(tiles):
                    emit_qk(tiles[idx + LA])
                emit_rest(tiles[idx])
            P.flush()

    def phase_H(self, layer, last):
        cfg = self.cfg
        for (l, tok0) in ((cfg.SEQ, cfg.CTX), (cfg.CTX, 0)):
            if last and tok0 == 0:
                continue
            with ExitStack() as zes:
                Z = self.sb(zes, "Zh", [128, l // 128, 512], BF16)
                self.hy_conv(layer, l, tok0, Z)
                self.hy_filters(layer, l)
                for o in range(2):
                    with ExitStack() as yes:
                        Y = self.sb(yes, "Yh", [128, l // 128, 2, 512], BF16)
                        self.hy_forward(layer, l, o, Z, Y)
                        self.hy_inverse(layer, l, tok0, o, Z, Y)

    def hy_conv(self, layer, l, tok0, Z):
        cfg, P, I, S = self.cfg, self.P, self.I, self.S
        nA = l // 128
        with ExitStack() as es:
            wc = self.sb(es, "wcH", [128, 3, 768])
            cb = self.sb(es, "cbH", [128, 768])
            self.dma("sp", wc[:].rearrange("p a c -> p (a c)"),
                     I["hy_conv_w"][layer].rearrange("a c -> (a c)").partition_broadcast(128), writes=["wcH"])
            self.dma("sp", cb[:], I["hy_conv_b"][layer].partition_broadcast(128), writes=["cbH"])
            um = [self.sb(es, "umH%d" % s, [128, 768]) for s in range(2)]
            u0 = [self.sb(es, "u0H%d" % s, [128, 768]) for s in range(2)]
            up = [self.sb(es, "upH%d" % s, [128, 768]) for s in range(2)]
            ta = [self.sb(es, "taH%d" % s, [128, 768]) for s in range(2)]
            tb = [self.sb(es, "tbH%d" % s, [128, 768]) for s in range(2)]
            it = 0
            for b in range(cfg.NB):
                for a in range(nA):
                    s = it % 2
                    it += 1
                    K = lambda n: "%sH%d" % (n, s)
                    r0 = tok0 + a * 128
                    src = S["px"][b]
                    self.dma("sp", u0[s][:], src[r0:r0 + 128, 1568:2336], writes=[K("u0")])
                    if a == 0:
                        P.op("pool", lambda e: e.memset(um[s][:], 0.0), [], [K("um")])
                        self.dma("sp", um[s][1:128, :], src[r0:r0 + 127, 1568:2336], writes=[K("um")])
                    else:
                        self.dma("sp", um[s][:], src[r0 - 1:r0 + 127, 1568:2336], writes=[K("um")])
                    if a == nA - 1:
                        P.op("pool", lambda e: e.memset(up[s][:], 0.0), [], [K("up")])
                        self.dma("sp", up[s][0:127, :], src[r0 + 1:r0 + 128, 1568:2336], writes=[K("up")])
                    else:
                        self.dma("sp", up[s][:], src[r0 + 1:r0 + 129, 1568:2336], writes=[K("up")])
                    P.op("pool", lambda e: e.tensor_tensor(out=ta[s][:], in0=um[s][:], in1=wc[:, 0, :], op=ALU.mult), [K("um"), "wcH"], [K("ta")])
                    P.op("dve", lambda e: e.tensor_tensor(out=tb[s][:], in0=u0[s][:], in1=wc[:, 1, :], op=ALU.mult), [K("u0"), "wcH"], [K("tb")])
                    P.op("dve", lambda e: e.tensor_tensor(out=tb[s][:], in0=tb[s][:], in1=ta[s][:], op=ALU.add), [K("ta"), K("tb")], [K("tb")])
                    P.op("pool", lambda e: e.tensor_tensor(out=ta[s][:], in0=up[s][:], in1=wc[:, 2, :], op=ALU.mult), [K("up"), "wcH", K("tb")], [K("ta")])
                    P.op("dve", lambda e: e.tensor_tensor(out=tb[s][:], in0=tb[s][:], in1=ta[s][:], op=ALU.add), [K("ta"), K("tb")], [K("tb")])
                    P.op("dve", lambda e: e.tensor_tensor(out=tb[s][:], in0=tb[s][:], in1=cb[:], op=ALU.add), [K("tb"), "cbH"], [K("tb")])
                    P.op("act", lambda e: e.copy(out=Z[:, a, b * 256:(b + 1) * 256], in_=tb[s][:, 0:256]), [K("tb")], ["Z%d" % a])
                    self.dma("sp", S["hx"][b, r0:r0 + 128, :], tb[s][:, 256:768], reads=[K("tb")])
            P.flush()

    def hy_filters(self, layer, l):
        cfg, P, I, S = self.cfg, self.P, self.I, self.S
        nA = l // 128
        CW = min(512, l)
        ncol = l // CW
        TWO_PI = 2.0 * math.pi
        with ExitStack() as fes:
            fs = self.sb(fes, "fsH", [128, nA, 2, 512], BF16)
            rn = self.sb(fes, "rnH", [128, 2, 256])
            with ExitStack() as es:
                embT = self.sb(es, "embTH", [33, l])
                fw1 = self.sb(es, "fw1H", [33, 64])
                fb1 = self.sb(es, "fb1H", [64, 1])
                fw2 = self.sb(es, "fw2H", [64, 64])
                fb2 = self.sb(es, "fb2H", [64, 1])
                fw3 = self.sb(es, "fw3H", [65, 1024])
                h1T = self.sb(es, "h1TH", [64, l])
                h2T = self.sb(es, "h2TH", [65, l])
                negr = self.sb(es, "negrH", [128, 256])
                tau = self.sb(es, "tauH", [128, nA])
                pre = [self.sb(es, "preH%d" % s, [64, 512]) for s in range(2)]
                kk = [self.sb(es, "kkH%d" % s, [64, 512]) for s in range(2)]
                wt = [self.sb(es, "wtH%d" % s, [128, 256]) for s in range(2)]
                filt2 = [self.sb(es, "filtH%d" % s, [128, 1024], BF16) for s in range(2)]
                absf = [self.sb(es, "absfH%d" % s, [128, 1024], BF16) for s in range(2)]
                onesb = self.sb(es, "onesH", [128, 128], BF16)
                nsum = self.sb(es, "nsumH", [128, 2, 256])
                pm = [self.ps(es, "pmH%d" % s, [128, 512]) for s in range(2)]
                p3 = [self.ps(es, "p3H%d" % s, [128, 512]) for s in range(2)]
                pn = [self.ps(es, "pnH%d" % s, [128, 512]) for s in range(2)]
                self.dma("sp", embT[:], I["embT_%d" % l], writes=["embT"])
                self.dma("sp", fw1[:], I["hy_fw1"][layer], writes=["fw1"])
                self.dma("sp", fb1[:], I["hy_fb1"][layer].rearrange("(p o) -> p o", o=1), writes=["fb1"], allow_slow_non_contiguous=True)
                self.dma("sp", fw2[:], I["hy_fw2"][layer], writes=["fw2"])
                self.dma("sp", fb2[:], I["hy_fb2"][layer].rearrange("(p o) -> p o", o=1), writes=["fb2"], allow_slow_non_contiguous=True)
                self.dma("sp", fw3[0:64, :], I["hy_fw3"][layer], writes=["fw3"])
                self.dma("sp", fw3[64:65, :], I["hy_fb3"][layer].rearrange("(o n) -> o n", o=1), writes=["fw3"])
                self.dma("sp", negr[:], I["negrates"].partition_broadcast(128), writes=["negr"])
                self.dma("sp", tau[:], I["tau_%d" % l], writes=["tau"])
                P.op("pool", lambda e: e.memset(h2T[64:65, :], 1.0), [], ["h2ones"])
                P.op("pool", lambda e: e.memset(onesb[:], 1.0), [], ["onesb"])

                def sin_layer(w, bcol, src, srck, dst, dstk, kdim):
                    for cc in range(ncol):
                        s = cc % 2
                        c0 = cc * CW
                        P.op("pe", lambda e: e.matmul(pm[s][0:64, 0:CW], lhsT=w[0:kdim, :], rhs=src[0:kdim, c0:c0 + CW], start=True, stop=True),
                             [srck, "fw1", "fw2"], ["pmH%d" % s])
                        P.op("dve", lambda e: e.tensor_scalar(out=pre[s][:, 0:CW], in0=pm[s][0:64, 0:CW], scalar1=bcol[:, 0:1], scalar2=None,
                                                              op0=ALU.add), ["pmH%d" % s, "fb1", "fb2"], ["preH%d" % s])
                        P.op("dve", lambda e: e.tensor_scalar(out=kk[s][:, 0:CW], in0=pre[s][:, 0:CW], scalar1=1.0 / TWO_PI, scalar2=MAGIC,
                                                              op0=ALU.mult, op1=ALU.add), ["preH%d" % s], ["kkH%d" % s])
                        P.op("dve", lambda e: e.tensor_scalar(out=kk[s][:, 0:CW], in0=kk[s][:, 0:CW], scalar1=MAGIC, scalar2=None,
                                                              op0=ALU.subtract), ["kkH%d" % s], ["kkH%d" % s])
                        P.op("dve", lambda e: e.scalar_tensor_tensor(out=pre[s][:, 0:CW], in0=kk[s][:, 0:CW], scalar=-TWO_PI, in1=pre[s][:, 0:CW],
                                                                     op0=ALU.mult, op1=ALU.add), ["kkH%d" % s, "preH%d" % s], ["preH%d" % s])
                        P.op("act", lambda e: e.activation(out=dst[0:64, c0:c0 + CW], in_=pre[s][:, 0:CW], func=AF.Sin), ["preH%d" % s], [dstk])

                sin_layer(fw1, fb1, embT, "embT", h1T, "h1T", 33)
                sin_layer(fw2, fb2, h1T, "h1T", h2T, "h2T", 64)
                for a in range(nA):
                    s = a % 2
                    P.op("act", lambda e: e.activation(out=wt[s][:], in_=negr[:], func=AF.Exp, scale=tau[:, a:a + 1]), ["negr", "tau"], ["wtH%d" % s])
                    filt = filt2[s]
                    fk = "filtH%d" % s
                    for o in range(2):
                        P.op("pe", lambda e: e.matmul(p3[o][:], lhsT=h2T[0:65, a * 128:(a + 1) * 128], rhs=fw3[0:65, o * 512:(o + 1) * 512],
                                                      start=True, stop=True), ["h2T", "h2ones", "fw3"], ["p3H%d" % o])
                        P.op("dve", lambda e: e.tensor_tensor(out=filt[:, o * 512:(o + 1) * 512].rearrange("p (d c) -> p d c", d=2),
                                                              in0=p3[o][:].rearrange("p (d c) -> p d c", d=2),
                                                              in1=wt[s][:].unsqueeze(1).to_broadcast([128, 2, 256]), op=ALU.mult),
                             ["p3H%d" % o, "wtH%d" % s], [fk])
                    if a == 0:
                        for o in range(2):
                            P.op("dve", lambda e: e.memset(filt[0:1, o * 512 + 256:o * 512 + 512], 0.0), [fk], [fk])
                    fv = filt[:].rearrange("p (o d c) -> p o d c", o=2, d=2)
                    P.op("pool", lambda e: e.tensor_tensor(out=fs[:, a, 0, :].rearrange("p (o c) -> p o c", o=2), in0=fv[:, :, 0, :], in1=fv[:, :, 1, :],
                                                           op=ALU.add), [fk], ["fs"])
                    P.op("pool", lambda e: e.tensor_tensor(out=fs[:, a, 1, :].rearrange("p (o c) -> p o c", o=2), in0=fv[:, :, 1, :], in1=fv[:, :, 0, :],
                                                           op=ALU.subtract), [fk], ["fs"])
                    P.op("act", lambda e: e.activation(out=absf[s][:], in_=filt[:], func=AF.Abs), [fk], ["absfH%d" % s])
                    for o in range(2):
                        P.op("pe", lambda e: e.matmul(pn[o][:], lhsT=onesb[:], rhs=absf[s][:, o * 512:(o + 1) * 512], start=(a == 0),
                                                      stop=(a == nA - 1)), ["absfH%d" % s, "onesb"], ["pnH%d" % o])
                for o in range(2):
                    P.op("act", lambda e: e.copy(out=nsum[:, o, :], in_=pn[o][:, 0:256]), ["pnH%d" % o], ["nsum"])
                    P.op("dve", lambda e: e.tensor_tensor(out=nsum[:, o, :], in0=nsum[:, o, :], in1=pn[o][:, 256:512], op=ALU.add),
                         ["pnH%d" % o, "nsum"], ["nsum"])
                P.op("dve", lambda e: e.reciprocal(out=rn[:], in_=nsum[:]), ["nsum"], ["rn"])
                P.flush()
            with ExitStack() as es:
                ng = max(1, nA // 2)
                GW = min(256, l)
                nfc = GW // 128
                tc = [self.sb(es, "tcF%d" % s, [128, nA, GW], BF16) for s in range(2)]
                ts = [self.sb(es, "tsF%d" % s, [128, nA, GW], BF16) for s in range(2)]
                kfo = [self.sb(es, "kfoF%d" % s, [128, 2, 2, 256]) for s in range(2)]
                pA = [self.ps(es, "pAF%d" % s, [128, 512]) for s in range(4)]
                ctf = S["ctf_%d" % l].rearrange("(a p) f -> p a f", p=128)
                stf = S["stf_%d" % l].rearrange("(a p) f -> p a f", p=128)
                ci = 0
                for g in range(ng):
                    s = g % 2
                    self.dma("sp", tc[s][:], ctf[:, :, g * GW:(g + 1) * GW], writes=["tcF%d" % s])
                    self.dma("sp", ts[s][:], stf[:, :, g * GW:(g + 1) * GW], writes=["tsF%d" % s])
                    for fc in range(nfc):
                        ks = ci % 2
                        ci += 1
                        for a in range(nA):
                            P.op("pe", lambda e: e.matmul(pA[ks * 2][:], lhsT=tc[s][:, a, fc * 128:(fc + 1) * 128], rhs=fs[:, a, 0, :], start=(a == 0),
                                                          stop=(a == nA - 1)), ["tcF%d" % s, "fs"], ["pAF%d" % (ks * 2)])
                            P.op("pe", lambda e: e.matmul(pA[ks * 2 + 1][:], lhsT=ts[s][:, a, fc * 128:(fc + 1) * 128], rhs=fs[:, a, 1, :], start=(a == 0),
                                                          stop=(a == nA - 1)), ["tsF%d" % s, "fs"], ["pAF%d" % (ks * 2 + 1)])
                        for q in range(2):
                            j = ks * 2 + q
                            P.op("dve", lambda e: e.tensor_tensor(out=kfo[ks][:, q, :, :], in0=pA[j][:].rearrange("p (o c) -> p o c", o=2), in1=rn[:],
                                                                  op=ALU.mult), ["pAF%d" % j, "rn"], ["kfoF%d" % ks])
                        self.dma("sp", S["kf_%d" % l][g * nfc + fc], kfo[ks][:], reads=["kfoF%d" % ks])
                P.flush()

    def hy_forward(self, layer, l, o, Z, Y):
        cfg, P, I, S = self.cfg, self.P, self.I, self.S
        nA = l // 128
        with ExitStack() as es:
            ng = max(1, nA // 2)
            GW = min(256, l)
            nfc = GW // 128
            tc = [self.sb(es, "tcW%d" % s, [128, nA, GW], BF16) for s in range(2)]
            ts = [self.sb(es, "tsW%d" % s, [128, nA, GW], BF16) for s in range(2)]
            sA = [self.sb(es, "sAW%d" % s, [128, 2, 256]) for s in range(4)]
            t1 = [self.sb(es, "t1W%d" % s, [128, 2, 256]) for s in range(2)]
            t2 = [self.sb(es, "t2W%d" % s, [128, 2, 256]) for s in range(2)]
            kfc = [self.sb(es, "kfcW%d" % s, [128, 2, 256]) for s in range(2)]
            pA = [self.ps(es, "pAW%d" % s, [128, 512]) for s in range(4)]
            ctf = S["ctf_%d" % l].rearrange("(a p) f -> p a f", p=128)
            stf = S["stf_%d" % l].rearrange("(a p) f -> p a f", p=128)
            ci = 0
            for g in range(ng):
                s = g % 2
                self.dma("sp", tc[s][:], ctf[:, :, g * GW:(g + 1) * GW], writes=["tcW%d" % s])
                self.dma("sp", ts[s][:], stf[:, :, g * GW:(g + 1) * GW], writes=["tsW%d" % s])
                for fc in range(nfc):
                    ks = ci % 2
                    ci += 1
                    fch = g * nfc + fc
                    self.dma("sp", kfc[ks][:], S["kf_%d" % l][fch][:, :, o, :], writes=["kfcW%d" % ks])
                    jc, js = ks * 2, ks * 2 + 1
                    for a in range(nA):
                        P.op("pe", lambda e: e.matmul(pA[jc][:], lhsT=tc[s][:, a, fc * 128:(fc + 1) * 128], rhs=Z[:, a, :], start=(a == 0),
                                                      stop=(a == nA - 1)), ["tcW%d" % s, "Z%d" % a], ["pAW%d" % jc])
                        P.op("pe", lambda e: e.matmul(pA[js][:], lhsT=ts[s][:, a, fc * 128:(fc + 1) * 128], rhs=Z[:, a, :], start=(a == 0),
                                                      stop=(a == nA - 1)), ["tsW%d" % s, "Z%d" % a], ["pAW%d" % js])
                    P.op("act", lambda e: e.copy(out=sA[jc][:].rearrange("p b c -> p (b c)"), in_=pA[jc][:]), ["pAW%d" % jc], ["sAW%d" % jc])
                    P.op("act", lambda e: e.copy(out=sA[js][:].rearrange("p b c -> p (b c)"), in_=pA[js][:]), ["pAW%d" % js], ["sAW%d" % js])
                    kre = kfc[ks][:, 0, :].unsqueeze(1).to_broadcast([128, 2, 256])
                    kim = kfc[ks][:, 1, :].unsqueeze(1).to_broadcast([128, 2, 256])
                    kk = "kfcW%d" % ks
                    yre = Y[:, fch, 0, :].rearrange("p (b c) -> p b c", b=2)
                    yim = Y[:, fch, 1, :].rearrange("p (b c) -> p b c", b=2)
                    P.op("dve", lambda e: e.tensor_tensor(out=t1[0][:], in0=sA[jc][:], in1=kre, op=ALU.mult), ["sAW%d" % jc, kk], ["t1W0"])
                    P.op("pool", lambda e: e.tensor_tensor(out=t2[0][:], in0=sA[js][:], in1=kim, op=ALU.mult), ["sAW%d" % js, kk], ["t2W0"])
                    P.op("dve", lambda e: e.tensor_tensor(out=yre, in0=t1[0][:], in1=t2[0][:], op=ALU.add), ["t1W0", "t2W0"], ["Y"])
                    P.op("dve", lambda e: e.tensor_tensor(out=t1[1][:], in0=sA[js][:], in1=kre, op=ALU.mult), ["sAW%d" % js, kk], ["t1W1"])
                    P.op("pool", lambda e: e.tensor_tensor(out=t2[1][:], in0=sA[jc][:], in1=kim, op=ALU.mult), ["sAW%d" % jc, kk], ["t2W1"])
                    P.op("dve", lambda e: e.tensor_tensor(out=yim, in0=t1[1][:], in1=t2[1][:], op=ALU.subtract), ["t1W1", "t2W1"], ["Y"])
            P.flush()

    def hy_inverse(self, layer, l, tok0, o, Z, Y):
        cfg, P, I, S = self.cfg, self.P, self.I, self.S
        nA = l // 128
        with ExitStack() as es:
            ng = max(1, nA // 2)
            GW = min(256, l)
            ntc = GW // 128
            tc = [self.sb(es, "tcV%d" % s, [128, nA, GW], BF16) for s in range(2)]
            ts = [self.sb(es, "tsV%d" % s, [128, nA, GW], BF16) for s in range(2)]
            hb = self.sb(es, "hbV", [128, 256])
            tz = [self.sb(es, "tzV%d" % s, [128, 2, 256]) for s in range(2)]
            yv = [self.sb(es, "yvV%d" % s, [128, 2, 256]) for s in range(2)]
            xg = [self.sb(es, "xgV%d" % s, [128, 2, 256]) for s in range(2)]
            og = [self.sb(es, "ogV%d" % s, [128, 2, 256], BF16) for s in range(2)]
            py = [self.ps(es, "pyV%d" % s, [128, 512]) for s in range(2)]
            cft = S["cft_%d" % l].rearrange("(a p) t -> p a t", p=128)
            sft = S["sft_%d" % l].rearrange("(a p) t -> p a t", p=128)
            self.dma("sp", hb[:], I["hy_bias"][layer, o].partition_broadcast(128), writes=["hbV"])
            ci = 0
            for g in range(ng):
                s = g % 2
                self.dma("sp", tc[s][:], cft[:, :, g * GW:(g + 1) * GW], writes=["tcV%d" % s])
                self.dma("sp", ts[s][:], sft[:, :, g * GW:(g + 1) * GW], writes=["tsV%d" % s])
                for tcn in range(ntc):
                    ks = ci % 2
                    ci += 1
                    a = g * ntc + tcn
                    r0 = tok0 + a * 128
                    for b in range(cfg.NB):
                        self.dma("sp", xg[ks][:, b, :], S["hx"][b, r0:r0 + 128, o * 256:(o + 1) * 256], writes=["xgV%d" % ks])
                    for fc in range(nA):
                        P.op("pe", lambda e: e.matmul(py[ks][:], lhsT=tc[s][:, fc, tcn * 128:(tcn + 1) * 128], rhs=Y[:, fc, 0, :], start=(fc == 0),
                                                      stop=False), ["tcV%d" % s, "Y"], ["pyV%d" % ks])
                        P.op("pe", lambda e: e.matmul(py[ks][:], lhsT=ts[s][:, fc, tcn * 128:(tcn + 1) * 128], rhs=Y[:, fc, 1, :], start=False,
                                                      stop=(fc == nA - 1)), ["tsV%d" % s, "Y"], ["pyV%d" % ks])
                    zv = Z[:, a, :].rearrange("p (b c) -> p b c", b=2)
                    hbb = hb[:].unsqueeze(1).to_broadcast([128, 2, 256])
                    P.op("pool", lambda e: e.tensor_tensor(out=tz[ks][:], in0=zv, in1=hbb, op=ALU.mult), ["Z%d" % a, "hbV"], ["tzV%d" % ks])
                    P.op("dve", lambda e: e.scalar_tensor_tensor(out=yv[ks][:], in0=py[ks][:].rearrange("p (b c) -> p b c", b=2), scalar=1.0 / l,
                                                                 in1=tz[ks][:], op0=ALU.mult, op1=ALU.add), ["pyV%d" % ks, "tzV%d" % ks], ["yvV%d" % ks])
                    if o == 0:
                        P.op("dve", lambda e: e.tensor_tensor(out=zv, in0=yv[ks][:], in1=xg[ks][:], op=ALU.mult),
                             ["yvV%d" % ks, "xgV%d" % ks], ["Z%d" % a])
                    else:
                        P.op("dve", lambda e: e.tensor_tensor(out=og[ks][:], in0=yv[ks][:], in1=xg[ks][:], op=ALU.mult),
                             ["yvV%d" % ks, "xgV%d" % ks], ["ogV%d" % ks])
                        for b in range(cfg.NB):
                            self.dma("sp", S["ox"][b, r0:r0 + 128, 768:1024], og[ks][:, b, :], reads=["ogV%d" % ks])
            P.flush()

    def bcast_vecs(self, es, tag, layer, bias_name, gate_slot, lng, lnb):
        P, I, S = self.P, self.I, self.S
        V = dict(tag=tag)
        bt = self.sb(es, "bB" + tag, [128, D])
        self.dma("sp", bt[:], I[bias_name][layer].partition_broadcast(128), writes=["bB" + tag])
        V["g"], V["bg"] = [], []
        for j in range(3):
            g = self.sb(es, "gB%s%d" % (tag, j), [128, D])
            bg = self.sb(es, "bgB%s%d" % (tag, j), [128, D])
            self.dma("sp", g[:], S["mod"][j, gate_slot * D:(gate_slot + 1) * D].partition_broadcast(128), writes=["gB%s%d" % (tag, j)])
            P.op("pool", lambda e: e.tensor_tensor(out=bg[:], in0=bt[:], in1=g[:], op=ALU.mult), ["bB" + tag, "gB%s%d" % (tag, j)],
                 ["bgB%s%d" % (tag, j)])
            V["g"].append(g)
            V["bg"].append(bg)
        V["lng"] = self.sb(es, "lngB" + tag, [128, D])
        V["lnb"] = self.sb(es, "lnbB" + tag, [128, D])
        self.dma("sp", V["lng"][:], I[lng][layer].partition_broadcast(128), writes=["lngB" + tag])
        self.dma("sp", V["lnb"][:], I[lnb][layer].partition_broadcast(128), writes=["lnbB" + tag])
        return V

    def resid_ln(self, acc, acck, xt, xk, V, j, Wk, slot, out, outk, aux="pool"):
        P = self.P
        tag = V["tag"]
        t, st, mv, rs = Wk["t"][slot], Wk["st"][slot], Wk["mv"][slot], Wk["rs"][slot]
        sk = "%s%d" % (Wk["tag"], slot)
        alpha = (2.0 * self.cfg.DEPTH) ** 0.25
        for n in range(2):
            P.op("dve", lambda e: e.tensor_tensor(out=t[:, n * 512:(n + 1) * 512], in0=acc[n], in1=V["g"][j][:, n * 512:(n + 1) * 512], op=ALU.mult),
                 [acck[n], "gB%s%d" % (tag, j)], ["t" + sk])
        P.op(aux, lambda e: e.tensor_tensor(out=t[:], in0=t[:], in1=V["bg"][j][:], op=ALU.add), ["t" + sk, "bgB%s%d" % (tag, j)], ["t" + sk])
        P.op("dve", lambda e: e.scalar_tensor_tensor(out=t[:], in0=xt, scalar=alpha, in1=t[:], op0=ALU.mult, op1=ALU.add), [xk, "t" + sk], ["t" + sk])
        for h in range(2):
            P.op("dve", lambda e: e.bn_stats(out=st[:, h, :], in_=t[:, h * 512:(h + 1) * 512]), ["t" + sk], ["st" + sk + str(h)])
        P.op("dve", lambda e: e.bn_aggr(out=mv[:], in_=st[:].rearrange("p a b -> p (a b)")), ["st" + sk + "0", "st" + sk + "1"], ["mv" + sk])
        P.op("act", lambda e: e.activation(out=rs[:], in_=mv[:, 1:2], func=AF.Sqrt, bias=LN_EPS), ["mv" + sk], ["rs" + sk])
        P.op("dve", lambda e: e.reciprocal(out=rs[:], in_=rs[:]), ["rs" + sk], ["rs" + sk])
        P.op("dve", lambda e: e.tensor_scalar(out=t[:], in0=t[:], scalar1=mv[:, 0:1], scalar2=rs[:], op0=ALU.subtract, op1=ALU.mult),
             ["t" + sk, "mv" + sk, "rs" + sk], ["t" + sk])
        P.op(aux, lambda e: e.tensor_tensor(out=t[:], in0=t[:], in1=V["lng"][:], op=ALU.mult), ["t" + sk, "lngB" + tag], ["t" + sk])
        P.op("dve", lambda e: e.tensor_tensor(out=out, in0=t[:], in1=V["lnb"][:], op=ALU.add), ["t" + sk, "lnbB" + tag], [outk])

    def resid_ws(self, es, tag, nslot=2):
        Wk = dict(tag=tag)
        Wk["t"] = [self.sb(es, "tR%s%d" % (tag, s), [128, D]) for s in range(nslot)]
        Wk["st"] = [self.sb(es, "stR%s%d" % (tag, s), [128, 2, 6]) for s in range(nslot)]
        Wk["mv"] = [self.sb(es, "mvR%s%d" % (tag, s), [128, 2]) for s in range(nslot)]
        Wk["rs"] = [self.sb(es, "rsR%s%d" % (tag, s), [128, 1]) for s in range(nslot)]
        return Wk

    def phase_DE1(self, layer, last):
        cfg, P, I, S = self.cfg, self.P, self.I, self.S
        with ExitStack() as es:
            wout = self.sb(es, "wout", [128, 8, D], BF16)
            wff1 = self.sb(es, "wff1", [128, 8, DFF], BF16)
            self.dma("pool", wout[:], I["w_out"][layer].rearrange("(k p) n -> p k n", p=128), writes=["wout"])
            w1v = I["w_ff1"][layer].rearrange("(k p) n -> p k n", p=128)
            for k in range(8):
                self.dma("pool", wff1[:, k, :], w1v[:, k, :], writes=["wff1"])
            bf1 = self.sb(es, "bf1", [128, 32])
            self.dma("sp", bf1[:], I["b_ff1"][layer].rearrange("(c p) -> p c", p=128), writes=["bf1"], allow_slow_non_contiguous=True)
            V = self.bcast_vecs(es, "D", layer, "b_out", 2, "ln1_g", "ln1_b")
            mod2 = self.load_mod_fm(es, "modE", 3, 4)
            W = self.ln_ws(es, "E", "modE", nslot=4, npslot=1)
            Wk = self.resid_ws(es, "D")
            oxb = [self.sb(es, "oxbD%d" % s, [128, D], BF16) for s in range(2)]
            oxT = [self.sb(es, "oxTD%d" % s, [128, 8, 128], BF16) for s in range(2)]
            xt = [self.sb(es, "xtD%d" % s, [128, D]) for s in range(2)]
            x1t = [self.sb(es, "x1tD%d" % s, [128, D]) for s in range(2)]
            xm2T = [self.sb(es, "xm2TD%d" % s, [128, 8, 256], BF16) for s in range(2)]
            rt = [self.sb(es, "rtD%d" % s, [128, 256]) for s in range(2)]
            hT = self.sb(es, "hTD0", [128, 32, 256], BF16)
            ptr = self.ps(es, "ptrD", [128, 8, 128], BF16)
            pw = [[self.ps(es, "pwD%d_%d" % (ti, nn), [128, 512]) for nn in range(2)] for ti in range(2)]
            pf = [self.ps(es, "pfD%d" % s, [128, 512]) for s in range(2)]
            groups = [(b, g) for b in range(cfg.NB) for g in range(cfg.NG) if not (last and g == 0)]
            fi = [0]

            def SA_pe(q):
                b, g = groups[q]
                for ti in range(2):
                    n = q * 2 + ti
                    s = n % 2
                    i = g * 2 + ti
                    r0 = i * 128
                    K = lambda nm: "%sD%d" % (nm, s)
                    self.dma("sp", oxb[s][:], S["ox"][b, r0:r0 + 128, :], writes=[K("oxb")])
                    self.dma("sp", xt[s][:], self.x_src(layer, b, i), writes=[K("xt")])
                    for k in range(8):
                        P.op("pe", lambda e: e.transpose(out=ptr[:, k, :], in_=oxb[s][:, k * 128:(k + 1) * 128], identity=self.idb[:]),
                             [K("oxb"), "idb"], ["ptrD"])
                    P.op("act", lambda e: e.copy(out=oxT[s][:], in_=ptr[:]), ["ptrD"], [K("oxT")])
                    for nn in range(2):
                        for k in range(8):
                            P.op("pe", lambda e: e.matmul(pw[ti][nn][:], lhsT=oxT[s][:, k, :], rhs=wout[:, k, nn * 512:(nn + 1) * 512], start=(k == 0),
                                                          stop=(k == 7)), [K("oxT"), "wout"], ["pwD%d_%d" % (ti, nn)])

            def chain_ops(q):
                b, g = groups[q]
                j = 2 if g == 0 else b
                P.defer = []
                for ti in range(2):
                    n = q * 2 + ti
                    s, s4 = n % 2, n % 4
                    r0 = (g * 2 + ti) * 128
                    K = lambda nm: "%sD%d" % (nm, s)
                    self.resid_ln([pw[ti][0][:], pw[ti][1][:]], ["pwD%d_0" % ti, "pwD%d_1" % ti], xt[s][:], K("xt"), V, j, Wk, s, x1t[s][:], K("x1t"), aux="dve")
                    self.dma("sp", S["xs"][b, r0:r0 + 128, :], x1t[s][:], reads=[K("x1t")])
                    self.ln_chain(x1t[s][:], K("x1t"), W, s4)
                lst = P.defer
                P.defer = None
                out = []
                for it_ in lst:
                    if it_[0] == "act":
                        out += [None] * 3 + [it_] + [None] * 3
                    else:
                        out.append(it_)
                return out

            def SB(q):
                b, g = groups[q]
                j = 2 if g == 0 else b
                for ti in range(2):
                    n = q * 2 + ti
                    self.ln_T(W, n % 4, 0, mod2, j, xm2T[q % 2], "xm2TD%d" % (q % 2), ti * 128)

            def SF(q, pend):
                b, g = groups[q]
                gs = q % 2
                per = (len(pend) + 31) // 32
                for fc in range(32):
                    fs = fi[0] % 2
                    fi[0] += 1
                    hk = "hTD_%d" % (fc // 8)
                    for k in range(8):
                        P.op("pe", lambda e: e.matmul(pf[fs][:, 0:256], lhsT=wff1[:, k, fc * 128:(fc + 1) * 128], rhs=xm2T[gs][:, k, :], start=(k == 0),
                                                      stop=(k == 7)), ["wff1", "xm2TD%d" % gs], ["pfD%d" % fs])
                    P.op("act", lambda e: e.activation(out=rt[fs][:], in_=pf[fs][:, 0:256], func=AF.Relu, bias=bf1[:, fc:fc + 1]),
                         ["pfD%d" % fs, "bf1"], ["rtD%d" % fs])
                    P.op("pool", lambda e: e.tensor_tensor(out=hT[:, fc, :], in0=rt[fs][:], in1=rt[fs][:], op=ALU.mult), ["rtD%d" % fs], [hk])
                    if fc % 8 == 7:
                        c8 = fc // 8
                        self.dma("sp", S["ht"][b].rearrange("c p t -> p c t")[:, c8 * 8:(c8 + 1) * 8, g * 256:(g + 1) * 256],
                                 hT[:, c8 * 8:(c8 + 1) * 8, :], reads=[hk])
                    for _ in range(per):
                        if pend:
                            it_ = pend.pop(0)
                            if it_ is not None:
                                P.add(*it_)
                while pend:
                    it_ = pend.pop(0)
                    if it_ is not None:
                        P.add(*it_)

            NQ = len(groups)
            for q0 in range(min(2, NQ)):
                SA_pe(q0)
                for it_ in chain_ops(q0):
                    if it_ is not None:
                        P.add(*it_)
            SB(0)
            for q in range(NQ):
                pend = []
                if q + 2 < NQ:
                    SA_pe(q + 2)
                    pend = chain_ops(q + 2)
                if q + 1 < NQ:
                    SB(q + 1)
                SF(q, pend)
            P.flush()

    def phase_E2(self, layer, last):
        cfg, P, I, S = self.cfg, self.P, self.I, self.S
        with ExitStack() as es:
            wff2 = self.sb(es, "wff2", [128, 32, D], BF16)
            w2v = I["w_ff2"][layer].rearrange("(k p) n -> p k n", p=128)
            for k4 in range(8):
                self.dma("pool", wff2[:, k4 * 4:(k4 + 1) * 4, :], w2v[:, k4 * 4:(k4 + 1) * 4, :], writes=["wff2"])
            V = self.bcast_vecs(es, "F", layer, "b_ff2", 5, "ln2_g", "ln2_b")
            Wk = self.resid_ws(es, "F")
            hT = [self.sb(es, "hTF%d" % s, [128, 32, 256], BF16) for s in range(2)]
            xt = [self.sb(es, "xtF%d" % s, [128, D]) for s in range(2)]
            x2t = [self.sb(es, "x2tF%d" % s, [128, D]) for s in range(2)]
            po = [self.ps(es, "poF%d" % s, [128, 512]) for s in range(4)]
            it, gi = 0, 0
            nct = cfg.CTX // 128
            for b in range(cfg.NB):
                for g in range(cfg.NG):
                    if last and g == 0:
                        continue
                    j = 2 if g == 0 else b
                    gs = gi % 2
                    gi += 1
                    self.dma("sp", hT[gs][:], S["ht"][b].rearrange("c p t -> p c t")[:, :, g * 256:(g + 1) * 256], writes=["hTF%d" % gs])
                    for ti in range(2):
                        s = it % 2
                        it += 1
                        i = g * 2 + ti
                        r0 = i * 128
                        K = lambda n: "%sF%d" % (n, s)
                        self.dma("sp", xt[s][:], S["xs"][b, r0:r0 + 128, :], reads=["xs%d_%d" % (b, i)], writes=[K("xt")])
                        for n in range(2):
                            pb = s * 2 + n
                            for k in range(32):
                                P.op("pe", lambda e: e.matmul(po[pb][:], lhsT=hT[gs][:, k, ti * 128:(ti + 1) * 128], rhs=wff2[:, k, n * 512:(n + 1) * 512],
                                                              start=(k == 0), stop=(k == 31)), ["hTF%d" % gs, "wff2"], ["poF%d" % pb])
                        self.resid_ln([po[s * 2][:], po[s * 2 + 1][:]], ["poF%d" % (s * 2), "poF%d" % (s * 2 + 1)], xt[s][:], K("xt"), V, j, Wk, s,
                                      x2t[s][:], K("x2t"))
                        if last:
                            dst = self.out[b, (i - nct) * 128:(i - nct + 1) * 128, :]
                        else:
                            dst = S["xs"][b, r0:r0 + 128, :]
                        self.dma("sp", dst, x2t[s][:], reads=[K("x2t")], writes=["xs%d_%d" % (b, i)])
            P.flush()


def build_program(cfg):
    bld = Builder(cfg)
    nc = bld.build()
    return bld, nc


_CACHE = {}


def kernel(**inputs):
    cfg = Cfg(SEQ=4096, CTX=256, DEPTH=4, NB=2)
    n_cores = 8
    if "prog" not in _CACHE:
        _CACHE["prog"] = build_program(cfg)
    bld, nc = _CACHE["prog"]
    consts = host_consts(cfg)
    in_maps = []
    for c in range(n_cores):
        m = {}
        sl = slice(c * cfg.NB, (c + 1) * cfg.NB)
        m["x"] = np.ascontiguousarray(inputs["x"][sl], dtype=np.float32)
        m["ctx"] = np.ascontiguousarray(inputs["ctx"][sl], dtype=np.float32)
        m["c"] = np.ascontiguousarray(inputs["c"][sl], dtype=np.float32)
        m["c_ctx"] = np.ascontiguousarray(inputs["c_ctx"], dtype=np.float32)
        for n, _ in WSPEC:
            m[n] = np.ascontiguousarray(inputs[n], dtype=np.float32)
        m.update(consts)
        in_maps.append(m)
    res = run_bass_kernel_spmd(nc, in_maps, core_ids=list(range(n_cores)))
    return np.concatenate([np.asarray(r["out"], dtype=np.float32) for r in res.results], axis=0)
```

```python
import math
from contextlib import ExitStack

import numpy as np
import ml_dtypes
import concourse.bass as bass
import concourse.mybir as mybir
from concourse.bass_utils import run_bass_kernel_spmd

F32 = mybir.dt.float32
BF16 = mybir.dt.bfloat16
AF = mybir.ActivationFunctionType
ALU = mybir.AluOpType

ENGS = ("pe", "act", "dve", "pool", "sp")
DMAQ = ("sp", "pool", "act")

LN_EPS = 1e-6
ROPE_BASE = 10000.0
GRID_W = 64
D = 1024
DIN = 2336
DFF = 4096
MAGIC = 12582912.0
SAME_ENGINE_SYNC = True
HOIST_SP = True
HOIST_SET = ("init", "mod", "A", "B", "C", "H", "DE1", "E2")


class _Rec:
    def __init__(self):
        self.call = None

    def __getattr__(self, name):
        def f(*a, **k):
            self.call = (name, a, k)
            return self
        return f


class Prog:
    RING = 12

    def __init__(self, nc, es, same_engine_sync=True):
        self.nc = nc
        self.same_engine_sync = same_engine_sync
        self.esem = {e: es.enter_context(nc.semaphore("s_" + e)) for e in ENGS}
        self.rings = {e: [es.enter_context(nc.semaphore("d_%s_%d" % (e, j))) for j in range(self.RING)]
                      for e in DMAQ}
        self.fsem = es.enter_context(nc.semaphore("fence"))
        self.cnt = {e: 0 for e in ENGS}
        self.dcnt = {e: 0 for e in DMAQ}
        self.waited = {e: {} for e in ENGS}
        self.nfence = 0
        self.n_emitted = 0
        self.hoist_sp = HOIST_SP
        self.defer = None
        self._reset()

    def _reset(self):
        self.ops = []
        self.last_writer = {}
        self.readers = {}

    def op(self, eng, fn, reads=(), writes=(), dma=False):
        rec = _Rec()
        fn(rec)
        assert rec.call is not None
        if self.defer is not None:
            self.defer.append((eng, rec.call, tuple(reads), tuple(writes), dma))
            return None
        return self.add(eng, rec.call, reads, writes, dma)

    def add(self, eng, call, reads=(), writes=(), dma=False):
        i = len(self.ops)
        deps = set()
        for k in reads:
            lw = self.last_writer.get(k)
            if lw is not None:
                deps.add(lw)
        for k in writes:
            lw = self.last_writer.get(k)
            if lw is not None:
                deps.add(lw)
            r = self.readers.get(k)
            if r:
                deps.update(r["eng"].values())
                deps.update(r["dma"])
        for k in reads:
            r = self.readers.setdefault(k, {"eng": {}, "dma": []})
            if dma:
                r["dma"].append(i)
            else:
                r["eng"][eng] = i
        for k in writes:
            self.last_writer[k] = i
            self.readers[k] = {"eng": {}, "dma": []}
        deps.discard(i)
        self.ops.append(dict(eng=eng, call=call, deps=deps, dma=dma))
        return i

    def _skip(self, od, o):
        if od["dma"] or o["dma"]:
            return False
        if od["eng"] == o["eng"]:
            if o["eng"] == "pe":
                return True
            if not self.same_engine_sync:
                return True
        return False

    def flush(self):
        nc = self.nc
        ops = self.ops
        if not ops:
            return
        need = [False] * len(ops)
        for o in ops:
            for d in o["deps"]:
                od = ops[d]
                if od["dma"] or self._skip(od, o):
                    continue
                need[d] = True
        per = {e: [i for i, o in enumerate(ops) if o["eng"] == e] for e in ENGS}
        if self.hoist_sp:
            per["sp"].sort(key=lambda i: (max(ops[i]["deps"]) if ops[i]["deps"] else -1, i))
        for e in ENGS:
            for i in reversed(per[e]):
                if not ops[i]["dma"]:
                    need[i] = True
                    break
        for e in ENGS:
            for i in per[e]:
                o = ops[i]
                if o["dma"]:
                    j = self.dcnt[e]
                    self.dcnt[e] += 1
                    o["sem"] = self.rings[e][j % self.RING]
                    o["val"] = 16 * (j // self.RING + 1)
                    o["ringprev"] = (o["sem"], o["val"] - 16) if j >= self.RING else None
                elif need[i]:
                    self.cnt[e] += 1
                    o["sem"] = self.esem[e]
                    o["val"] = self.cnt[e]
        final = {}
        for e in DMAQ:
            n = self.dcnt[e]
            for j in range(min(n, self.RING)):
                final[self.rings[e][j]] = 16 * ((n - 1 - j) // self.RING + 1)
        for e in ENGS:
            if self.cnt[e] > 0:
                final[self.esem[e]] = self.cnt[e]
        fence_in = self.nfence
        self.nfence += 1

        def run(e, engobj):
            waited = self.waited[e]
            if fence_in > 0 and e != "sp":
                engobj.wait_ge(self.fsem, fence_in)
            for i in per[e]:
                o = ops[i]
                want = {}
                for d in o["deps"]:
                    od = ops[d]
                    if self._skip(od, o):
                        continue
                    s, v = od["sem"], od["val"]
                    if want.get(s, 0) < v:
                        want[s] = v
                if o["dma"] and o["ringprev"] is not None:
                    s, v = o["ringprev"]
                    if want.get(s, 0) < v:
                        want[s] = v
                for s, v in want.items():
                    if waited.get(s, 0) < v:
                        engobj.wait_ge(s, v)
                        waited[s] = v
                name, a, k = o["call"]
                inst = getattr(engobj, name)(*a, **k)
                self.n_emitted += 1
                if o["dma"]:
                    inst.then_inc(o["sem"], 16)
                elif need[i]:
                    inst.then_inc(o["sem"], 1)
            if e == "sp":
                for s, v in final.items():
                    if waited.get(s, 0) < v:
                        engobj.wait_ge(s, v)
                        waited[s] = v
                engobj.sem_inc(self.fsem, 1)

        with nc.Block() as block:
            @block.tensor
            def _(t):
                run("pe", t)

            @block.scalar
            def _(t):
                run("act", t)

            @block.vector
            def _(t):
                run("dve", t)

            @block.gpsimd
            def _(t):
                run("pool", t)

            @block.sync
            def _(t):
                run("sp", t)
        self._reset()


class Cfg:
    def __init__(self, SEQ=4096, CTX=256, DEPTH=4, NB=2, debug=False, stop_after=None):
        self.SEQ, self.CTX, self.DEPTH, self.NB = SEQ, CTX, DEPTH, NB
        self.T = SEQ + CTX
        self.NT = self.T // 128
        self.NG = self.T // 256
        self.debug = debug
        self.stop_after = stop_after


WSPEC = [
    ("w_mod", (D, 6 * D)), ("b_mod", (6 * D,)), ("w_in", (D, DIN)), ("mla_q_norm", (256,)),
    ("w_q_up", (256, 576)), ("mla_kv_norm", (128,)), ("w_kv_up", (128, 768)), ("diff_lambda", (4, 48)),
    ("diff_subln", (96,)), ("hy_conv_w", (3, 768)), ("hy_conv_b", (768,)), ("hy_fw1", (33, 64)),
    ("hy_fb1", (64,)), ("hy_fw2", (64, 64)), ("hy_fb2", (64,)), ("hy_fw3", (64, 1024)), ("hy_fb3", (1024,)),
    ("hy_bias", (2, 256)), ("w_out", (D, D)), ("b_out", (D,)), ("ln1_g", (D,)), ("ln1_b", (D,)),
    ("w_ff1", (D, DFF)), ("b_ff1", (DFF,)), ("w_ff2", (DFF, D)), ("b_ff2", (D,)), ("ln2_g", (D,)), ("ln2_b", (D,)),
]


def host_consts(cfg):
    out = {}
    out["idn"] = np.eye(128, dtype=np.float32)
    T, CTX, SEQ = cfg.T, cfg.CTX, cfg.SEQ
    t = np.arange(SEQ)
    row = (t // GRID_W).astype(np.float32)
    col = (t % GRID_W).astype(np.float32)

    def rope_tab(h):
        inv = (ROPE_BASE ** (-(np.arange(h // 2, dtype=np.float32) * (2.0 / h)))).astype(np.float32)
        tab = np.zeros((T, 2, 2, h // 2), np.float32)
        tab[:, 0] = 1.0
        for a, pos in enumerate((row, col)):
            ang = (pos[:, None] * inv[None, :]).astype(np.float32)
            tab[CTX:, 0, a] = np.cos(ang)
            tab[CTX:, 1, a] = np.sin(ang)
        return tab.reshape(T, 2 * 2 * (h // 2))

    out["rope_m"] = rope_tab(16)
    out["rope_d"] = rope_tab(24)
    rates = np.linspace(-math.log(1e-2) / 1.5, -math.log(1e-2) / 0.3, 256, dtype=np.float32)
    out["negrates"] = (-rates).astype(np.float32)
    for l in sorted({cfg.SEQ, cfg.CTX}):
        tt = np.arange(l, dtype=np.float32)
        bands = np.arange(1, 17, dtype=np.float32)
        ang = (np.float32(2.0 * math.pi / l) * tt[:, None] * bands[None, :]).astype(np.float32)
        emb = np.concatenate([(tt / np.float32(l))[:, None], np.cos(ang), np.sin(ang)], -1).astype(np.float32)
        out["embT_%d" % l] = np.ascontiguousarray(emb.T)
        out["tau_%d" % l] = np.ascontiguousarray((tt / np.float32(l)).reshape(l // 128, 128).T)
        f = np.arange(l, dtype=np.float64)
        w = 2.0 * np.pi / (4.0 * l)
        ang = w * (2 * f[None, :] + 1) * f[:, None]
        c = np.cos(ang)
        sn = np.sin(ang)
        out["ctf_%d" % l] = c.astype(ml_dtypes.bfloat16)
        out["stf_%d" % l] = sn.astype(ml_dtypes.bfloat16)
        out["cft_%d" % l] = np.ascontiguousarray(c.T).astype(ml_dtypes.bfloat16)
        out["sft_%d" % l] = np.ascontiguousarray(sn.T).astype(ml_dtypes.bfloat16)
    return out


class Builder:
    def __init__(self, cfg):
        self.cfg = cfg
        self.nc = bass.Bass("TRN2", target_bir_lowering=False)
        self.dbg = []

    def din(self, name, shape, dt=F32):
        return self.nc.dram_tensor(name, list(shape), dt, kind="ExternalInput").ap()

    def scr(self, name, shape, dt=F32, dbg=False):
        if dbg and self.cfg.debug:
            self.dbg.append(name)
            return self.nc.dram_tensor(name, list(shape), dt, kind="ExternalOutput").ap()
        return self.nc.dram_tensor(name, list(shape), dt).ap()

    def _uniq(self, name):
        self._uid = getattr(self, "_uid", 0) + 1
        return "%s_u%d" % (name, self._uid)

    def sb(self, es, name, shape, dt=F32):
        return es.enter_context(self.nc.sbuf_tensor(self._uniq(name), list(shape), dt))

    def ps(self, es, name, shape, dt=F32):
        return es.enter_context(self.nc.psum_tensor(self._uniq(name), list(shape), dt))

    def dma(self, q, out, in_, reads=(), writes=(), **kw):
        self.P.op(q, lambda e: e.dma_start(out=out, in_=in_, **kw), reads, writes, dma=True)

    def build(self):
        cfg, nc = self.cfg, self.nc
        NB, SEQ, CTX, T, L = cfg.NB, cfg.SEQ, cfg.CTX, cfg.T, cfg.DEPTH
        I = self.I = {}
        I["x"] = self.din("x", (NB, SEQ, D))
        I["ctx"] = self.din("ctx", (NB, CTX, D))
        I["c"] = self.din("c", (NB, D))
        I["c_ctx"] = self.din("c_ctx", (D,))
        for n, s in WSPEC:
            I[n] = self.din(n, (L,) + s)
        hc = host_consts(cfg)
        for n, v in hc.items():
            I[n] = self.din(n, v.shape, BF16 if v.dtype == ml_dtypes.bfloat16 else F32)
        self.out = nc.dram_tensor("out", [NB, SEQ, D], F32, kind="ExternalOutput").ap()
        S = self.S = {}
        S["xs"] = self.scr("xs", (NB, T, D), dbg=True)
        S["mod"] = self.scr("mod", (3, 6 * D), dbg=True)
        S["px"] = self.scr("px", (NB, T, DIN), dbg=True)
        S["qtm"] = self.scr("qtm", (NB, 6, 96, T), BF16, dbg=True)
        S["ktm"] = self.scr("ktm", (NB, 6, 96, T), BF16, dbg=True)
        S["vm"] = self.scr("vm", (NB, T, 6 * 65), BF16, dbg=True)
        S["qtd"] = self.scr("qtd", (NB, 8, 48, T), BF16, dbg=True)
        S["ktd"] = self.scr("ktd", (NB, 8, 48, T), BF16, dbg=True)
        S["vd"] = self.scr("vd", (NB, T, 4 * 97), BF16, dbg=True)
        S["ox"] = self.scr("ox", (NB, T, D), BF16, dbg=True)
        S["dbg_qb"] = self.scr("dbg_qb", (NB, T, 576), BF16, dbg=True)
        S["hx"] = self.scr("hx", (NB, T, 512), F32, dbg=True)
        S["ht"] = self.scr("ht", (NB, 32, 128, T), BF16)
        for l in sorted({SEQ, CTX}):
            for nm in ("ctf", "stf", "cft", "sft"):
                S["%s_%d" % (nm, l)] = I["%s_%d" % (nm, l)]
            S["kf_%d" % l] = self.scr("kf_%d" % l, (l // 128, 128, 2, 2, 256), F32, dbg=True)
        with ExitStack() as es:
            self.P = Prog(nc, es, same_engine_sync=SAME_ENGINE_SYNC)
            self.g = ExitStack()
            es.enter_context(self.g)
            self.idb = self.sb(self.g, "idb", [128, 128], BF16)
            self.cS = self.sb(self.g, "cS", [128, 8, 3], F32)
            self.phase_init()
            stop = cfg.stop_after
            for layer in range(L):
                last = layer == L - 1
                self.P.hoist_sp = HOIST_SP and ("mod" in HOIST_SET)
                self.phase_mod(layer)
                if stop == "mod":
                    break
                self.P.hoist_sp = HOIST_SP and ("A" in HOIST_SET)
                self.phase_A(layer)
                if stop == "A":
                    break
                self.P.hoist_sp = HOIST_SP and ("B" in HOIST_SET)
                self.phase_B(layer)
                if stop == "B":
                    break
                self.P.hoist_sp = HOIST_SP and ("C" in HOIST_SET)
                self.phase_C(layer, "mla", last)
                self.phase_C(layer, "diff", last)
                if stop == "C":
                    break
                self.P.hoist_sp = HOIST_SP and ("H" in HOIST_SET)
                self.phase_H(layer, last)
                if stop == "H":
                    break
                self.P.hoist_sp = HOIST_SP and ("DE1" in HOIST_SET)
                self.phase_DE1(layer, last)
                if stop == "DE1":
                    break
                self.P.hoist_sp = HOIST_SP and ("E2" in HOIST_SET)
                self.phase_E2(layer, last)
                if stop == "E2":
                    break
        return nc

    def phase_init(self):
        cfg, P, I = self.cfg, self.P, self.I
        with ExitStack() as es:
            idf = self.sb(es, "idf", [128, 128])
            ct = self.sb(es, "ct", [128, 8, 3])
            self.dma("sp", idf[:], I["idn"], writes=["idf"])
            P.op("dve", lambda e: e.tensor_copy(out=self.idb[:], in_=idf[:]), ["idf"], ["idb"])
            for b in range(cfg.NB):
                self.dma("sp", ct[:, :, b], I["c"][b].rearrange("(k p) -> p k", p=128), writes=["ct"],
                         allow_slow_non_contiguous=True)
            self.dma("sp", ct[:, :, 2], I["c_ctx"].rearrange("(k p) -> p k", p=128), writes=["ct"],
                     allow_slow_non_contiguous=True)
            P.op("act", lambda e: e.activation(out=self.cS[:], in_=ct[:], func=AF.Silu), ["ct"], ["cS"])
            P.flush()

    def gen_tables(self, l):
        P, I, S = self.P, self.I, self.S
        na = l // 128
        for (an, bn, cn, sn) in (("fa_%d", "fb_%d", "ctf_%d", "stf_%d"), ("ia_%d", "ib_%d", "cft_%d", "sft_%d")):
            At, Bt = I[an % l], I[bn % l]
            Ct, St = S[cn % l], S[sn % l]
            with ExitStack() as es:
                cB = self.sb(es, "cB", [128, l])
                sB = self.sb(es, "sB", [128, l])
                self.dma("sp", cB[:], Bt[:, 0, :], writes=["cB"])
                self.dma("sp", sB[:], Bt[:, 1, :], writes=["sB"])
                bufs = []
                for s in range(2):
                    bufs.append(dict(
                        cA=self.sb(es, "cA%d" % s, [128, l]), sA=self.sb(es, "sA%d" % s, [128, l]),
                        t1=self.sb(es, "t1%d" % s, [128, l]), t2=self.sb(es, "t2%d" % s, [128, l]),
                        oc=self.sb(es, "oc%d" % s, [128, l], BF16), os=self.sb(es, "os%d" % s, [128, l], BF16)))
                for a in range(na):
                    s = a % 2
                    bf = bufs[s]
                    k = lambda n: "%s%d" % (n, s)
                    self.dma("sp", bf["cA"][:], At[a, 0, :].partition_broadcast(128), writes=[k("cA")])
                    self.dma("sp", bf["sA"][:], At[a, 1, :].partition_broadcast(128), writes=[k("sA")])
                    ve = "dve" if s == 0 else "pool"
                    P.op(ve, lambda e, bf=bf: e.tensor_tensor(out=bf["t1"][:], in0=cB[:], in1=bf["cA"][:], op=ALU.mult),
                         ["cB", k("cA")], [k("t1")])
                    P.op(ve, lambda e, bf=bf: e.tensor_tensor(out=bf["t2"][:], in0=sB[:], in1=bf["sA"][:], op=ALU.mult),
                         ["sB", k("sA")], [k("t2")])
                    P.op(ve, lambda e, bf=bf: e.tensor_tensor(out=bf["oc"][:], in0=bf["t1"][:], in1=bf["t2"][:], op=ALU.subtract),
                         [k("t1"), k("t2")], [k("oc")])
                    P.op(ve, lambda e, bf=bf: e.tensor_tensor(out=bf["t1"][:], in0=cB[:], in1=bf["sA"][:], op=ALU.mult),
                         ["cB", k("sA"), k("oc")], [k("t1")])
                    P.op(ve, lambda e, bf=bf: e.tensor_tensor(out=bf["t2"][:], in0=sB[:], in1=bf["cA"][:], op=ALU.mult),
                         ["sB", k("cA"), k("oc")], [k("t2")])
                    P.op(ve, lambda e, bf=bf: e.tensor_tensor(out=bf["os"][:], in0=bf["t1"][:], in1=bf["t2"][:], op=ALU.add),
                         [k("t1"), k("t2")], [k("os")])
                    self.dma("sp", Ct[a * 128:(a + 1) * 128, :], bf["oc"][:], reads=[k("oc")])
                    self.dma("sp", St[a * 128:(a + 1) * 128, :], bf["os"][:], reads=[k("os")])
                P.flush()

    def phase_mod(self, layer):
        P, I, S = self.P, self.I, self.S
        with ExitStack() as es:
            wm = [self.sb(es, "wm%d" % s, [128, 8, 512]) for s in range(2)]
            bm = self.sb(es, "bm", [3, 6 * D])
            mr = self.sb(es, "mr", [3, 6 * D])
            pm = [self.ps(es, "pm%d" % s, [128, 512]) for s in range(2)]
            self.dma("sp", bm[:], I["b_mod"][layer].partition_broadcast(3), writes=["bm"])
            wv = I["w_mod"][layer].rearrange("(k p) n -> p k n", p=128)
            for n in range(12):
                s = n % 2
                self.dma("sp", wm[s][:], wv[:, :, n * 512:(n + 1) * 512], writes=["wm%d" % s])
                for k in range(8):
                    P.op("pe", lambda e, s=s, k=k: e.matmul(pm[s][0:3, :], lhsT=self.cS[:, k, :], rhs=wm[s][:, k, :],
                                                           start=(k == 0), stop=(k == 7)),
                         ["cS", "wm%d" % s], ["pm%d" % s])
                P.op("dve", lambda e, s=s, n=n: e.tensor_tensor(out=mr[:, n * 512:(n + 1) * 512], in0=pm[s][0:3, :],
                                                                in1=bm[:, n * 512:(n + 1) * 512], op=ALU.add),
                     ["pm%d" % s, "bm"], ["mr"])
            self.dma("sp", S["mod"], mr[:], reads=["mr"])
            P.flush()

    def load_mod_fm(self, es, name, slot_shift, slot_scale):
        P, S = self.P, self.S
        t = self.sb(es, name, [128, 3, 2, 8])
        for j in range(3):
            for q, sl in enumerate((slot_shift, slot_scale)):
                self.dma("sp", t[:, j, q, :], S["mod"][j, sl * D:(sl + 1) * D].rearrange("(k p) -> p k", p=128),
                         writes=[name], allow_slow_non_contiguous=True)
        P.op("dve", lambda e: e.tensor_scalar_add(out=t[:, :, 1, :], in0=t[:, :, 1, :], scalar1=1.0), [name], [name])
        return t

    def ln_mod_T(self, xt, xkey, W, slot, modt, j, outT, outkey, col0):
        self.ln_chain(xt, xkey, W, slot)
        self.ln_T(W, slot, slot % len(W["pT"]), modt, j, outT, outkey, col0)

    def ln_chain(self, xt, xkey, W, slot):
        P = self.P
        st, mv, rs, xn = W["st"][slot], W["mv"][slot], W["rs"][slot], W["xn"][slot]
        sk = "_%s%d" % (W["tag"], slot)
        for h in range(2):
            P.op("dve", lambda e: e.bn_stats(out=st[:, h, :], in_=xt[:, h * 512:(h + 1) * 512]), [xkey], ["st" + sk + str(h)])
        P.op("dve", lambda e: e.bn_aggr(out=mv[:], in_=st[:].rearrange("p a b -> p (a b)")),
             ["st" + sk + "0", "st" + sk + "1"], ["mv" + sk])
        P.op("act", lambda e: e.activation(out=rs[:], in_=mv[:, 1:2], func=AF.Sqrt, bias=LN_EPS), ["mv" + sk], ["rs" + sk])
        P.op("dve", lambda e: e.reciprocal(out=rs[:], in_=rs[:]), ["rs" + sk], ["rs" + sk])
        P.op("dve", lambda e: e.tensor_scalar(out=xn[:], in0=xt, scalar1=mv[:, 0:1], scalar2=rs[:], op0=ALU.subtract, op1=ALU.mult),
             [xkey, "mv" + sk, "rs" + sk], ["xn" + sk])

    def ln_T(self, W, slot, pslot, modt, j, outT, outkey, col0):
        P = self.P
        xn, pT = W["xn"][slot], W["pT"][pslot]
        sk = "_%s%d" % (W["tag"], slot)
        pk = "pT_%s%d" % (W["tag"], pslot)
        for k in range(8):
            P.op("pe", lambda e: e.transpose(out=pT[:, k, :], in_=xn[:, k * 128:(k + 1) * 128], identity=self.idb[:]),
                 ["xn" + sk, "idb"], [pk])
        for k in range(8):
            P.op("act", lambda e: e.activation(out=outT[:, k, col0:col0 + 128], in_=pT[:, k, :], func=AF.Identity,
                                               scale=modt[:, j, 1, k:k + 1], bias=modt[:, j, 0, k:k + 1]),
                 [pk, W["modkey"]], [outkey])

    def ln_ws(self, es, tag, modkey, nslot=2, npslot=2):
        W = dict(tag=tag, modkey=modkey)
        W["st"] = [self.sb(es, "st%s%d" % (tag, s), [128, 2, 6]) for s in range(nslot)]
        W["mv"] = [self.sb(es, "mv%s%d" % (tag, s), [128, 2]) for s in range(nslot)]
        W["rs"] = [self.sb(es, "rs%s%d" % (tag, s), [128, 1]) for s in range(nslot)]
        W["xn"] = [self.sb(es, "xn%s%d" % (tag, s), [128, 1024], BF16) for s in range(nslot)]
        W["pT"] = [self.ps(es, "pT%s%d" % (tag, s), [128, 8, 128], BF16) for s in range(npslot)]
        return W

    def x_src(self, layer, b, i):
        cfg = self.cfg
        if layer == 0:
            nct = cfg.CTX // 128
            if i < nct:
                return self.I["ctx"][b, i * 128:(i + 1) * 128, :]
            return self.I["x"][b, (i - nct) * 128:(i - nct + 1) * 128, :]
        return self.S["xs"][b, i * 128:(i + 1) * 128, :]

    def phase_A(self, layer):
        cfg, P, I, S = self.cfg, self.P, self.I, self.S
        with ExitStack() as es:
            win = self.sb(es, "win", [128, 8, DIN], BF16)
            self.dma("pool", win[:], I["w_in"][layer].rearrange("(k p) n -> p k n", p=128), writes=["win"])
            modt = self.load_mod_fm(es, "modA", 0, 1)
            W = self.ln_ws(es, "A", "modA")
            xt = [self.sb(es, "xtA%d" % s, [128, 1024]) for s in range(2)]
            xmT = [self.sb(es, "xmTA%d" % s, [128, 8, 128], BF16) for s in range(2)]
            pxs = [self.sb(es, "pxsA%d" % s, [128, DIN]) for s in range(2)]
            ppx = [self.ps(es, "ppxA%d" % s, [128, 512]) for s in range(3)]
            chunks = [(0, 512), (512, 512), (1024, 512), (1536, 512), (2048, DIN - 2048)]
            nct = cfg.CTX // 128
            seq = [(b, i) for b in range(cfg.NB) for i in range(cfg.NT)]
            cc = [0]

            def front(n):
                b, i = seq[n]
                s = n % 2
                j = 2 if i < nct else b
                self.dma("sp", xt[s][:], self.x_src(layer, b, i), writes=["xtA%d" % s])
                self.ln_mod_T(xt[s][:], "xtA%d" % s, W, s, modt, j, xmT[s], "xmTA%d" % s, 0)

            def back(n):
                b, i = seq[n]
                s = n % 2
                for (c0, cw) in chunks:
                    pb = cc[0] % 3
                    cc[0] += 1
                    for k in range(8):
                        P.op("pe", lambda e: e.matmul(ppx[pb][:, 0:cw], lhsT=xmT[s][:, k, :], rhs=win[:, k, c0:c0 + cw], start=(k == 0),
                                                      stop=(k == 7)), ["xmTA%d" % s, "win"], ["ppxA%d" % pb])
                    if cc[0] % 2 == 0:
                        P.op("dve", lambda e: e.tensor_copy(out=pxs[s][:, c0:c0 + cw], in_=ppx[pb][:, 0:cw]), ["ppxA%d" % pb], ["pxsA%d" % s])
                    else:
                        P.op("act", lambda e: e.copy(out=pxs[s][:, c0:c0 + cw], in_=ppx[pb][:, 0:cw]), ["ppxA%d" % pb], ["pxsA%d" % s])
                self.dma("sp", S["px"][b, i * 128:(i + 1) * 128, :], pxs[s][:], reads=["pxsA%d" % s])

            front(0)
            for n in range(len(seq)):
                if n + 1 < len(seq):
                    front(n + 1)
                back(n)
            P.flush()

    def rope(self, eng, src, dst, cosv, sinv, ta, tb, rkeys, wkey, tkey):
        P = self.P
        s0, s1 = src[:, :, :, 0, :], src[:, :, :, 1, :]
        d0, d1 = dst[:, :, :, 0, :], dst[:, :, :, 1, :]
        P.op(eng, lambda e: e.tensor_tensor(out=ta, in0=s0, in1=cosv, op=ALU.mult), rkeys, [tkey + "a"])
        P.op(eng, lambda e: e.tensor_tensor(out=tb, in0=s1, in1=sinv, op=ALU.mult), rkeys, [tkey + "b"])
        P.op(eng, lambda e: e.tensor_tensor(out=d0, in0=ta, in1=tb, op=ALU.subtract), [tkey + "a", tkey + "b"], [wkey])
        P.op(eng, lambda e: e.tensor_tensor(out=ta, in0=s1, in1=cosv, op=ALU.mult), rkeys + [wkey], [tkey + "a"])
        P.op(eng, lambda e: e.tensor_tensor(out=tb, in0=s0, in1=sinv, op=ALU.mult), rkeys + [wkey], [tkey + "b"])
        P.op(eng, lambda e: e.tensor_tensor(out=d1, in0=ta, in1=tb, op=ALU.add), [tkey + "a", tkey + "b"], [wkey])

    def phase_B(self, layer):
        cfg, P, I, S = self.cfg, self.P, self.I, self.S
        with ExitStack() as es:
            wqf = self.sb(es, "wqf", [128, 2, 576])
            wq = self.sb(es, "wq", [128, 2, 576], BF16)
            gq = self.sb(es, "gq", [128, 2])
            wkvf = self.sb(es, "wkvf", [128, 768])
            wkv = self.sb(es, "wkv", [128, 768], BF16)
            gkv = self.sb(es, "gkv", [128, 1])
            self.dma("sp", wqf[:], I["w_q_up"][layer].rearrange("(k p) n -> p k n", p=128), writes=["wqf"])
            self.dma("sp", gq[:], I["mla_q_norm"][layer].rearrange("(k p) -> p k", p=128), writes=["gq"],
                     allow_slow_non_contiguous=True)
            self.dma("sp", wkvf[:], I["w_kv_up"][layer], writes=["wkvf"])
            self.dma("sp", gkv[:], I["mla_kv_norm"][layer].rearrange("(p o) -> p o", o=1), writes=["gkv"],
                     allow_slow_non_contiguous=True)
            for k in range(2):
                P.op("dve", lambda e, k=k: e.tensor_scalar(out=wq[:, k, :], in0=wqf[:, k, :], scalar1=gq[:, k:k + 1], scalar2=None,
                                                          op0=ALU.mult), ["wqf", "gq"], ["wq"])
            P.op("dve", lambda e: e.tensor_scalar(out=wkv[:], in0=wkvf[:], scalar1=gkv[:, 0:1], scalar2=None, op0=ALU.mult),
                 ["wkvf", "gkv"], ["wkv"])
            NS = 2
            pxt = [self.sb(es, "pxtB%d" % s, [128, 1568]) for s in range(NS)]
            rm = [self.sb(es, "rmB%d" % s, [128, 32]) for s in range(NS)]
            rd = [self.sb(es, "rdB%d" % s, [128, 48]) for s in range(NS)]
            junk = self.sb(es, "junkB", [128, 256])
            ssq = [self.sb(es, "ssqB%d" % s, [128, 2]) for s in range(NS)]
            rr = [self.sb(es, "rrB%d" % s, [128, 2]) for s in range(NS)]
            cn = [self.sb(es, "cnB%d" % s, [128, 384], BF16) for s in range(NS)]
            cT = [self.sb(es, "cTB%d" % s, [128, 3, 128], BF16) for s in range(NS)]
            qf = [self.sb(es, "qfB%d" % s, [128, 576]) for s in range(NS)]
            kvf = [self.sb(es, "kvfB%d" % s, [128, 768]) for s in range(NS)]
            kr = [self.sb(es, "krB%d" % s, [128, 32]) for s in range(NS)]
            qb = [self.sb(es, "qbB%d" % s, [128, 6, 96], BF16) for s in range(NS)]
            kb = [self.sb(es, "kbB%d" % s, [128, 6, 96], BF16) for s in range(NS)]
            vb = [self.sb(es, "vbB%d" % s, [128, 6, 65], BF16) for s in range(NS)]
            qdb = [self.sb(es, "qdbB%d" % s, [128, 8, 48], BF16) for s in range(NS)]
            kdb = [self.sb(es, "kdbB%d" % s, [128, 8, 48], BF16) for s in range(NS)]
            vdb = [self.sb(es, "vdbB%d" % s, [128, 4, 97], BF16) for s in range(NS)]
            tas = [self.sb(es, "taB%d" % q, [128, 8, 2, 12]) for q in range(4)]
            tbs = [self.sb(es, "tbB%d" % q, [128, 8, 2, 12]) for q in range(4)]
            qTs = [self.sb(es, "qTsB%d" % s, [96, 6, 256], BF16) for s in range(2)]
            kTs = [self.sb(es, "kTsB%d" % s, [96, 6, 256], BF16) for s in range(2)]
            qdTs = [self.sb(es, "qdTsB%d" % s, [48, 8, 256], BF16) for s in range(2)]
            kdTs = [self.sb(es, "kdTsB%d" % s, [48, 8, 256], BF16) for s in range(2)]
            ptr = [self.ps(es, "ptrB%d" % s, [128, 8, 128], BF16) for s in range(3)]
            pq0 = self.ps(es, "pq0B", [128, 512])
            pq1 = self.ps(es, "pq1B", [128, 512])
            pkv0 = self.ps(es, "pkv0B", [128, 512])
            pkv1 = self.ps(es, "pkv1B", [128, 512])
            for s in range(NS):
                P.op("pool", lambda e, s=s: e.memset(vb[s][:, :, 64:65], 1.0), [], ["vbB%d" % s])
                P.op("pool", lambda e, s=s: e.memset(vdb[s][:, :, 96:97], 1.0), [], ["vdbB%d" % s])
            trc = [0]

            def next_tr():
                t = trc[0] % 3
                trc[0] += 1
                return t

            it = 0
            for b in range(cfg.NB):
                for i in range(cfg.NT):
                    s = it % NS
                    it += 1
                    g, gs, off = i // 2, (i // 2) % 2, (i % 2) * 128
                    r0 = i * 128
                    K = lambda n: "%sB%d" % (n, s)
                    self.dma("sp", pxt[s][:], S["px"][b, r0:r0 + 128, 0:1568], writes=[K("pxt")])
                    self.dma("sp", rm[s][:], I["rope_m"][r0:r0 + 128, :], writes=[K("rm")])
                    self.dma("sp", rd[s][:], I["rope_d"][r0:r0 + 128, :], writes=[K("rd")])
                    P.op("act", lambda e, s=s: e.activation(out=junk[:, 0:256], in_=pxt[s][:, 0:256], func=AF.Square,
                                                            accum_out=ssq[s][:, 0:1]), [K("pxt")], ["junkB", K("ssq")])
                    P.op("act", lambda e, s=s: e.activation(out=junk[:, 0:128], in_=pxt[s][:, 256:384], func=AF.Square,
                                                            accum_out=ssq[s][:, 1:2]), [K("pxt")], ["junkB", K("ssq")])
                    P.op("act", lambda e, s=s: e.activation(out=rr[s][:, 0:1], in_=ssq[s][:, 0:1], func=AF.Sqrt, scale=1.0 / 256,
                                                            bias=LN_EPS), [K("ssq")], [K("rr")])
                    P.op("act", lambda e, s=s: e.activation(out=rr[s][:, 1:2], in_=ssq[s][:, 1:2], func=AF.Sqrt, scale=1.0 / 128,
                                                            bias=LN_EPS), [K("ssq")], [K("rr")])
                    P.op("dve", lambda e, s=s: e.reciprocal(out=rr[s][:], in_=rr[s][:]), [K("rr")], [K("rr")])
                    P.op("dve", lambda e, s=s: e.tensor_scalar(out=cn[s][:, 0:256], in0=pxt[s][:, 0:256], scalar1=rr[s][:, 0:1],
                                                               scalar2=None, op0=ALU.mult), [K("pxt"), K("rr")], [K("cn")])
                    P.op("dve", lambda e, s=s: e.tensor_scalar(out=cn[s][:, 256:384], in0=pxt[s][:, 256:384], scalar1=rr[s][:, 1:2],
                                                               scalar2=None, op0=ALU.mult), [K("pxt"), K("rr")], [K("cn")])
                    t = next_tr()
                    for k in range(3):
                        P.op("pe", lambda e, s=s, k=k, t=t: e.transpose(out=ptr[t][:, k, :], in_=cn[s][:, k * 128:(k + 1) * 128],
                                                                        identity=self.idb[:]), [K("cn"), "idb"], ["ptrB%d" % t])
                    P.op("act", lambda e, s=s, t=t: e.copy(out=cT[s][:], in_=ptr[t][:, 0:3, :]), ["ptrB%d" % t], [K("cT")])
                    for k in range(2):
                        P.op("pe", lambda e, s=s, k=k: e.matmul(pq0[:], lhsT=cT[s][:, k, :], rhs=wq[:, k, 0:512], start=(k == 0),
                                                                stop=(k == 1)), [K("cT"), "wq"], ["pq0B"])
                    for k in range(2):
                        P.op("pe", lambda e, s=s, k=k: e.matmul(pq1[:, 0:64], lhsT=cT[s][:, k, :], rhs=wq[:, k, 512:576], start=(k == 0),
                                                                stop=(k == 1)), [K("cT"), "wq"], ["pq1B"])
                    P.op("pe", lambda e, s=s: e.matmul(pkv0[:], lhsT=cT[s][:, 2, :], rhs=wkv[:, 0:512], start=True, stop=True),
                         [K("cT"), "wkv"], ["pkv0B"])
                    P.op("pe", lambda e, s=s: e.matmul(pkv1[:, 0:256], lhsT=cT[s][:, 2, :], rhs=wkv[:, 512:768], start=True, stop=True),
                         [K("cT"), "wkv"], ["pkv1B"])
                    P.op("act", lambda e, s=s: e.copy(out=qf[s][:, 0:512], in_=pq0[:]), ["pq0B"], [K("qf")])
                    P.op("dve", lambda e, s=s: e.tensor_copy(out=qf[s][:, 512:576], in_=pq1[:, 0:64]), ["pq1B"], [K("qf")])
                    P.op("dve", lambda e, s=s: e.tensor_copy(out=kvf[s][:, 0:512], in_=pkv0[:]), ["pkv0B"], [K("kvf")])
                    P.op("act", lambda e, s=s: e.copy(out=kvf[s][:, 512:768], in_=pkv1[:, 0:256]), ["pkv1B"], [K("kvf")])
                    qv = qf[s][:].rearrange("p (h d) -> p h d", h=6)
                    kvv = kvf[s][:].rearrange("p (h d) -> p h d", h=6)
                    cosm = rm[s][:, 0:16].rearrange("p (a m) -> p a m", a=2).unsqueeze(1)
                    sinm = rm[s][:, 16:32].rearrange("p (a m) -> p a m", a=2).unsqueeze(1)
                    P.op("pool", lambda e, s=s, qv=qv: e.tensor_copy(out=qb[s][:, :, 0:64], in_=qv[:, :, 0:64]), [K("qf")], [K("qb")])
                    self.rope("dve", qv[:, :, 64:96].rearrange("p h (a s m) -> p h a s m", a=2, s=2),
                              qb[s][:, :, 64:96].rearrange("p h (a s m) -> p h a s m", a=2, s=2),
                              cosm.to_broadcast([128, 6, 2, 8]), sinm.to_broadcast([128, 6, 2, 8]),
                              tas[0][:, 0:6, :, 0:8], tbs[0][:, 0:6, :, 0:8], [K("qf"), K("rm")], K("qb"), "ropeB0")
                    self.rope("dve", pxt[s][:, 384:416].rearrange("p (h a s m) -> p h a s m", h=1, a=2, s=2),
                              kr[s][:].rearrange("p (h a s m) -> p h a s m", h=1, a=2, s=2),
                              cosm.to_broadcast([128, 1, 2, 8]), sinm.to_broadcast([128, 1, 2, 8]),
                              tas[1][:, 0:1, :, 0:8], tbs[1][:, 0:1, :, 0:8], [K("pxt"), K("rm")], K("kr"), "ropeB1")
                    P.op("pool", lambda e, s=s, kvv=kvv: e.tensor_copy(out=kb[s][:, :, 0:64], in_=kvv[:, :, 0:64]), [K("kvf")], [K("kb")])
                    P.op("dve", lambda e, s=s: e.tensor_copy(out=kb[s][:, :, 64:96], in_=kr[s][:].unsqueeze(1).to_broadcast([128, 6, 32])),
                         [K("kr")], [K("kb")])
                    P.op("pool", lambda e, s=s, kvv=kvv: e.tensor_copy(out=vb[s][:, :, 0:64], in_=kvv[:, :, 64:128]), [K("kvf")], [K("vb")])
                    self.dma("sp", S["vm"][b, r0:r0 + 128, :], vb[s][:].rearrange("p h d -> p (h d)"), reads=[K("vb")])
                    cosd = rd[s][:, 0:24].rearrange("p (a m) -> p a m", a=2).unsqueeze(1).to_broadcast([128, 8, 2, 12])
                    sind = rd[s][:, 24:48].rearrange("p (a m) -> p a m", a=2).unsqueeze(1).to_broadcast([128, 8, 2, 12])
                    self.rope("pool", pxt[s][:, 416:800].rearrange("p (g a s m) -> p g a s m", g=8, a=2, s=2),
                              qdb[s][:].rearrange("p g (a s m) -> p g a s m", a=2, s=2), cosd, sind, tas[2][:], tbs[2][:],
                              [K("pxt"), K("rd")], K("qdb"), "ropeB2")
                    self.rope("dve", pxt[s][:, 800:1184].rearrange("p (g a s m) -> p g a s m", g=8, a=2, s=2),
                              kdb[s][:].rearrange("p g (a s m) -> p g a s m", a=2, s=2), cosd, sind, tas[3][:], tbs[3][:],
                              [K("pxt"), K("rd")], K("kdb"), "ropeB3")
                    P.op("pool", lambda e, s=s: e.tensor_copy(out=vdb[s][:, :, 0:96],
                                                              in_=pxt[s][:, 1184:1568].rearrange("p (h d) -> p h d", h=4)),
                         [K("pxt")], [K("vdb")])
                    self.dma("sp", S["vd"][b, r0:r0 + 128, :], vdb[s][:].rearrange("p h d -> p (h d)"), reads=[K("vdb")])
                    if cfg.debug:
                        self.dma("sp", S["dbg_qb"][b, r0:r0 + 128, :], qb[s][:].rearrange("p h d -> p (h d)"), reads=[K("qb")])
                    for (srcb, skey, nh, dd, stg, stkey) in ((qb, "qb", 6, 96, qTs, "qTs"), (kb, "kb", 6, 96, kTs, "kTs"),
                                                             (qdb, "qdb", 8, 48, qdTs, "qdTs"), (kdb, "kdb", 8, 48, kdTs, "kdTs")):
                        t = next_tr()
                        for h in range(nh):
                            P.op("pe", lambda e, s=s, h=h, t=t, srcb=srcb, dd=dd: e.transpose(
                                out=ptr[t][0:dd, h, :], in_=srcb[s][:, h, :], identity=self.idb[:]),
                                 [K(skey), "idb"], ["ptrB%d" % t])
                        eng = "act" if skey in ("qb", "qdb") else "dve"
                        if eng == "act":
                            P.op("act", lambda e, t=t, nh=nh, dd=dd, stg=stg: e.copy(out=stg[gs][:, :, off:off + 128], in_=ptr[t][0:dd, 0:nh, :]),
                                 ["ptrB%d" % t], ["%sB%d" % (stkey, gs)])
                        else:
                            P.op("dve", lambda e, t=t, nh=nh, dd=dd, stg=stg: e.tensor_copy(out=stg[gs][:, :, off:off + 128],
                                                                                           in_=ptr[t][0:dd, 0:nh, :]),
                                 ["ptrB%d" % t], ["%sB%d" % (stkey, gs)])
                    if i % 2 == 1:
                        c0 = g * 256
                        for (stg, stkey, dst) in ((qTs, "qTs", "qtm"), (kTs, "kTs", "ktm"), (qdTs, "qdTs", "qtd"), (kdTs, "kdTs", "ktd")):
                            self.dma("sp", S[dst][b].rearrange("h d t -> d h t")[:, :, c0:c0 + 256], stg[gs][:],
                                     reads=["%sB%d" % (stkey, gs)])
            P.flush()

    def phase_C(self, layer, fam, last):
        cfg, P, I, S = self.cfg, self.P, self.I, self.S
        T, CTX, SEQ, NT = cfg.T, cfg.CTX, cfg.SEQ, cfg.NT
        if fam == "mla":
            d, dv, nu, qsrc, ksrc, vsrc, col0 = 96, 64, 6, "qtm", "ktm", "vm", 0
        else:
            d, dv, nu, qsrc, ksrc, vsrc, col0 = 48, 96, 8, "qtd", "ktd", "vd", 384
        nh = 6 if fam == "mla" else 4
        scale = float(d) ** -0.5
        lam_init = 0.8 - 0.6 * math.exp(-0.3 * layer)
        with ExitStack() as es:
            KP = 128
            kt = self.sb(es, "ktC", [KP, nu, T], BF16)
            vt = self.sb(es, "vtC", [128, NT, nh * (dv + 1)], BF16)
            qt = [self.sb(es, "qtC%d" % s, [KP, nu, 512], BF16) for s in range(2)]
            if d < 96:
                P.op("pool", lambda e: e.memset(kt[:], 0.0), [], ["ktC"])
                for s_ in range(2):
                    P.op("dve", lambda e: e.memset(qt[s_][:], 0.0), [], ["qtC%d" % s_])
            pt = [self.sb(es, "ptC%d" % s, [128, 512], BF16) for s in range(3)]
            oxs = [self.sb(es, "oxsC%d" % s, [128, 4, 384], BF16) for s in range(2)]
            rinv = [self.sb(es, "rinvC%d" % s, [128, 4]) for s in range(4)]
            pS = [self.ps(es, "pSC%d" % s, [128, 512]) for s in range(3)]
            pO = [self.ps(es, "pOC%d" % s, [128, 4, 128]) for s in range(2)]
            pOT = [self.ps(es, "pOTC%d" % s, [128, 512]) for s in range(2)]
            oT = [self.sb(es, "oTC%d" % s, [128, 512]) for s in range(2)]
            idf = self.sb(es, "idfC", [128, 128])
            self.dma("sp", idf[:], I["idn"], writes=["idfC"])
            if fam == "diff":
                dl = self.sb(es, "dlC", [128, 192])
                tmp = self.sb(es, "tmpC", [128, 48])
                ss = self.sb(es, "ssC", [128, 2])
                nl = self.sb(es, "nlC", [128, 1])
                sg = self.sb(es, "sgC", [128, 96])
                t1 = self.sb(es, "t1C", [128, 4, 96])
                t2 = self.sb(es, "t2C", [128, 4, 96])
                oo = self.sb(es, "ooC", [128, 4, 96])
                junk = self.sb(es, "junkC", [128, 4, 96])
                sq = self.sb(es, "sqC", [128, 4])
                self.dma("sp", dl[:], I["diff_lambda"][layer].rearrange("a b -> (a b)").partition_broadcast(128), writes=["dl"])
                self.dma("sp", sg[:], I["diff_subln"][layer].partition_broadcast(128), writes=["sg"])
                for q in range(2):
                    P.op("dve", lambda e, q=q: e.tensor_tensor(out=tmp[:], in0=dl[:, q * 96:q * 96 + 48], in1=dl[:, q * 96 + 48:q * 96 + 96],
                                                               op=ALU.mult), ["dl"], ["tmpC"])
                    P.op("dve", lambda e, q=q: e.reduce_sum(out=ss[:, q:q + 1], in_=tmp[:], axis=mybir.AxisListType.X), ["tmpC"], ["ssC"])
                P.op("act", lambda e: e.activation(out=ss[:], in_=ss[:], func=AF.Exp), ["ssC"], ["ssC"])
                P.op("dve", lambda e: e.tensor_tensor(out=nl[:], in0=ss[:, 1:2], in1=ss[:, 0:1], op=ALU.subtract), ["ssC"], ["nlC"])
                P.op("dve", lambda e: e.tensor_scalar_add(out=nl[:], in0=nl[:], scalar1=-lam_init), ["nlC"], ["nlC"])
                P.op("dve", lambda e: e.tensor_scalar(out=sg[:], in0=sg[:], scalar1=1.0 - lam_init, scalar2=None, op0=ALU.mult), ["sg"], ["sg"])
            tiles = []
            qci, uci = 0, 0
            for b in range(cfg.NB):
                qchunks = [] if last else [(0, CTX, CTX // 128)]
                qchunks += [(CTX + j * 512, 512, NT) for j in range(SEQ // 512)]
                for ci, (tok0, Wd, nkc) in enumerate(qchunks):
                    for u in range(nu):
                        for kc in range(nkc):
                            tiles.append(dict(b=b, tok0=tok0, Wd=Wd, nkc=nkc, u=u, kc=kc, qs=qci % 2, po=uci % 2,
                                              first_b=(ci == 0 and u == 0 and kc == 0), first_c=(u == 0 and kc == 0)))
                        uci += 1
                    qci += 1
            for idx, t in enumerate(tiles):
                t["si"] = idx % 3
                t["pi"] = idx % 3

            def emit_qk(t):
                b, qs, Wd, u, kc, si = t["b"], t["qs"], t["Wd"], t["u"], t["kc"], t["si"]
                if t["first_b"]:
                    self.dma("sp", kt[0:d], S[ksrc][b].rearrange("h d t -> d h t"), writes=["ktC"])
                if t["first_c"]:
                    self.dma("sp", qt[qs][0:d, :, 0:Wd], S[qsrc][b].rearrange("h d t -> d h t")[:, :, t["tok0"]:t["tok0"] + Wd],
                             writes=["qtC%d" % qs])
                kd = d if d >= 96 else KP
                P.op("pe", lambda e: e.matmul(pS[si][:, 0:Wd], lhsT=kt[0:kd, u, kc * 128:(kc + 1) * 128], rhs=qt[qs][0:kd, u, 0:Wd],
                                              start=True, stop=True), ["ktC", "qtC%d" % qs], ["pSC%d" % si])

            def emit_rest(t):
                b, qs, Wd, u, kc, si, pi, po, nkc = t["b"], t["qs"], t["Wd"], t["u"], t["kc"], t["si"], t["pi"], t["po"], t["nkc"]
                nqb = Wd // 128
                oxk = "oxsC%d" % qs
                hv = u if fam == "mla" else u // 2
                if t["first_b"]:
                    self.dma("sp", vt[:], S[vsrc][b].rearrange("(n p) c -> p n c", p=128), writes=["vtC"])
                P.op("act", lambda e: e.activation(out=pt[pi][:, 0:Wd], in_=pS[si][:, 0:Wd], func=AF.Exp, scale=scale),
                     ["pSC%d" % si], ["ptC%d" % pi])
                P.op("pe", lambda e: e.matmul(pOT[po][0:dv + 1, 0:Wd], lhsT=vt[:, kc, hv * (dv + 1):(hv + 1) * (dv + 1)], rhs=pt[pi][:, 0:Wd],
                                              start=(kc == 0), stop=(kc == nkc - 1)), ["ptC%d" % pi, "vtC"], ["pOTC%d" % po])
                if kc != nkc - 1:
                    return
                P.op("dve", lambda e: e.tensor_copy(out=oT[po][0:dv + 1, 0:Wd], in_=pOT[po][0:dv + 1, 0:Wd]), ["pOTC%d" % po], ["oTC%d" % po])
                for qb in range(nqb):
                    P.op("pe", lambda e: e.transpose(out=pO[po][:, qb, 0:dv + 1], in_=oT[po][0:dv + 1, qb * 128:(qb + 1) * 128],
                                                     identity=idf[0:dv + 1, 0:dv + 1]), ["oTC%d" % po, "idfC"], ["pOC%d" % po])
                rk = "rinvC%d" % po
                P.op("dve", lambda e: e.reciprocal(out=rinv[po][:, 0:nqb], in_=pO[po][:, 0:nqb, dv]), ["pOC%d" % po], [rk])
                rb = rinv[po][:, 0:nqb].unsqueeze(2).to_broadcast([128, nqb, dv])
                if fam == "mla":
                    P.op("dve", lambda e: e.tensor_tensor(out=oxs[qs][:, 0:nqb, u * 64:(u + 1) * 64], in0=pO[po][:, 0:nqb, 0:dv], in1=rb,
                                                          op=ALU.mult), ["pOC%d" % po, rk], [oxk])
                else:
                    h, m = u // 2, u % 2
                    tt = t1 if m == 0 else t2
                    tk = "t1C" if m == 0 else "t2C"
                    P.op("dve", lambda e: e.tensor_tensor(out=tt[:, 0:nqb, :], in0=pO[po][:, 0:nqb, 0:dv], in1=rb, op=ALU.mult),
                         ["pOC%d" % po, rk], [tk])
                    if m == 1:
                        P.op("dve", lambda e: e.scalar_tensor_tensor(out=oo[:, 0:nqb, :], in0=t2[:, 0:nqb, :], scalar=nl[:, 0:1],
                                                                     in1=t1[:, 0:nqb, :], op0=ALU.mult, op1=ALU.add),
                             ["t1C", "t2C", "nlC"], ["ooC"])
                        for qb in range(nqb):
                            P.op("pool", lambda e: e.tensor_tensor(out=junk[:, qb, :], in0=oo[:, qb, :], in1=oo[:, qb, :], op=ALU.mult),
                                 ["ooC"], ["junkC"])
                        P.op("dve", lambda e: e.reduce_sum(out=sq[:, 0:nqb], in_=junk[:, 0:nqb, :], axis=mybir.AxisListType.X), ["junkC"], ["sqC"])
                        P.op("act", lambda e: e.activation(out=sq[:, 0:nqb], in_=sq[:, 0:nqb], func=AF.Sqrt, scale=1.0 / 96, bias=LN_EPS),
                             ["sqC"], ["sqC"])
                        P.op("dve", lambda e: e.reciprocal(out=sq[:, 0:nqb], in_=sq[:, 0:nqb]), ["sqC"], ["sqC"])
                        P.op("dve", lambda e: e.tensor_tensor(out=oo[:, 0:nqb, :], in0=oo[:, 0:nqb, :],
                                                              in1=sq[:, 0:nqb].unsqueeze(2).to_broadcast([128, nqb, 96]), op=ALU.mult),
                             ["ooC", "sqC"], ["ooC"])
                        P.op("dve", lambda e: e.tensor_tensor(out=oxs[qs][:, 0:nqb, h * 96:(h + 1) * 96], in0=oo[:, 0:nqb, :],
                                                              in1=sg[:].unsqueeze(1).to_broadcast([128, nqb, 96]), op=ALU.mult),
                             ["ooC", "sg"], [oxk])
                if u == nu - 1:
                    self.dma("sp", S["ox"][b, t["tok0"]:t["tok0"] + Wd, col0:col0 + 384].rearrange("(q p) c -> p q c", p=128),
                             oxs[qs][:, 0:nqb, :], reads=[oxk])

            LA = 2
            for idx in range(min(LA, len(tiles))):
                emit_qk(tiles[idx])
            for idx in range(len(tiles)):
                if idx + LA < len(tiles):
                    emit_qk(tiles[idx + LA])
                emit_rest(tiles[idx])
            P.flush()

    def phase_H(self, layer, last):
        cfg = self.cfg
        for (l, tok0) in ((cfg.SEQ, cfg.CTX), (cfg.CTX, 0)):
            if last and tok0 == 0:
                continue
            with ExitStack() as zes:
                Z = self.sb(zes, "Zh", [128, l // 128, 512], BF16)
                self.hy_conv(layer, l, tok0, Z)
                self.hy_filters(layer, l)
                for o in range(2):
                    with ExitStack() as yes:
                        Y = self.sb(yes, "Yh", [128, l // 128, 2, 512], BF16)
                        self.hy_forward(layer, l, o, Z, Y)
                        self.hy_inverse(layer, l, tok0, o, Z, Y)

    def hy_conv(self, layer, l, tok0, Z):
        cfg, P, I, S = self.cfg, self.P, self.I, self.S
        nA = l // 128
        with ExitStack() as es:
            wc = self.sb(es, "wcH", [128, 3, 768])
            cb = self.sb(es, "cbH", [128, 768])
            self.dma("sp", wc[:].rearrange("p a c -> p (a c)"),
                     I["hy_conv_w"][layer].rearrange("a c -> (a c)").partition_broadcast(128), writes=["wcH"])
            self.dma("sp", cb[:], I["hy_conv_b"][layer].partition_broadcast(128), writes=["cbH"])
            um = [self.sb(es, "umH%d" % s, [128, 768]) for s in range(2)]
            u0 = [self.sb(es, "u0H%d" % s, [128, 768]) for s in range(2)]
            up = [self.sb(es, "upH%d" % s, [128, 768]) for s in range(2)]
            ta = [self.sb(es, "taH%d" % s, [128, 768]) for s in range(2)]
            tb = [self.sb(es, "tbH%d" % s, [128, 768]) for s in range(2)]
            it = 0
            for b in range(cfg.NB):
                for a in range(nA):
                    s = it % 2
                    it += 1
                    K = lambda n: "%sH%d" % (n, s)
                    r0 = tok0 + a * 128
                    src = S["px"][b]
                    self.dma("sp", u0[s][:], src[r0:r0 + 128, 1568:2336], writes=[K("u0")])
                    if a == 0:
                        P.op("pool", lambda e: e.memset(um[s][:], 0.0), [], [K("um")])
                        self.dma("sp", um[s][1:128, :], src[r0:r0 + 127, 1568:2336], writes=[K("um")])
                    else:
                        self.dma("sp", um[s][:], src[r0 - 1:r0 + 127, 1568:2336], writes=[K("um")])
                    if a == nA - 1:
                        P.op("pool", lambda e: e.memset(up[s][:], 0.0), [], [K("up")])
                        self.dma("sp", up[s][0:127, :], src[r0 + 1:r0 + 128, 1568:2336], writes=[K("up")])
                    else:
                        self.dma("sp", up[s][:], src[r0 + 1:r0 + 129, 1568:2336], writes=[K("up")])
                    P.op("pool", lambda e: e.tensor_tensor(out=ta[s][:], in0=um[s][:], in1=wc[:, 0, :], op=ALU.mult), [K("um"), "wcH"], [K("ta")])
                    P.op("dve", lambda e: e.tensor_tensor(out=tb[s][:], in0=u0[s][:], in1=wc[:, 1, :], op=ALU.mult), [K("u0"), "wcH"], [K("tb")])
                    P.op("dve", lambda e: e.tensor_tensor(out=tb[s][:], in0=tb[s][:], in1=ta[s][:], op=ALU.add), [K("ta"), K("tb")], [K("tb")])
                    P.op("pool", lambda e: e.tensor_tensor(out=ta[s][:], in0=up[s][:], in1=wc[:, 2, :], op=ALU.mult), [K("up"), "wcH", K("tb")], [K("ta")])
                    P.op("dve", lambda e: e.tensor_tensor(out=tb[s][:], in0=tb[s][:], in1=ta[s][:], op=ALU.add), [K("ta"), K("tb")], [K("tb")])
                    P.op("dve", lambda e: e.tensor_tensor(out=tb[s][:], in0=tb[s][:], in1=cb[:], op=ALU.add), [K("tb"), "cbH"], [K("tb")])
                    P.op("act", lambda e: e.copy(out=Z[:, a, b * 256:(b + 1) * 256], in_=tb[s][:, 0:256]), [K("tb")], ["Z%d" % a])
                    self.dma("sp", S["hx"][b, r0:r0 + 128, :], tb[s][:, 256:768], reads=[K("tb")])
            P.flush()

    def hy_filters(self, layer, l):
        cfg, P, I, S = self.cfg, self.P, self.I, self.S
        nA = l // 128
        CW = min(512, l)
        ncol = l // CW
        TWO_PI = 2.0 * math.pi
        with ExitStack() as fes:
            fs = self.sb(fes, "fsH", [128, nA, 2, 512], BF16)
            rn = self.sb(fes, "rnH", [128, 2, 256])
            with ExitStack() as es:
                embT = self.sb(es, "embTH", [33, l])
                fw1 = self.sb(es, "fw1H", [33, 64])
                fb1 = self.sb(es, "fb1H", [64, 1])
                fw2 = self.sb(es, "fw2H", [64, 64])
                fb2 = self.sb(es, "fb2H", [64, 1])
                fw3 = self.sb(es, "fw3H", [65, 1024])
                h1T = self.sb(es, "h1TH", [64, l])
                h2T = self.sb(es, "h2TH", [65, l])
                negr = self.sb(es, "negrH", [128, 256])
                tau = self.sb(es, "tauH", [128, nA])
                pre = [self.sb(es, "preH%d" % s, [64, 512]) for s in range(2)]
                kk = [self.sb(es, "kkH%d" % s, [64, 512]) for s in range(2)]
                wt = [self.sb(es, "wtH%d" % s, [128, 256]) for s in range(2)]
                filt2 = [self.sb(es, "filtH%d" % s, [128, 1024], BF16) for s in range(2)]
                absf = [self.sb(es, "absfH%d" % s, [128, 1024], BF16) for s in range(2)]
                onesb = self.sb(es, "onesH", [128, 128], BF16)
                nsum = self.sb(es, "nsumH", [128, 2, 256])
                pm = [self.ps(es, "pmH%d" % s, [128, 512]) for s in range(2)]
                p3 = [self.ps(es, "p3H%d" % s, [128, 512]) for s in range(2)]
                pn = [self.ps(es, "pnH%d" % s, [128, 512]) for s in range(2)]
                self.dma("sp", embT[:], I["embT_%d" % l], writes=["embT"])
                self.dma("sp", fw1[:], I["hy_fw1"][layer], writes=["fw1"])
                self.dma("sp", fb1[:], I["hy_fb1"][layer].rearrange("(p o) -> p o", o=1), writes=["fb1"], allow_slow_non_contiguous=True)
                self.dma("sp", fw2[:], I["hy_fw2"][layer], writes=["fw2"])
                self.dma("sp", fb2[:], I["hy_fb2"][layer].rearrange("(p o) -> p o", o=1), writes=["fb2"], allow_slow_non_contiguous=True)
                self.dma("sp", fw3[0:64, :], I["hy_fw3"][layer], writes=["fw3"])
                self.dma("sp", fw3[64:65, :], I["hy_fb3"][layer].rearrange("(o n) -> o n", o=1), writes=["fw3"])
                self.dma("sp", negr[:], I["negrates"].partition_broadcast(128), writes=["negr"])
                self.dma("sp", tau[:], I["tau_%d" % l], writes=["tau"])
                P.op("pool", lambda e: e.memset(h2T[64:65, :], 1.0), [], ["h2ones"])
                P.op("pool", lambda e: e.memset(onesb[:], 1.0), [], ["onesb"])

                def sin_layer(w, bcol, src, srck, dst, dstk, kdim):
                    for cc in range(ncol):
                        s = cc % 2
                        c0 = cc * CW
                        P.op("pe", lambda e: e.matmul(pm[s][0:64, 0:CW], lhsT=w[0:kdim, :], rhs=src[0:kdim, c0:c0 + CW], start=True, stop=True),
                             [srck, "fw1", "fw2"], ["pmH%d" % s])
                        P.op("dve", lambda e: e.tensor_scalar(out=pre[s][:, 0:CW], in0=pm[s][0:64, 0:CW], scalar1=bcol[:, 0:1], scalar2=None,
                                                              op0=ALU.add), ["pmH%d" % s, "fb1", "fb2"], ["preH%d" % s])
                        P.op("dve", lambda e: e.tensor_scalar(out=kk[s][:, 0:CW], in0=pre[s][:, 0:CW], scalar1=1.0 / TWO_PI, scalar2=MAGIC,
                                                              op0=ALU.mult, op1=ALU.add), ["preH%d" % s], ["kkH%d" % s])
                        P.op("dve", lambda e: e.tensor_scalar(out=kk[s][:, 0:CW], in0=kk[s][:, 0:CW], scalar1=MAGIC, scalar2=None,
                                                              op0=ALU.subtract), ["kkH%d" % s], ["kkH%d" % s])
                        P.op("dve", lambda e: e.scalar_tensor_tensor(out=pre[s][:, 0:CW], in0=kk[s][:, 0:CW], scalar=-TWO_PI, in1=pre[s][:, 0:CW],
                                                                     op0=ALU.mult, op1=ALU.add), ["kkH%d" % s, "preH%d" % s], ["preH%d" % s])
                        P.op("act", lambda e: e.activation(out=dst[0:64, c0:c0 + CW], in_=pre[s][:, 0:CW], func=AF.Sin), ["preH%d" % s], [dstk])

                sin_layer(fw1, fb1, embT, "embT", h1T, "h1T", 33)
                sin_layer(fw2, fb2, h1T, "h1T", h2T, "h2T", 64)
                for a in range(nA):
                    s = a % 2
                    P.op("act", lambda e: e.activation(out=wt[s][:], in_=negr[:], func=AF.Exp, scale=tau[:, a:a + 1]), ["negr", "tau"], ["wtH%d" % s])
                    filt = filt2[s]
                    fk = "filtH%d" % s
                    for o in range(2):
                        P.op("pe", lambda e: e.matmul(p3[o][:], lhsT=h2T[0:65, a * 128:(a + 1) * 128], rhs=fw3[0:65, o * 512:(o + 1) * 512],
                                                      start=True, stop=True), ["h2T", "h2ones", "fw3"], ["p3H%d" % o])
                        P.op("dve", lambda e: e.tensor_tensor(out=filt[:, o * 512:(o + 1) * 512].rearrange("p (d c) -> p d c", d=2),
                                                              in0=p3[o][:].rearrange("p (d c) -> p d c", d=2),
                                                              in1=wt[s][:].unsqueeze(1).to_broadcast([128, 2, 256]), op=ALU.mult),
                             ["p3H%d" % o, "wtH%d" % s], [fk])
                    if a == 0:
                        for o in range(2):
                            P.op("dve", lambda e: e.memset(filt[0:1, o * 512 + 256:o * 512 + 512], 0.0), [fk], [fk])
                    fv = filt[:].rearrange("p (o d c) -> p o d c", o=2, d=2)
                    P.op("pool", lambda e: e.tensor_tensor(out=fs[:, a, 0, :].rearrange("p (o c) -> p o c", o=2), in0=fv[:, :, 0, :], in1=fv[:, :, 1, :],
                                                           op=ALU.add), [fk], ["fs"])
                    P.op("pool", lambda e: e.tensor_tensor(out=fs[:, a, 1, :].rearrange("p (o c) -> p o c", o=2), in0=fv[:, :, 1, :], in1=fv[:, :, 0, :],
                                                           op=ALU.subtract), [fk], ["fs"])
                    P.op("act", lambda e: e.activation(out=absf[s][:], in_=filt[:], func=AF.Abs), [fk], ["absfH%d" % s])
                    for o in range(2):
                        P.op("pe", lambda e: e.matmul(pn[o][:], lhsT=onesb[:], rhs=absf[s][:, o * 512:(o + 1) * 512], start=(a == 0),
                                                      stop=(a == nA - 1)), ["absfH%d" % s, "onesb"], ["pnH%d" % o])
                for o in range(2):
                    P.op("act", lambda e: e.copy(out=nsum[:, o, :], in_=pn[o][:, 0:256]), ["pnH%d" % o], ["nsum"])
                    P.op("dve", lambda e: e.tensor_tensor(out=nsum[:, o, :], in0=nsum[:, o, :], in1=pn[o][:, 256:512], op=ALU.add),
                         ["pnH%d" % o, "nsum"], ["nsum"])
                P.op("dve", lambda e: e.reciprocal(out=rn[:], in_=nsum[:]), ["nsum"], ["rn"])
                P.flush()
            with ExitStack() as es:
                ng = max(1, nA // 2)
                GW = min(256, l)
                nfc = GW // 128
                tc = [self.sb(es, "tcF%d" % s, [128, nA, GW], BF16) for s in range(2)]
                ts = [self.sb(es, "tsF%d" % s, [128, nA, GW], BF16) for s in range(2)]
                kfo = [self.sb(es, "kfoF%d" % s, [128, 2, 2, 256]) for s in range(2)]
                pA = [self.ps(es, "pAF%d" % s, [128, 512]) for s in range(4)]
                ctf = S["ctf_%d" % l].rearrange("(a p) f -> p a f", p=128)
                stf = S["stf_%d" % l].rearrange("(a p) f -> p a f", p=128)
                ci = 0
                for g in range(ng):
                    s = g % 2
                    self.dma("sp", tc[s][:], ctf[:, :, g * GW:(g + 1) * GW], writes=["tcF%d" % s])
                    self.dma("sp", ts[s][:], stf[:, :, g * GW:(g + 1) * GW], writes=["tsF%d" % s])
                    for fc in range(nfc):
                        ks = ci % 2
                        ci += 1
                        for a in range(nA):
                            P.op("pe", lambda e: e.matmul(pA[ks * 2][:], lhsT=tc[s][:, a, fc * 128:(fc + 1) * 128], rhs=fs[:, a, 0, :], start=(a == 0),
                                                          stop=(a == nA - 1)), ["tcF%d" % s, "fs"], ["pAF%d" % (ks * 2)])
                            P.op("pe", lambda e: e.matmul(pA[ks * 2 + 1][:], lhsT=ts[s][:, a, fc * 128:(fc + 1) * 128], rhs=fs[:, a, 1, :], start=(a == 0),
                                                          stop=(a == nA - 1)), ["tsF%d" % s, "fs"], ["pAF%d" % (ks * 2 + 1)])
                        for q in range(2):
                            j = ks * 2 + q
                            P.op("dve", lambda e: e.tensor_tensor(out=kfo[ks][:, q, :, :], in0=pA[j][:].rearrange("p (o c) -> p o c", o=2), in1=rn[:],
                                                                  op=ALU.mult), ["pAF%d" % j, "rn"], ["kfoF%d" % ks])
                        self.dma("sp", S["kf_%d" % l][g * nfc + fc], kfo[ks][:], reads=["kfoF%d" % ks])
                P.flush()

    def hy_forward(self, layer, l, o, Z, Y):
        cfg, P, I, S = self.cfg, self.P, self.I, self.S
        nA = l // 128
        with ExitStack() as es:
            ng = max(1, nA // 2)
            GW = min(256, l)
            nfc = GW // 128
            tc = [self.sb(es, "tcW%d" % s, [128, nA, GW], BF16) for s in range(2)]
            ts = [self.sb(es, "tsW%d" % s, [128, nA, GW], BF16) for s in range(2)]
            sA = [self.sb(es, "sAW%d" % s, [128, 2, 256]) for s in range(4)]
            t1 = [self.sb(es, "t1W%d" % s, [128, 2, 256]) for s in range(2)]
            t2 = [self.sb(es, "t2W%d" % s, [128, 2, 256]) for s in range(2)]
            kfc = [self.sb(es, "kfcW%d" % s, [128, 2, 256]) for s in range(2)]
            pA = [self.ps(es, "pAW%d" % s, [128, 512]) for s in range(4)]
            ctf = S["ctf_%d" % l].rearrange("(a p) f -> p a f", p=128)
            stf = S["stf_%d" % l].rearrange("(a p) f -> p a f", p=128)
            ci = 0
            for g in range(ng):
                s = g % 2
                self.dma("sp", tc[s][:], ctf[:, :, g * GW:(g + 1) * GW], writes=["tcW%d" % s])
                self.dma("sp", ts[s][:], stf[:, :, g * GW:(g + 1) * GW], writes=["tsW%d" % s])
                for fc in range(nfc):
                    ks = ci % 2
                    ci += 1
                    fch = g * nfc + fc
                    self.dma("sp", kfc[ks][:], S["kf_%d" % l][fch][:, :, o, :], writes=["kfcW%d" % ks])
                    jc, js = ks * 2, ks * 2 + 1
                    for a in range(nA):
                        P.op("pe", lambda e: e.matmul(pA[jc][:], lhsT=tc[s][:, a, fc * 128:(fc + 1) * 128], rhs=Z[:, a, :], start=(a == 0),
                                                      stop=(a == nA - 1)), ["tcW%d" % s, "Z%d" % a], ["pAW%d" % jc])
                        P.op("pe", lambda e: e.matmul(pA[js][:], lhsT=ts[s][:, a, fc * 128:(fc + 1) * 128], rhs=Z[:, a, :], start=(a == 0),
                                                      stop=(a == nA - 1)), ["tsW%d" % s, "Z%d" % a], ["pAW%d" % js])
                    P.op("act", lambda e: e.copy(out=sA[jc][:].rearrange("p b c -> p (b c)"), in_=pA[jc][:]), ["pAW%d" % jc], ["sAW%d" % jc])
                    P.op("act", lambda e: e.copy(out=sA[js][:].rearrange("p b c -> p (b c)"), in_=pA[js][:]), ["pAW%d" % js], ["sAW%d" % js])
                    kre = kfc[ks][:, 0, :].unsqueeze(1).to_broadcast([128, 2, 256])
                    kim = kfc[ks][:, 1, :].unsqueeze(1).to_broadcast([128, 2, 256])
                    kk = "kfcW%d" % ks
                    yre = Y[:, fch, 0, :].rearrange("p (b c) -> p b c", b=2)
                    yim = Y[:, fch, 1, :].rearrange("p (b c) -> p b c", b=2)
                    P.op("dve", lambda e: e.tensor_tensor(out=t1[0][:], in0=sA[jc][:], in1=kre, op=ALU.mult), ["sAW%d" % jc, kk], ["t1W0"])
                    P.op("pool", lambda e: e.tensor_tensor(out=t2[0][:], in0=sA[js][:], in1=kim, op=ALU.mult), ["sAW%d" % js, kk], ["t2W0"])
                    P.op("dve", lambda e: e.tensor_tensor(out=yre, in0=t1[0][:], in1=t2[0][:], op=ALU.add), ["t1W0", "t2W0"], ["Y"])
                    P.op("dve", lambda e: e.tensor_tensor(out=t1[1][:], in0=sA[js][:], in1=kre, op=ALU.mult), ["sAW%d" % js, kk], ["t1W1"])
                    P.op("pool", lambda e: e.tensor_tensor(out=t2[1][:], in0=sA[jc][:], in1=kim, op=ALU.mult), ["sAW%d" % jc, kk], ["t2W1"])
                    P.op("dve", lambda e: e.tensor_tensor(out=yim, in0=t1[1][:], in1=t2[1][:], op=ALU.subtract), ["t1W1", "t2W1"], ["Y"])
            P.flush()

    def hy_inverse(self, layer, l, tok0, o, Z, Y):
        cfg, P, I, S = self.cfg, self.P, self.I, self.S
        nA = l // 128
        with ExitStack() as es:
            ng = max(1, nA // 2)
            GW = min(256, l)
            ntc = GW // 128
            tc = [self.sb(es, "tcV%d" % s, [128, nA, GW], BF16) for s in range(2)]
            ts = [self.sb(es, "tsV%d" % s, [128, nA, GW], BF16) for s in range(2)]
            hb = self.sb(es, "hbV", [128, 256])
            tz = [self.sb(es, "tzV%d" % s, [128, 2, 256]) for s in range(2)]
            yv = [self.sb(es, "yvV%d" % s, [128, 2, 256]) for s in range(2)]
            xg = [self.sb(es, "xgV%d" % s, [128, 2, 256]) for s in range(2)]
            og = [self.sb(es, "ogV%d" % s, [128, 2, 256], BF16) for s in range(2)]
            py = [self.ps(es, "pyV%d" % s, [128, 512]) for s in range(2)]
            cft = S["cft_%d" % l].rearrange("(a p) t -> p a t", p=128)
            sft = S["sft_%d" % l].rearrange("(a p) t -> p a t", p=128)
            self.dma("sp", hb[:], I["hy_bias"][layer, o].partition_broadcast(128), writes=["hbV"])
            ci = 0
            for g in range(ng):
                s = g % 2
                self.dma("sp", tc[s][:], cft[:, :, g * GW:(g + 1) * GW], writes=["tcV%d" % s])
                self.dma("sp", ts[s][:], sft[:, :, g * GW:(g + 1) * GW], writes=["tsV%d" % s])
                for tcn in range(ntc):
                    ks = ci % 2
                    ci += 1
                    a = g * ntc + tcn
                    r0 = tok0 + a * 128
                    for b in range(cfg.NB):
                        self.dma("sp", xg[ks][:, b, :], S["hx"][b, r0:r0 + 128, o * 256:(o + 1) * 256], writes=["xgV%d" % ks])
                    for fc in range(nA):
                        P.op("pe", lambda e: e.matmul(py[ks][:], lhsT=tc[s][:, fc, tcn * 128:(tcn + 1) * 128], rhs=Y[:, fc, 0, :], start=(fc == 0),
                                                      stop=False), ["tcV%d" % s, "Y"], ["pyV%d" % ks])
                        P.op("pe", lambda e: e.matmul(py[ks][:], lhsT=ts[s][:, fc, tcn * 128:(tcn + 1) * 128], rhs=Y[:, fc, 1, :], start=False,
                                                      stop=(fc == nA - 1)), ["tsV%d" % s, "Y"], ["pyV%d" % ks])
                    zv = Z[:, a, :].rearrange("p (b c) -> p b c", b=2)
                    hbb = hb[:].unsqueeze(1).to_broadcast([128, 2, 256])
                    P.op("pool", lambda e: e.tensor_tensor(out=tz[ks][:], in0=zv, in1=hbb, op=ALU.mult), ["Z%d" % a, "hbV"], ["tzV%d" % ks])
                    P.op("dve", lambda e: e.scalar_tensor_tensor(out=yv[ks][:], in0=py[ks][:].rearrange("p (b c) -> p b c", b=2), scalar=1.0 / l,
                                                                 in1=tz[ks][:], op0=ALU.mult, op1=ALU.add), ["pyV%d" % ks, "tzV%d" % ks], ["yvV%d" % ks])
                    if o == 0:
                        P.op("dve", lambda e: e.tensor_tensor(out=zv, in0=yv[ks][:], in1=xg[ks][:], op=ALU.mult),
                             ["yvV%d" % ks, "xgV%d" % ks], ["Z%d" % a])
                    else:
                        P.op("dve", lambda e: e.tensor_tensor(out=og[ks][:], in0=yv[ks][:], in1=xg[ks][:], op=ALU.mult),
                             ["yvV%d" % ks, "xgV%d" % ks], ["ogV%d" % ks])
                        for b in range(cfg.NB):
                            self.dma("sp", S["ox"][b, r0:r0 + 128, 768:1024], og[ks][:, b, :], reads=["ogV%d" % ks])
            P.flush()

    def bcast_vecs(self, es, tag, layer, bias_name, gate_slot, lng, lnb):
        P, I, S = self.P, self.I, self.S
        V = dict(tag=tag)
        bt = self.sb(es, "bB" + tag, [128, D])
        self.dma("sp", bt[:], I[bias_name][layer].partition_broadcast(128), writes=["bB" + tag])
        V["g"], V["bg"] = [], []
        for j in range(3):
            g = self.sb(es, "gB%s%d" % (tag, j), [128, D])
            bg = self.sb(es, "bgB%s%d" % (tag, j), [128, D])
            self.dma("sp", g[:], S["mod"][j, gate_slot * D:(gate_slot + 1) * D].partition_broadcast(128), writes=["gB%s%d" % (tag, j)])
            P.op("pool", lambda e: e.tensor_tensor(out=bg[:], in0=bt[:], in1=g[:], op=ALU.mult), ["bB" + tag, "gB%s%d" % (tag, j)],
                 ["bgB%s%d" % (tag, j)])
            V["g"].append(g)
            V["bg"].append(bg)
        V["lng"] = self.sb(es, "lngB" + tag, [128, D])
        V["lnb"] = self.sb(es, "lnbB" + tag, [128, D])
        self.dma("sp", V["lng"][:], I[lng][layer].partition_broadcast(128), writes=["lngB" + tag])
        self.dma("sp", V["lnb"][:], I[lnb][layer].partition_broadcast(128), writes=["lnbB" + tag])
        return V

    def resid_ln(self, acc, acck, xt, xk, V, j, Wk, slot, out, outk, aux="pool"):
        P = self.P
        tag = V["tag"]
        t, st, mv, rs = Wk["t"][slot], Wk["st"][slot], Wk["mv"][slot], Wk["rs"][slot]
        sk = "%s%d" % (Wk["tag"], slot)
        alpha = (2.0 * self.cfg.DEPTH) ** 0.25
        for n in range(2):
            P.op("dve", lambda e: e.tensor_tensor(out=t[:, n * 512:(n + 1) * 512], in0=acc[n], in1=V["g"][j][:, n * 512:(n + 1) * 512], op=ALU.mult),
                 [acck[n], "gB%s%d" % (tag, j)], ["t" + sk])
        P.op(aux, lambda e: e.tensor_tensor(out=t[:], in0=t[:], in1=V["bg"][j][:], op=ALU.add), ["t" + sk, "bgB%s%d" % (tag, j)], ["t" + sk])
        P.op("dve", lambda e: e.scalar_tensor_tensor(out=t[:], in0=xt, scalar=alpha, in1=t[:], op0=ALU.mult, op1=ALU.add), [xk, "t" + sk], ["t" + sk])
        for h in range(2):
            P.op("dve", lambda e: e.bn_stats(out=st[:, h, :], in_=t[:, h * 512:(h + 1) * 512]), ["t" + sk], ["st" + sk + str(h)])
        P.op("dve", lambda e: e.bn_aggr(out=mv[:], in_=st[:].rearrange("p a b -> p (a b)")), ["st" + sk + "0", "st" + sk + "1"], ["mv" + sk])
        P.op("act", lambda e: e.activation(out=rs[:], in_=mv[:, 1:2], func=AF.Sqrt, bias=LN_EPS), ["mv" + sk], ["rs" + sk])
        P.op("dve", lambda e: e.reciprocal(out=rs[:], in_=rs[:]), ["rs" + sk], ["rs" + sk])
        P.op("dve", lambda e: e.tensor_scalar(out=t[:], in0=t[:], scalar1=mv[:, 0:1], scalar2=rs[:], op0=ALU.subtract, op1=ALU.mult),
             ["t" + sk, "mv" + sk, "rs" + sk], ["t" + sk])
        P.op(aux, lambda e: e.tensor_tensor(out=t[:], in0=t[:], in1=V["lng"][:], op=ALU.mult), ["t" + sk, "lngB" + tag], ["t" + sk])
        P.op("dve", lambda e: e.tensor_tensor(out=out, in0=t[:], in1=V["lnb"][:], op=ALU.add), ["t" + sk, "lnbB" + tag], [outk])

    def resid_ws(self, es, tag, nslot=2):
        Wk = dict(tag=tag)
        Wk["t"] = [self.sb(es, "tR%s%d" % (tag, s), [128, D]) for s in range(nslot)]
        Wk["st"] = [self.sb(es, "stR%s%d" % (tag, s), [128, 2, 6]) for s in range(nslot)]
        Wk["mv"] = [self.sb(es, "mvR%s%d" % (tag, s), [128, 2]) for s in range(nslot)]
        Wk["rs"] = [self.sb(es, "rsR%s%d" % (tag, s), [128, 1]) for s in range(nslot)]
        return Wk

    def phase_DE1(self, layer, last):
        cfg, P, I, S = self.cfg, self.P, self.I, self.S
        with ExitStack() as es:
            wout = self.sb(es, "wout", [128, 8, D], BF16)
            wff1 = self.sb(es, "wff1", [128, 8, DFF], BF16)
            self.dma("pool", wout[:], I["w_out"][layer].rearrange("(k p) n -> p k n", p=128), writes=["wout"])
            w1v = I["w_ff1"][layer].rearrange("(k p) n -> p k n", p=128)
            for k in range(8):
                self.dma("pool", wff1[:, k, :], w1v[:, k, :], writes=["wff1"])
            bf1 = self.sb(es, "bf1", [128, 32])
            self.dma("sp", bf1[:], I["b_ff1"][layer].rearrange("(c p) -> p c", p=128), writes=["bf1"], allow_slow_non_contiguous=True)
            V = self.bcast_vecs(es, "D", layer, "b_out", 2, "ln1_g", "ln1_b")
            mod2 = self.load_mod_fm(es, "modE", 3, 4)
            W = self.ln_ws(es, "E", "modE", nslot=4, npslot=1)
            Wk = self.resid_ws(es, "D")
            oxb = [self.sb(es, "oxbD%d" % s, [128, D], BF16) for s in range(2)]
            oxT = [self.sb(es, "oxTD%d" % s, [128, 8, 128], BF16) for s in range(2)]
            xt = [self.sb(es, "xtD%d" % s, [128, D]) for s in range(2)]
            x1t = [self.sb(es, "x1tD%d" % s, [128, D]) for s in range(2)]
            xm2T = [self.sb(es, "xm2TD%d" % s, [128, 8, 256], BF16) for s in range(2)]
            rt = [self.sb(es, "rtD%d" % s, [128, 256]) for s in range(2)]
            hT = self.sb(es, "hTD0", [128, 32, 256], BF16)
            ptr = self.ps(es, "ptrD", [128, 8, 128], BF16)
            pw = [[self.ps(es, "pwD%d_%d" % (ti, nn), [128, 512]) for nn in range(2)] for ti in range(2)]
            pf = [self.ps(es, "pfD%d" % s, [128, 512]) for s in range(2)]
            groups = [(b, g) for b in range(cfg.NB) for g in range(cfg.NG) if not (last and g == 0)]
            fi = [0]

            def SA_pe(q):
                b, g = groups[q]
                for ti in range(2):
                    n = q * 2 + ti
                    s = n % 2
                    i = g * 2 + ti
                    r0 = i * 128
                    K = lambda nm: "%sD%d" % (nm, s)
                    self.dma("sp", oxb[s][:], S["ox"][b, r0:r0 + 128, :], writes=[K("oxb")])
                    self.dma("sp", xt[s][:], self.x_src(layer, b, i), writes=[K("xt")])
                    for k in range(8):
                        P.op("pe", lambda e: e.transpose(out=ptr[:, k, :], in_=oxb[s][:, k * 128:(k + 1) * 128], identity=self.idb[:]),
                             [K("oxb"), "idb"], ["ptrD"])
                    P.op("act", lambda e: e.copy(out=oxT[s][:], in_=ptr[:]), ["ptrD"], [K("oxT")])
                    for nn in range(2):
                        for k in range(8):
                            P.op("pe", lambda e: e.matmul(pw[ti][nn][:], lhsT=oxT[s][:, k, :], rhs=wout[:, k, nn * 512:(nn + 1) * 512], start=(k == 0),
                                                          stop=(k == 7)), [K("oxT"), "wout"], ["pwD%d_%d" % (ti, nn)])

            def chain_ops(q):
                b, g = groups[q]
                j = 2 if g == 0 else b
                P.defer = []
                for ti in range(2):
                    n = q * 2 + ti
                    s, s4 = n % 2, n % 4
                    r0 = (g * 2 + ti) * 128
                    K = lambda nm: "%sD%d" % (nm, s)
                    self.resid_ln([pw[ti][0][:], pw[ti][1][:]], ["pwD%d_0" % ti, "pwD%d_1" % ti], xt[s][:], K("xt"), V, j, Wk, s, x1t[s][:], K("x1t"), aux="dve")
                    self.dma("sp", S["xs"][b, r0:r0 + 128, :], x1t[s][:], reads=[K("x1t")])
                    self.ln_chain(x1t[s][:], K("x1t"), W, s4)
                lst = P.defer
                P.defer = None
                out = []
                for it_ in lst:
                    if it_[0] == "act":
                        out += [None] * 3 + [it_] + [None] * 3
                    else:
                        out.append(it_)
                return out

            def SB(q):
                b, g = groups[q]
                j = 2 if g == 0 else b
                for ti in range(2):
                    n = q * 2 + ti
                    self.ln_T(W, n % 4, 0, mod2, j, xm2T[q % 2], "xm2TD%d" % (q % 2), ti * 128)

            def SF(q, pend):
                b, g = groups[q]
                gs = q % 2
                per = (len(pend) + 31) // 32
                for fc in range(32):
                    fs = fi[0] % 2
                    fi[0] += 1
                    hk = "hTD_%d" % (fc // 8)
                    for k in range(8):
                        P.op("pe", lambda e: e.matmul(pf[fs][:, 0:256], lhsT=wff1[:, k, fc * 128:(fc + 1) * 128], rhs=xm2T[gs][:, k, :], start=(k == 0),
                                                      stop=(k == 7)), ["wff1", "xm2TD%d" % gs], ["pfD%d" % fs])
                    P.op("act", lambda e: e.activation(out=rt[fs][:], in_=pf[fs][:, 0:256], func=AF.Relu, bias=bf1[:, fc:fc + 1]),
                         ["pfD%d" % fs, "bf1"], ["rtD%d" % fs])
                    P.op("pool", lambda e: e.tensor_tensor(out=hT[:, fc, :], in0=rt[fs][:], in1=rt[fs][:], op=ALU.mult), ["rtD%d" % fs], [hk])
                    if fc % 8 == 7:
                        c8 = fc // 8
                        self.dma("sp", S["ht"][b].rearrange("c p t -> p c t")[:, c8 * 8:(c8 + 1) * 8, g * 256:(g + 1) * 256],
                                 hT[:, c8 * 8:(c8 + 1) * 8, :], reads=[hk])
                    for _ in range(per):
                        if pend:
                            it_ = pend.pop(0)
                            if it_ is not None:
                                P.add(*it_)
                while pend:
                    it_ = pend.pop(0)
                    if it_ is not None:
                        P.add(*it_)

            NQ = len(groups)
            for q0 in range(min(2, NQ)):
                SA_pe(q0)
                for it_ in chain_ops(q0):
                    if it_ is not None:
                        P.add(*it_)
            SB(0)
            for q in range(NQ):
                pend = []
                if q + 2 < NQ:
                    SA_pe(q + 2)
                    pend = chain_ops(q + 2)
                if q + 1 < NQ:
                    SB(q + 1)
                SF(q, pend)
            P.flush()

    def phase_E2(self, layer, last):
        cfg, P, I, S = self.cfg, self.P, self.I, self.S
        with ExitStack() as es:
            wff2 = self.sb(es, "wff2", [128, 32, D], BF16)
            w2v = I["w_ff2"][layer].rearrange("(k p) n -> p k n", p=128)
            for k4 in range(8):
                self.dma("pool", wff2[:, k4 * 4:(k4 + 1) * 4, :], w2v[:, k4 * 4:(k4 + 1) * 4, :], writes=["wff2"])
            V = self.bcast_vecs(es, "F", layer, "b_ff2", 5, "ln2_g", "ln2_b")
            Wk = self.resid_ws(es, "F")
            hT = [self.sb(es, "hTF%d" % s, [128, 32, 256], BF16) for s in range(2)]
            xt = [self.sb(es, "xtF%d" % s, [128, D]) for s in range(2)]
            x2t = [self.sb(es, "x2tF%d" % s, [128, D]) for s in range(2)]
            po = [self.ps(es, "poF%d" % s, [128, 512]) for s in range(4)]
            it, gi = 0, 0
            nct = cfg.CTX // 128
            for b in range(cfg.NB):
                for g in range(cfg.NG):
                    if last and g == 0:
                        continue
                    j = 2 if g == 0 else b
                    gs = gi % 2
                    gi += 1
                    self.dma("sp", hT[gs][:], S["ht"][b].rearrange("c p t -> p c t")[:, :, g * 256:(g + 1) * 256], writes=["hTF%d" % gs])
                    for ti in range(2):
                        s = it % 2
                        it += 1
                        i = g * 2 + ti
                        r0 = i * 128
                        K = lambda n: "%sF%d" % (n, s)
                        self.dma("sp", xt[s][:], S["xs"][b, r0:r0 + 128, :], reads=["xs%d_%d" % (b, i)], writes=[K("xt")])
                        for n in range(2):
                            pb = s * 2 + n
                            for k in range(32):
                                P.op("pe", lambda e: e.matmul(po[pb][:], lhsT=hT[gs][:, k, ti * 128:(ti + 1) * 128], rhs=wff2[:, k, n * 512:(n + 1) * 512],
                                                              start=(k == 0), stop=(k == 31)), ["hTF%d" % gs, "wff2"], ["poF%d" % pb])
                        self.resid_ln([po[s * 2][:], po[s * 2 + 1][:]], ["poF%d" % (s * 2), "poF%d" % (s * 2 + 1)], xt[s][:], K("xt"), V, j, Wk, s,
                                      x2t[s][:], K("x2t"))
                        if last:
                            dst = self.out[b, (i - nct) * 128:(i - nct + 1) * 128, :]
                        else:
                            dst = S["xs"][b, r0:r0 + 128, :]
                        self.dma("sp", dst, x2t[s][:], reads=[K("x2t")], writes=["xs%d_%d" % (b, i)])
            P.flush()


def build_program(cfg):
    bld = Builder(cfg)
    nc = bld.build()
    return bld, nc


_CACHE = {}


def kernel(**inputs):
    cfg = Cfg(SEQ=4096, CTX=256, DEPTH=4, NB=2)
    n_cores = 8
    if "prog" not in _CACHE:
        _CACHE["prog"] = build_program(cfg)
    bld, nc = _CACHE["prog"]
    consts = host_consts(cfg)
    in_maps = []
    for c in range(n_cores):
        m = {}
        sl = slice(c * cfg.NB, (c + 1) * cfg.NB)
        m["x"] = np.ascontiguousarray(inputs["x"][sl], dtype=np.float32)
        m["ctx"] = np.ascontiguousarray(inputs["ctx"][sl], dtype=np.float32)
        m["c"] = np.ascontiguousarray(inputs["c"][sl], dtype=np.float32)
        m["c_ctx"] = np.ascontiguousarray(inputs["c_ctx"], dtype=np.float32)
        for n, _ in WSPEC:
            m[n] = np.ascontiguousarray(inputs[n], dtype=np.float32)
        m.update(consts)
        in_maps.append(m)
    res = run_bass_kernel_spmd(nc, in_maps, core_ids=list(range(n_cores)))
    return np.concatenate([np.asarray(r["out"], dtype=np.float32) for r in res.results], axis=0)
```

```python
import math
from contextlib import ExitStack

import numpy as np
import ml_dtypes
import concourse.bass as bass
import concourse.mybir as mybir
from concourse.bass_utils import run_bass_kernel_spmd

F32 = mybir.dt.float32
BF16 = mybir.dt.bfloat16
AF = mybir.ActivationFunctionType
ALU = mybir.AluOpType

ENGS = ("pe", "act", "dve", "pool", "sp")
DMAQ = ("sp", "pool", "act")

LN_EPS = 1e-6
ROPE_BASE = 10000.0
GRID_W = 64
D = 1024
DIN = 2336
DFF = 4096
MAGIC = 12582912.0
SAME_ENGINE_SYNC = True
HOIST_SP = True
HOIST_SET = ("init", "mod", "A", "B", "C", "H", "DE1", "E2")


class _Rec:
    def __init__(self):
        self.call = None

    def __getattr__(self, name):
        def f(*a, **k):
            self.call = (name, a, k)
            return self
        return f


class Prog:
    RING = 12

    def __init__(self, nc, es, same_engine_sync=True):
        self.nc = nc
        self.same_engine_sync = same_engine_sync
        self.esem = {e: es.enter_context(nc.semaphore("s_" + e)) for e in ENGS}
        self.rings = {e: [es.enter_context(nc.semaphore("d_%s_%d" % (e, j))) for j in range(self.RING)]
                      for e in DMAQ}
        self.fsem = es.enter_context(nc.semaphore("fence"))
        self.cnt = {e: 0 for e in ENGS}
        self.dcnt = {e: 0 for e in DMAQ}
        self.waited = {e: {} for e in ENGS}
        self.nfence = 0
        self.n_emitted = 0
        self.hoist_sp = HOIST_SP
        self.defer = None
        self._reset()

    def _reset(self):
        self.ops = []
        self.last_writer = {}
        self.readers = {}

    def op(self, eng, fn, reads=(), writes=(), dma=False):
        rec = _Rec()
        fn(rec)
        assert rec.call is not None
        if self.defer is not None:
            self.defer.append((eng, rec.call, tuple(reads), tuple(writes), dma))
            return None
        return self.add(eng, rec.call, reads, writes, dma)

    def add(self, eng, call, reads=(), writes=(), dma=False):
        i = len(self.ops)
        deps = set()
        for k in reads:
            lw = self.last_writer.get(k)
            if lw is not None:
                deps.add(lw)
        for k in writes:
            lw = self.last_writer.get(k)
            if lw is not None:
                deps.add(lw)
            r = self.readers.get(k)
            if r:
                deps.update(r["eng"].values())
                deps.update(r["dma"])
        for k in reads:
            r = self.readers.setdefault(k, {"eng": {}, "dma": []})
            if dma:
                r["dma"].append(i)
            else:
                r["eng"][eng] = i
        for k in writes:
            self.last_writer[k] = i
            self.readers[k] = {"eng": {}, "dma": []}
        deps.discard(i)
        self.ops.append(dict(eng=eng, call=call, deps=deps, dma=dma))
        return i

    def _skip(self, od, o):
        if od["dma"] or o["dma"]:
            return False
        if od["eng"] == o["eng"]:
            if o["eng"] == "pe":
                return True
            if not self.same_engine_sync:
                return True
        return False

    def flush(self):
        nc = self.nc
        ops = self.ops
        if not ops:
            return
        need = [False] * len(ops)
        for o in ops:
            for d in o["deps"]:
                od = ops[d]
                if od["dma"] or self._skip(od, o):
                    continue
                need[d] = True
        per = {e: [i for i, o in enumerate(ops) if o["eng"] == e] for e in ENGS}
        if self.hoist_sp:
            per["sp"].sort(key=lambda i: (max(ops[i]["deps"]) if ops[i]["deps"] else -1, i))
        for e in ENGS:
            for i in reversed(per[e]):
                if not ops[i]["dma"]:
                    need[i] = True
                    break
        for e in ENGS:
            for i in per[e]:
                o = ops[i]
                if o["dma"]:
                    j = self.dcnt[e]
                    self.dcnt[e] += 1
                    o["sem"] = self.rings[e][j % self.RING]
                    o["val"] = 16 * (j // self.RING + 1)
                    o["ringprev"] = (o["sem"], o["val"] - 16) if j >= self.RING else None
                elif need[i]:
                    self.cnt[e] += 1
                    o["sem"] = self.esem[e]
                    o["val"] = self.cnt[e]
        final = {}
        for e in DMAQ:
            n = self.dcnt[e]
            for j in range(min(n, self.RING)):
                final[self.rings[e][j]] = 16 * ((n - 1 - j) // self.RING + 1)
        for e in ENGS:
            if self.cnt[e] > 0:
                final[self.esem[e]] = self.cnt[e]
        fence_in = self.nfence
        self.nfence += 1

        def run(e, engobj):
            waited = self.waited[e]
            if fence_in > 0 and e != "sp":
                engobj.wait_ge(self.fsem, fence_in)
            for i in per[e]:
                o = ops[i]
                want = {}
                for d in o["deps"]:
                    od = ops[d]
                    if self._skip(od, o):
                        continue
                    s, v = od["sem"], od["val"]
                    if want.get(s, 0) < v:
                        want[s] = v
                if o["dma"] and o["ringprev"] is not None:
                    s, v = o["ringprev"]
                    if want.get(s, 0) < v:
                        want[s] = v
                for s, v in want.items():
                    if waited.get(s, 0) < v:
                        engobj.wait_ge(s, v)
                        waited[s] = v
                name, a, k = o["call"]
                inst = getattr(engobj, name)(*a, **k)
                self.n_emitted += 1
                if o["dma"]:
                    inst.then_inc(o["sem"], 16)
                elif need[i]:
                    inst.then_inc(o["sem"], 1)
            if e == "sp":
                for s, v in final.items():
                    if waited.get(s, 0) < v:
                        engobj.wait_ge(s, v)
                        waited[s] = v
                engobj.sem_inc(self.fsem, 1)

        with nc.Block() as block:
            @block.tensor
            def _(t):
                run("pe", t)

            @block.scalar
            def _(t):
                run("act", t)

            @block.vector
            def _(t):
                run("dve", t)

            @block.gpsimd
            def _(t):
                run("pool", t)

            @block.sync
            def _(t):
                run("sp", t)
        self._reset()


class Cfg:
    def __init__(self, SEQ=4096, CTX=256, DEPTH=4, NB=2, debug=False, stop_after=None):
        self.SEQ, self.CTX, self.DEPTH, self.NB = SEQ, CTX, DEPTH, NB
        self.T = SEQ + CTX
        self.NT = self.T // 128
        self.NG = self.T // 256
        self.debug = debug
        self.stop_after = stop_after


WSPEC = [
    ("w_mod", (D, 6 * D)), ("b_mod", (6 * D,)), ("w_in", (D, DIN)), ("mla_q_norm", (256,)),
    ("w_q_up", (256, 576)), ("mla_kv_norm", (128,)), ("w_kv_up", (128, 768)), ("diff_lambda", (4, 48)),
    ("diff_subln", (96,)), ("hy_conv_w", (3, 768)), ("hy_conv_b", (768,)), ("hy_fw1", (33, 64)),
    ("hy_fb1", (64,)), ("hy_fw2", (64, 64)), ("hy_fb2", (64,)), ("hy_fw3", (64, 1024)), ("hy_fb3", (1024,)),
    ("hy_bias", (2, 256)), ("w_out", (D, D)), ("b_out", (D,)), ("ln1_g", (D,)), ("ln1_b", (D,)),
    ("w_ff1", (D, DFF)), ("b_ff1", (DFF,)), ("w_ff2", (DFF, D)), ("b_ff2", (D,)), ("ln2_g", (D,)), ("ln2_b", (D,)),
]


def host_consts(cfg):
    out = {}
    out["idn"] = np.eye(128, dtype=np.float32)
    T, CTX, SEQ = cfg.T, cfg.CTX, cfg.SEQ
    t = np.arange(SEQ)
    row = (t // GRID_W).astype(np.float32)
    col = (t % GRID_W).astype(np.float32)

    def rope_tab(h):
        inv = (ROPE_BASE ** (-(np.arange(h // 2, dtype=np.float32) * (2.0 / h)))).astype(np.float32)
        tab = np.zeros((T, 2, 2, h // 2), np.float32)
        tab[:, 0] = 1.0
        for a, pos in enumerate((row, col)):
            ang = (pos[:, None] * inv[None, :]).astype(np.float32)
            tab[CTX:, 0, a] = np.cos(ang)
            tab[CTX:, 1, a] = np.sin(ang)
        return tab.reshape(T, 2 * 2 * (h // 2))

    out["rope_m"] = rope_tab(16)
    out["rope_d"] = rope_tab(24)
    rates = np.linspace(-math.log(1e-2) / 1.5, -math.log(1e-2) / 0.3, 256, dtype=np.float32)
    out["negrates"] = (-rates).astype(np.float32)
    for l in sorted({cfg.SEQ, cfg.CTX}):
        tt = np.arange(l, dtype=np.float32)
        bands = np.arange(1, 17, dtype=np.float32)
        ang = (np.float32(2.0 * math.pi / l) * tt[:, None] * bands[None, :]).astype(np.float32)
        emb = np.concatenate([(tt / np.float32(l))[:, None], np.cos(ang), np.sin(ang)], -1).astype(np.float32)
        out["embT_%d" % l] = np.ascontiguousarray(emb.T)
        out["tau_%d" % l] = np.ascontiguousarray((tt / np.float32(l)).reshape(l // 128, 128).T)
        f = np.arange(l, dtype=np.float64)
        w = 2.0 * np.pi / (4.0 * l)
        ang = w * (2 * f[None, :] + 1) * f[:, None]
        c = np.cos(ang)
        sn = np.sin(ang)
        out["ctf_%d" % l] = c.astype(ml_dtypes.bfloat16)
        out["stf_%d" % l] = sn.astype(ml_dtypes.bfloat16)
        out["cft_%d" % l] = np.ascontiguousarray(c.T).astype(ml_dtypes.bfloat16)
        out["sft_%d" % l] = np.ascontiguousarray(sn.T).astype(ml_dtypes.bfloat16)
    return out


class Builder:
    def __init__(self, cfg):
        self.cfg = cfg
        self.nc = bass.Bass("TRN2", target_bir_lowering=False)
        self.dbg = []

    def din(self, name, shape, dt=F32):
        return self.nc.dram_tensor(name, list(shape), dt, kind="ExternalInput").ap()

    def scr(self, name, shape, dt=F32, dbg=False):
        if dbg and self.cfg.debug:
            self.dbg.append(name)
            return self.nc.dram_tensor(name, list(shape), dt, kind="ExternalOutput").ap()
        return self.nc.dram_tensor(name, list(shape), dt).ap()

    def _uniq(self, name):
        self._uid = getattr(self, "_uid", 0) + 1
        return "%s_u%d" % (name, self._uid)

    def sb(self, es, name, shape, dt=F32):
        return es.enter_context(self.nc.sbuf_tensor(self._uniq(name), list(shape), dt))

    def ps(self, es, name, shape, dt=F32):
        return es.enter_context(self.nc.psum_tensor(self._uniq(name), list(shape), dt))

    def dma(self, q, out, in_, reads=(), writes=(), **kw):
        self.P.op(q, lambda e: e.dma_start(out=out, in_=in_, **kw), reads, writes, dma=True)

    def build(self):
        cfg, nc = self.cfg, self.nc
        NB, SEQ, CTX, T, L = cfg.NB, cfg.SEQ, cfg.CTX, cfg.T, cfg.DEPTH
        I = self.I = {}
        I["x"] = self.din("x", (NB, SEQ, D))
        I["ctx"] = self.din("ctx", (NB, CTX, D))
        I["c"] = self.din("c", (NB, D))
        I["c_ctx"] = self.din("c_ctx", (D,))
        for n, s in WSPEC:
            I[n] = self.din(n, (L,) + s)
        hc = host_consts(cfg)
        for n, v in hc.items():
            I[n] = self.din(n, v.shape, BF16 if v.dtype == ml_dtypes.bfloat16 else F32)
        self.out = nc.dram_tensor("out", [NB, SEQ, D], F32, kind="ExternalOutput").ap()
        S = self.S = {}
        S["xs"] = self.scr("xs", (NB, T, D), dbg=True)
        S["mod"] = self.scr("mod", (3, 6 * D), dbg=True)
        S["px"] = self.scr("px", (NB, T, DIN), dbg=True)
        S["qtm"] = self.scr("qtm", (NB, 6, 96, T), BF16, dbg=True)
        S["ktm"] = self.scr("ktm", (NB, 6, 96, T), BF16, dbg=True)
        S["vm"] = self.scr("vm", (NB, T, 6 * 65), BF16, dbg=True)
        S["qtd"] = self.scr("qtd", (NB, 8, 48, T), BF16, dbg=True)
        S["ktd"] = self.scr("ktd", (NB, 8, 48, T), BF16, dbg=True)
        S["vd"] = self.scr("vd", (NB, T, 4 * 97), BF16, dbg=True)
        S["ox"] = self.scr("ox", (NB, T, D), BF16, dbg=True)
        S["dbg_qb"] = self.scr("dbg_qb", (NB, T, 576), BF16, dbg=True)
        S["hx"] = self.scr("hx", (NB, T, 512), F32, dbg=True)
        S["ht"] = self.scr("ht", (NB, 32, 128, T), BF16)
        for l in sorted({SEQ, CTX}):
            for nm in ("ctf", "stf", "cft", "sft"):
                S["%s_%d" % (nm, l)] = I["%s_%d" % (nm, l)]
            S["kf_%d" % l] = self.scr("kf_%d" % l, (l // 128, 128, 2, 2, 256), F32, dbg=True)
        with ExitStack() as es:
            self.P = Prog(nc, es, same_engine_sync=SAME_ENGINE_SYNC)
            self.g = ExitStack()
            es.enter_context(self.g)
            self.idb = self.sb(self.g, "idb", [128, 128], BF16)
            self.cS = self.sb(self.g, "cS", [128, 8, 3], F32)
            self.phase_init()
            stop = cfg.stop_after
            for layer in range(L):
                last = layer == L - 1
                self.P.hoist_sp = HOIST_SP and ("mod" in HOIST_SET)
                self.phase_mod(layer)
                if stop == "mod":
                    break
                self.P.hoist_sp = HOIST_SP and ("A" in HOIST_SET)
                self.phase_A(layer)
                if stop == "A":
                    break
                self.P.hoist_sp = HOIST_SP and ("B" in HOIST_SET)
                self.phase_B(layer)
                if stop == "B":
                    break
                self.P.hoist_sp = HOIST_SP and ("C" in HOIST_SET)
                self.phase_C(layer, "mla", last)
                self.phase_C(layer, "diff", last)
                if stop == "C":
                    break
                self.P.hoist_sp = HOIST_SP and ("H" in HOIST_SET)
                self.phase_H(layer, last)
                if stop == "H":
                    break
                self.P.hoist_sp = HOIST_SP and ("DE1" in HOIST_SET)
                self.phase_DE1(layer, last)
                if stop == "DE1":
                    break
                self.P.hoist_sp = HOIST_SP and ("E2" in HOIST_SET)
                self.phase_E2(layer, last)
                if stop == "E2":
                    break
        return nc

    def phase_init(self):
        cfg, P, I = self.cfg, self.P, self.I
        with ExitStack() as es:
            idf = self.sb(es, "idf", [128, 128])
            ct = self.sb(es, "ct", [128, 8, 3])
            self.dma("sp", idf[:], I["idn"], writes=["idf"])
            P.op("dve", lambda e: e.tensor_copy(out=self.idb[:], in_=idf[:]), ["idf"], ["idb"])
            for b in range(cfg.NB):
                self.dma("sp", ct[:, :, b], I["c"][b].rearrange("(k p) -> p k", p=128), writes=["ct"],
                         allow_slow_non_contiguous=True)
            self.dma("sp", ct[:, :, 2], I["c_ctx"].rearrange("(k p) -> p k", p=128), writes=["ct"],
                     allow_slow_non_contiguous=True)
            P.op("act", lambda e: e.activation(out=self.cS[:], in_=ct[:], func=AF.Silu), ["ct"], ["cS"])
            P.flush()

    def gen_tables(self, l):
        P, I, S = self.P, self.I, self.S
        na = l // 128
        for (an, bn, cn, sn) in (("fa_%d", "fb_%d", "ctf_%d", "stf_%d"), ("ia_%d", "ib_%d", "cft_%d", "sft_%d")):
            At, Bt = I[an % l], I[bn % l]
            Ct, St = S[cn % l], S[sn % l]
            with ExitStack() as es:
                cB = self.sb(es, "cB", [128, l])
                sB = self.sb(es, "sB", [128, l])
                self.dma("sp", cB[:], Bt[:, 0, :], writes=["cB"])
                self.dma("sp", sB[:], Bt[:, 1, :], writes=["sB"])
                bufs = []
                for s in range(2):
                    bufs.append(dict(
                        cA=self.sb(es, "cA%d" % s, [128, l]), sA=self.sb(es, "sA%d" % s, [128, l]),
                        t1=self.sb(es, "t1%d" % s, [128, l]), t2=self.sb(es, "t2%d" % s, [128, l]),
                        oc=self.sb(es, "oc%d" % s, [128, l], BF16), os=self.sb(es, "os%d" % s, [128, l], BF16)))
                for a in range(na):
                    s = a % 2
                    bf = bufs[s]
                    k = lambda n: "%s%d" % (n, s)
                    self.dma("sp", bf["cA"][:], At[a, 0, :].partition_broadcast(128), writes=[k("cA")])
                    self.dma("sp", bf["sA"][:], At[a, 1, :].partition_broadcast(128), writes=[k("sA")])
                    ve = "dve" if s == 0 else "pool"
                    P.op(ve, lambda e, bf=bf: e.tensor_tensor(out=bf["t1"][:], in0=cB[:], in1=bf["cA"][:], op=ALU.mult),
                         ["cB", k("cA")], [k("t1")])
                    P.op(ve, lambda e, bf=bf: e.tensor_tensor(out=bf["t2"][:], in0=sB[:], in1=bf["sA"][:], op=ALU.mult),
                         ["sB", k("sA")], [k("t2")])
                    P.op(ve, lambda e, bf=bf: e.tensor_tensor(out=bf["oc"][:], in0=bf["t1"][:], in1=bf["t2"][:], op=ALU.subtract),
                         [k("t1"), k("t2")], [k("oc")])
                    P.op(ve, lambda e, bf=bf: e.tensor_tensor(out=bf["t1"][:], in0=cB[:], in1=bf["sA"][:], op=ALU.mult),
                         ["cB", k("sA"), k("oc")], [k("t1")])
                    P.op(ve, lambda e, bf=bf: e.tensor_tensor(out=bf["t2"][:], in0=sB[:], in1=bf["cA"][:], op=ALU.mult),
                         ["sB", k("cA"), k("oc")], [k("t2")])
                    P.op(ve, lambda e, bf=bf: e.tensor_tensor(out=bf["os"][:], in0=bf["t1"][:], in1=bf["t2"][:], op=ALU.add),
                         [k("t1"), k("t2")], [k("os")])
                    self.dma("sp", Ct[a * 128:(a + 1) * 128, :], bf["oc"][:], reads=[k("oc")])
                    self.dma("sp", St[a * 128:(a + 1) * 128, :], bf["os"][:], reads=[k("os")])
                P.flush()

    def phase_mod(self, layer):
        P, I, S = self.P, self.I, self.S
        with ExitStack() as es:
            wm = [self.sb(es, "wm%d" % s, [128, 8, 512]) for s in range(2)]
            bm = self.sb(es, "bm", [3, 6 * D])
            mr = self.sb(es, "mr", [3, 6 * D])
            pm = [self.ps(es, "pm%d" % s, [128, 512]) for s in range(2)]
            self.dma("sp", bm[:], I["b_mod"][layer].partition_broadcast(3), writes=["bm"])
            wv = I["w_mod"][layer].rearrange("(k p) n -> p k n", p=128)
            for n in range(12):
                s = n % 2
                self.dma("sp", wm[s][:], wv[:, :, n * 512:(n + 1) * 512], writes=["wm%d" % s])
                for k in range(8):
                    P.op("pe", lambda e, s=s, k=k: e.matmul(pm[s][0:3, :], lhsT=self.cS[:, k, :], rhs=wm[s][:, k, :],
                                                           start=(k == 0), stop=(k == 7)),
                         ["cS", "wm%d" % s], ["pm%d" % s])
                P.op("dve", lambda e, s=s, n=n: e.tensor_tensor(out=mr[:, n * 512:(n + 1) * 512], in0=pm[s][0:3, :],
                                                                in1=bm[:, n * 512:(n + 1) * 512], op=ALU.add),
                     ["pm%d" % s, "bm"], ["mr"])
            self.dma("sp", S["mod"], mr[:], reads=["mr"])
            P.flush()

    def load_mod_fm(self, es, name, slot_shift, slot_scale):
        P, S = self.P, self.S
        t = self.sb(es, name, [128, 3, 2, 8])
        for j in range(3):
            for q, sl in enumerate((slot_shift, slot_scale)):
                self.dma("sp", t[:, j, q, :], S["mod"][j, sl * D:(sl + 1) * D].rearrange("(k p) -> p k", p=128),
                         writes=[name], allow_slow_non_contiguous=True)
        P.op("dve", lambda e: e.tensor_scalar_add(out=t[:, :, 1, :], in0=t[:, :, 1, :], scalar1=1.0), [name], [name])
        return t

    def ln_mod_T(self, xt, xkey, W, slot, modt, j, outT, outkey, col0):
        self.ln_chain(xt, xkey, W, slot)
        self.ln_T(W, slot, slot % len(W["pT"]), modt, j, outT, outkey, col0)

    def ln_chain(self, xt, xkey, W, slot):
        P = self.P
        st, mv, rs, xn = W["st"][slot], W["mv"][slot], W["rs"][slot], W["xn"][slot]
        sk = "_%s%d" % (W["tag"], slot)
        for h in range(2):
            P.op("dve", lambda e: e.bn_stats(out=st[:, h, :], in_=xt[:, h * 512:(h + 1) * 512]), [xkey], ["st" + sk + str(h)])
        P.op("dve", lambda e: e.bn_aggr(out=mv[:], in_=st[:].rearrange("p a b -> p (a b)")),
             ["st" + sk + "0", "st" + sk + "1"], ["mv" + sk])
        P.op("act", lambda e: e.activation(out=rs[:], in_=mv[:, 1:2], func=AF.Sqrt, bias=LN_EPS), ["mv" + sk], ["rs" + sk])
        P.op("dve", lambda e: e.reciprocal(out=rs[:], in_=rs[:]), ["rs" + sk], ["rs" + sk])
        P.op("dve", lambda e: e.tensor_scalar(out=xn[:], in0=xt, scalar1=mv[:, 0:1], scalar2=rs[:], op0=ALU.subtract, op1=ALU.mult),
             [xkey, "mv" + sk, "rs" + sk], ["xn" + sk])

    def ln_T(self, W, slot, pslot, modt, j, outT, outkey, col0):
        P = self.P
        xn, pT = W["xn"][slot], W["pT"][pslot]
        sk = "_%s%d" % (W["tag"], slot)
        pk = "pT_%s%d" % (W["tag"], pslot)
        for k in range(8):
            P.op("pe", lambda e: e.transpose(out=pT[:, k, :], in_=xn[:, k * 128:(k + 1) * 128], identity=self.idb[:]),
                 ["xn" + sk, "idb"], [pk])
        for k in range(8):
            P.op("act", lambda e: e.activation(out=outT[:, k, col0:col0 + 128], in_=pT[:, k, :], func=AF.Identity,
                                               scale=modt[:, j, 1, k:k + 1], bias=modt[:, j, 0, k:k + 1]),
                 [pk, W["modkey"]], [outkey])

    def ln_ws(self, es, tag, modkey, nslot=2, npslot=2):
        W = dict(tag=tag, modkey=modkey)
        W["st"] = [self.sb(es, "st%s%d" % (tag, s), [128, 2, 6]) for s in range(nslot)]
        W["mv"] = [self.sb(es, "mv%s%d" % (tag, s), [128, 2]) for s in range(nslot)]
        W["rs"] = [self.sb(es, "rs%s%d" % (tag, s), [128, 1]) for s in range(nslot)]
        W["xn"] = [self.sb(es, "xn%s%d" % (tag, s), [128, 1024], BF16) for s in range(nslot)]
        W["pT"] = [self.ps(es, "pT%s%d" % (tag, s), [128, 8, 128], BF16) for s in range(npslot)]
        return W

    def x_src(self, layer, b, i):
        cfg = self.cfg
        if layer == 0:
            nct = cfg.CTX // 128
            if i < nct:
                return self.I["ctx"][b, i * 128:(i + 1) * 128, :]
            return self.I["x"][b, (i - nct) * 128:(i - nct + 1) * 128, :]
        return self.S["xs"][b, i * 128:(i + 1) * 128, :]

    def phase_A(self, layer):
        cfg, P, I, S = self.cfg, self.P, self.I, self.S
        with ExitStack() as es:
            win = self.sb(es, "win", [128, 8, DIN], BF16)
            self.dma("pool", win[:], I["w_in"][layer].rearrange("(k p) n -> p k n", p=128), writes=["win"])
            modt = self.load_mod_fm(es, "modA", 0, 1)
            W = self.ln_ws(es, "A", "modA")
            xt = [self.sb(es, "xtA%d" % s, [128, 1024]) for s in range(2)]
            xmT = [self.sb(es, "xmTA%d" % s, [128, 8, 128], BF16) for s in range(2)]
            pxs = [self.sb(es, "pxsA%d" % s, [128, DIN]) for s in range(2)]
            ppx = [self.ps(es, "ppxA%d" % s, [128, 512]) for s in range(3)]
            chunks = [(0, 512), (512, 512), (1024, 512), (1536, 512), (2048, DIN - 2048)]
            nct = cfg.CTX // 128
            seq = [(b, i) for b in range(cfg.NB) for i in range(cfg.NT)]
            cc = [0]

            def front(n):
                b, i = seq[n]
                s = n % 2
                j = 2 if i < nct else b
                self.dma("sp", xt[s][:], self.x_src(layer, b, i), writes=["xtA%d" % s])
                self.ln_mod_T(xt[s][:], "xtA%d" % s, W, s, modt, j, xmT[s], "xmTA%d" % s, 0)

            def back(n):
                b, i = seq[n]
                s = n % 2
                for (c0, cw) in chunks:
                    pb = cc[0] % 3
                    cc[0] += 1
                    for k in range(8):
                        P.op("pe", lambda e: e.matmul(ppx[pb][:, 0:cw], lhsT=xmT[s][:, k, :], rhs=win[:, k, c0:c0 + cw], start=(k == 0),
                                                      stop=(k == 7)), ["xmTA%d" % s, "win"], ["ppxA%d" % pb])
                    if cc[0] % 2 == 0:
                        P.op("dve", lambda e: e.tensor_copy(out=pxs[s][:, c0:c0 + cw], in_=ppx[pb][:, 0:cw]), ["ppxA%d" % pb], ["pxsA%d" % s])
                    else:
                        P.op("act", lambda e: e.copy(out=pxs[s][:, c0:c0 + cw], in_=ppx[pb][:, 0:cw]), ["ppxA%d" % pb], ["pxsA%d" % s])
                self.dma("sp", S["px"][b, i * 128:(i + 1) * 128, :], pxs[s][:], reads=["pxsA%d" % s])

            front(0)
            for n in range(len(seq)):
                if n + 1 < len(seq):
                    front(n + 1)
                back(n)
            P.flush()

    def rope(self, eng, src, dst, cosv, sinv, ta, tb, rkeys, wkey, tkey):
        P = self.P
        s0, s1 = src[:, :, :, 0, :], src[:, :, :, 1, :]
        d0, d1 = dst[:, :, :, 0, :], dst[:, :, :, 1, :]
        P.op(eng, lambda e: e.tensor_tensor(out=ta, in0=s0, in1=cosv, op=ALU.mult), rkeys, [tkey + "a"])
        P.op(eng, lambda e: e.tensor_tensor(out=tb, in0=s1, in1=sinv, op=ALU.mult), rkeys, [tkey + "b"])
        P.op(eng, lambda e: e.tensor_tensor(out=d0, in0=ta, in1=tb, op=ALU.subtract), [tkey + "a", tkey + "b"], [wkey])
        P.op(eng, lambda e: e.tensor_tensor(out=ta, in0=s1, in1=cosv, op=ALU.mult), rkeys + [wkey], [tkey + "a"])
        P.op(eng, lambda e: e.tensor_tensor(out=tb, in0=s0, in1=sinv, op=ALU.mult), rkeys + [wkey], [tkey + "b"])
        P.op(eng, lambda e: e.tensor_tensor(out=d1, in0=ta, in1=tb, op=ALU.add), [tkey + "a", tkey + "b"], [wkey])

    def phase_B(self, layer):
        cfg, P, I, S = self.cfg, self.P, self.I, self.S
        with ExitStack() as es:
            wqf = self.sb(es, "wqf", [128, 2, 576])
            wq = self.sb(es, "wq", [128, 2, 576], BF16)
            gq = self.sb(es, "gq", [128, 2])
            wkvf = self.sb(es, "wkvf", [128, 768])
            wkv = self.sb(es, "wkv", [128, 768], BF16)
            gkv = self.sb(es, "gkv", [128, 1])
            self.dma("sp", wqf[:], I["w_q_up"][layer].rearrange("(k p) n -> p k n", p=128), writes=["wqf"])
            self.dma("sp", gq[:], I["mla_q_norm"][layer].rearrange("(k p) -> p k", p=128), writes=["gq"],
                     allow_slow_non_contiguous=True)
            self.dma("sp", wkvf[:], I["w_kv_up"][layer], writes=["wkvf"])
            self.dma("sp", gkv[:], I["mla_kv_norm"][layer].rearrange("(p o) -> p o", o=1), writes=["gkv"],
                     allow_slow_non_contiguous=True)
            for k in range(2):
                P.op("dve", lambda e, k=k: e.tensor_scalar(out=wq[:, k, :], in0=wqf[:, k, :], scalar1=gq[:, k:k + 1], scalar2=None,
                                                          op0=ALU.mult), ["wqf", "gq"], ["wq"])
            P.op("dve", lambda e: e.tensor_scalar(out=wkv[:], in0=wkvf[:], scalar1=gkv[:, 0:1], scalar2=None, op0=ALU.mult),
                 ["wkvf", "gkv"], ["wkv"])
            NS = 2
            pxt = [self.sb(es, "pxtB%d" % s, [128, 1568]) for s in range(NS)]
            rm = [self.sb(es, "rmB%d" % s, [128, 32]) for s in range(NS)]
            rd = [self.sb(es, "rdB%d" % s, [128, 48]) for s in range(NS)]
            junk = self.sb(es, "junkB", [128, 256])
            ssq = [self.sb(es, "ssqB%d" % s, [128, 2]) for s in range(NS)]
            rr = [self.sb(es, "rrB%d" % s, [128, 2]) for s in range(NS)]
            cn = [self.sb(es, "cnB%d" % s, [128, 384], BF16) for s in range(NS)]
            cT = [self.sb(es, "cTB%d" % s, [128, 3, 128], BF16) for s in range(NS)]
            qf = [self.sb(es, "qfB%d" % s, [128, 576]) for s in range(NS)]
            kvf = [self.sb(es, "kvfB%d" % s, [128, 768]) for s in range(NS)]
            kr = [self.sb(es, "krB%d" % s, [128, 32]) for s in range(NS)]
            qb = [self.sb(es, "qbB%d" % s, [128, 6, 96], BF16) for s in range(NS)]
            kb = [self.sb(es, "kbB%d" % s, [128, 6, 96], BF16) for s in range(NS)]
            vb = [self.sb(es, "vbB%d" % s, [128, 6, 65], BF16) for s in range(NS)]
            qdb = [self.sb(es, "qdbB%d" % s, [128, 8, 48], BF16) for s in range(NS)]
            kdb = [self.sb(es, "kdbB%d" % s, [128, 8, 48], BF16) for s in range(NS)]
            vdb = [self.sb(es, "vdbB%d" % s, [128, 4, 97], BF16) for s in range(NS)]
            tas = [self.sb(es, "taB%d" % q, [128, 8, 2, 12]) for q in range(4)]
            tbs = [self.sb(es, "tbB%d" % q, [128, 8, 2, 12]) for q in range(4)]
            qTs = [self.sb(es, "qTsB%d" % s, [96, 6, 256], BF16) for s in range(2)]
            kTs = [self.sb(es, "kTsB%d" % s, [96, 6, 256], BF16) for s in range(2)]
            qdTs = [self.sb(es, "qdTsB%d" % s, [48, 8, 256], BF16) for s in range(2)]
            kdTs = [self.sb(es, "kdTsB%d" % s, [48, 8, 256], BF16) for s in range(2)]
            ptr = [self.ps(es, "ptrB%d" % s, [128, 8, 128], BF16) for s in range(3)]
            pq0 = self.ps(es, "pq0B", [128, 512])
            pq1 = self.ps(es, "pq1B", [128, 512])
            pkv0 = self.ps(es, "pkv0B", [128, 512])
            pkv1 = self.ps(es, "pkv1B", [128, 512])
            for s in range(NS):
                P.op("pool", lambda e, s=s: e.memset(vb[s][:, :, 64:65], 1.0), [], ["vbB%d" % s])
                P.op("pool", lambda e, s=s: e.memset(vdb[s][:, :, 96:97], 1.0), [], ["vdbB%d" % s])
            trc = [0]

            def next_tr():
                t = trc[0] % 3
                trc[0] += 1
                return t

            it = 0
            for b in range(cfg.NB):
                for i in range(cfg.NT):
                    s = it % NS
                    it += 1
                    g, gs, off = i // 2, (i // 2) % 2, (i % 2) * 128
                    r0 = i * 128
                    K = lambda n: "%sB%d" % (n, s)
                    self.dma("sp", pxt[s][:], S["px"][b, r0:r0 + 128, 0:1568], writes=[K("pxt")])
                    self.dma("sp", rm[s][:], I["rope_m"][r0:r0 + 128, :], writes=[K("rm")])
                    self.dma("sp", rd[s][:], I["rope_d"][r0:r0 + 128, :], writes=[K("rd")])
                    P.op("act", lambda e, s=s: e.activation(out=junk[:, 0:256], in_=pxt[s][:, 0:256], func=AF.Square,
                                                            accum_out=ssq[s][:, 0:1]), [K("pxt")], ["junkB", K("ssq")])
                    P.op("act", lambda e, s=s: e.activation(out=junk[:, 0:128], in_=pxt[s][:, 256:384], func=AF.Square,
                                                            accum_out=ssq[s][:, 1:2]), [K("pxt")], ["junkB", K("ssq")])
                    P.op("act", lambda e, s=s: e.activation(out=rr[s][:, 0:1], in_=ssq[s][:, 0:1], func=AF.Sqrt, scale=1.0 / 256,
                                                            bias=LN_EPS), [K("ssq")], [K("rr")])
                    P.op("act", lambda e, s=s: e.activation(out=rr[s][:, 1:2], in_=ssq[s][:, 1:2], func=AF.Sqrt, scale=1.0 / 128,
                                                            bias=LN_EPS), [K("ssq")], [K("rr")])
                    P.op("dve", lambda e, s=s: e.reciprocal(out=rr[s][:], in_=rr[s][:]), [K("rr")], [K("rr")])
                    P.op("dve", lambda e, s=s: e.tensor_scalar(out=cn[s][:, 0:256], in0=pxt[s][:, 0:256], scalar1=rr[s][:, 0:1],
                                                               scalar2=None, op0=ALU.mult), [K("pxt"), K("rr")], [K("cn")])
                    P.op("dve", lambda e, s=s: e.tensor_scalar(out=cn[s][:, 256:384], in0=pxt[s][:, 256:384], scalar1=rr[s][:, 1:2],
                                                               scalar2=None, op0=ALU.mult), [K("pxt"), K("rr")], [K("cn")])
                    t = next_tr()
                    for k in range(3):
                        P.op("pe", lambda e, s=s, k=k, t=t: e.transpose(out=ptr[t][:, k, :], in_=cn[s][:, k * 128:(k + 1) * 128],
                                                                        identity=self.idb[:]), [K("cn"), "idb"], ["ptrB%d" % t])
                    P.op("act", lambda e, s=s, t=t: e.copy(out=cT[s][:], in_=ptr[t][:, 0:3, :]), ["ptrB%d" % t], [K("cT")])
                    for k in range(2):
                        P.op("pe", lambda e, s=s, k=k: e.matmul(pq0[:], lhsT=cT[s][:, k, :], rhs=wq[:, k, 0:512], start=(k == 0),
                                                                stop=(k == 1)), [K("cT"), "wq"], ["pq0B"])
                    for k in range(2):
                        P.op("pe", lambda e, s=s, k=k: e.matmul(pq1[:, 0:64], lhsT=cT[s][:, k, :], rhs=wq[:, k, 512:576], start=(k == 0),
                                                                stop=(k == 1)), [K("cT"), "wq"], ["pq1B"])
                    P.op("pe", lambda e, s=s: e.matmul(pkv0[:], lhsT=cT[s][:, 2, :], rhs=wkv[:, 0:512], start=True, stop=True),
                         [K("cT"), "wkv"], ["pkv0B"])
                    P.op("pe", lambda e, s=s: e.matmul(pkv1[:, 0:256], lhsT=cT[s][:, 2, :], rhs=wkv[:, 512:768], start=True, stop=True),
                         [K("cT"), "wkv"], ["pkv1B"])
                    P.op("act", lambda e, s=s: e.copy(out=qf[s][:, 0:512], in_=pq0[:]), ["pq0B"], [K("qf")])
                    P.op("dve", lambda e, s=s: e.tensor_copy(out=qf[s][:, 512:576], in_=pq1[:, 0:64]), ["pq1B"], [K("qf")])
                    P.op("dve", lambda e, s=s: e.tensor_copy(out=kvf[s][:, 0:512], in_=pkv0[:]), ["pkv0B"], [K("kvf")])
                    P.op("act", lambda e, s=s: e.copy(out=kvf[s][:, 512:768], in_=pkv1[:, 0:256]), ["pkv1B"], [K("kvf")])
                    qv = qf[s][:].rearrange("p (h d) -> p h d", h=6)
                    kvv = kvf[s][:].rearrange("p (h d) -> p h d", h=6)
                    cosm = rm[s][:, 0:16].rearrange("p (a m) -> p a m", a=2).unsqueeze(1)
                    sinm = rm[s][:, 16:32].rearrange("p (a m) -> p a m", a=2).unsqueeze(1)
                    P.op("pool", lambda e, s=s, qv=qv: e.tensor_copy(out=qb[s][:, :, 0:64], in_=qv[:, :, 0:64]), [K("qf")], [K("qb")])
                    self.rope("dve", qv[:, :, 64:96].rearrange("p h (a s m) -> p h a s m", a=2, s=2),
                              qb[s][:, :, 64:96].rearrange("p h (a s m) -> p h a s m", a=2, s=2),
                              cosm.to_broadcast([128, 6, 2, 8]), sinm.to_broadcast([128, 6, 2, 8]),
                              tas[0][:, 0:6, :, 0:8], tbs[0][:, 0:6, :, 0:8], [K("qf"), K("rm")], K("qb"), "ropeB0")
                    self.rope("dve", pxt[s][:, 384:416].rearrange("p (h a s m) -> p h a s m", h=1, a=2, s=2),
                              kr[s][:].rearrange("p (h a s m) -> p h a s m", h=1, a=2, s=2),
                              cosm.to_broadcast([128, 1, 2, 8]), sinm.to_broadcast([128, 1, 2, 8]),
                              tas[1][:, 0:1, :, 0:8], tbs[1][:, 0:1, :, 0:8], [K("pxt"), K("rm")], K("kr"), "ropeB1")
                    P.op("pool", lambda e, s=s, kvv=kvv: e.tensor_copy(out=kb[s][:, :, 0:64], in_=kvv[:, :, 0:64]), [K("kvf")], [K("kb")])
                    P.op("dve", lambda e, s=s: e.tensor_copy(out=kb[s][:, :, 64:96], in_=kr[s][:].unsqueeze(1).to_broadcast([128, 6, 32])),
                         [K("kr")], [K("kb")])
                    P.op("pool", lambda e, s=s, kvv=kvv: e.tensor_copy(out=vb[s][:, :, 0:64], in_=kvv[:, :, 64:128]), [K("kvf")], [K("vb")])
                    self.dma("sp", S["vm"][b, r0:r0 + 128, :], vb[s][:].rearrange("p h d -> p (h d)"), reads=[K("vb")])
                    cosd = rd[s][:, 0:24].rearrange("p (a m) -> p a m", a=2).unsqueeze(1).to_broadcast([128, 8, 2, 12])
                    sind = rd[s][:, 24:48].rearrange("p (a m) -> p a m", a=2).unsqueeze(1).to_broadcast([128, 8, 2, 12])
                    self.rope("pool", pxt[s][:, 416:800].rearrange("p (g a s m) -> p g a s m", g=8, a=2, s=2),
                              qdb[s][:].rearrange("p g (a s m) -> p g a s m", a=2, s=2), cosd, sind, tas[2][:], tbs[2][:],
                              [K("pxt"), K("rd")], K("qdb"), "ropeB2")
                    self.rope("dve", pxt[s][:, 800:1184].rearrange("p (g a s m) -> p g a s m", g=8, a=2, s=2),
                              kdb[s][:].rearrange("p g (a s m) -> p g a s m", a=2, s=2), cosd, sind, tas[3][:], tbs[3][:],
                              [K("pxt"), K("rd")], K("kdb"), "ropeB3")
                    P.op("pool", lambda e, s=s: e.tensor_copy(out=vdb[s][:, :, 0:96],
                                                              in_=pxt[s][:, 1184:1568].rearrange("p (h d) -> p h d", h=4)),
                         [K("pxt")], [K("vdb")])
                    self.dma("sp", S["vd"][b, r0:r0 + 128, :], vdb[s][:].rearrange("p h d -> p (h d)"), reads=[K("vdb")])
                    if cfg.debug:
                        self.dma("sp", S["dbg_qb"][b, r0:r0 + 128, :], qb[s][:].rearrange("p h d -> p (h d)"), reads=[K("qb")])
                    for (srcb, skey, nh, dd, stg, stkey) in ((qb, "qb", 6, 96, qTs, "qTs"), (kb, "kb", 6, 96, kTs, "kTs"),
                                                             (qdb, "qdb", 8, 48, qdTs, "qdTs"), (kdb, "kdb", 8, 48, kdTs, "kdTs")):
                        t = next_tr()
                        for h in range(nh):
                            P.op("pe", lambda e, s=s, h=h, t=t, srcb=srcb, dd=dd: e.transpose(
                                out=ptr[t][0:dd, h, :], in_=srcb[s][:, h, :], identity=self.idb[:]),
                                 [K(skey), "idb"], ["ptrB%d" % t])
                        eng = "act" if skey in ("qb", "qdb") else "dve"
                        if eng == "act":
                            P.op("act", lambda e, t=t, nh=nh, dd=dd, stg=stg: e.copy(out=stg[gs][:, :, off:off + 128], in_=ptr[t][0:dd, 0:nh, :]),
                                 ["ptrB%d" % t], ["%sB%d" % (stkey, gs)])
                        else:
                            P.op("dve", lambda e, t=t, nh=nh, dd=dd, stg=stg: e.tensor_copy(out=stg[gs][:, :, off:off + 128],
                                                                                           in_=ptr[t][0:dd, 0:nh, :]),
                                 ["ptrB%d" % t], ["%sB%d" % (stkey, gs)])
                    if i % 2 == 1:
                        c0 = g * 256
                        for (stg, stkey, dst) in ((qTs, "qTs", "qtm"), (kTs, "kTs", "ktm"), (qdTs, "qdTs", "qtd"), (kdTs, "kdTs", "ktd")):
                            self.dma("sp", S[dst][b].rearrange("h d t -> d h t")[:, :, c0:c0 + 256], stg[gs][:],
                                     reads=["%sB%d" % (stkey, gs)])
            P.flush()

    def phase_C(self, layer, fam, last):
        cfg, P, I, S = self.cfg, self.P, self.I, self.S
        T, CTX, SEQ, NT = cfg.T, cfg.CTX, cfg.SEQ, cfg.NT
        if fam == "mla":
            d, dv, nu, qsrc, ksrc, vsrc, col0 = 96, 64, 6, "qtm", "ktm", "vm", 0
        else:
            d, dv, nu, qsrc, ksrc, vsrc, col0 = 48, 96, 8, "qtd", "ktd", "vd", 384
        nh = 6 if fam == "mla" else 4
        scale = float(d) ** -0.5
        lam_init = 0.8 - 0.6 * math.exp(-0.3 * layer)
        with ExitStack() as es:
            KP = 128
            kt = self.sb(es, "ktC", [KP, nu, T], BF16)
            vt = self.sb(es, "vtC", [128, NT, nh * (dv + 1)], BF16)
            qt = [self.sb(es, "qtC%d" % s, [KP, nu, 512], BF16) for s in range(2)]
            if d < 96:
                P.op("pool", lambda e: e.memset(kt[:], 0.0), [], ["ktC"])
                for s_ in range(2):
                    P.op("dve", lambda e: e.memset(qt[s_][:], 0.0), [], ["qtC%d" % s_])
            pt = [self.sb(es, "ptC%d" % s, [128, 512], BF16) for s in range(3)]
            oxs = [self.sb(es, "oxsC%d" % s, [128, 4, 384], BF16) for s in range(2)]
            rinv = [self.sb(es, "rinvC%d" % s, [128, 4]) for s in range(4)]
            pS = [self.ps(es, "pSC%d" % s, [128, 512]) for s in range(3)]
            pO = [self.ps(es, "pOC%d" % s, [128, 4, 128]) for s in range(2)]
            pOT = [self.ps(es, "pOTC%d" % s, [128, 512]) for s in range(2)]
            oT = [self.sb(es, "oTC%d" % s, [128, 512]) for s in range(2)]
            idf = self.sb(es, "idfC", [128, 128])
            self.dma("sp", idf[:], I["idn"], writes=["idfC"])
            if fam == "diff":
                dl = self.sb(es, "dlC", [128, 192])
                tmp = self.sb(es, "tmpC", [128, 48])
                ss = self.sb(es, "ssC", [128, 2])
                nl = self.sb(es, "nlC", [128, 1])
                sg = self.sb(es, "sgC", [128, 96])
                t1 = self.sb(es, "t1C", [128, 4, 96])
                t2 = self.sb(es, "t2C", [128, 4, 96])
                oo = self.sb(es, "ooC", [128, 4, 96])
                junk = self.sb(es, "junkC", [128, 4, 96])
                sq = self.sb(es, "sqC", [128, 4])
                self.dma("sp", dl[:], I["diff_lambda"][layer].rearrange("a b -> (a b)").partition_broadcast(128), writes=["dl"])
                self.dma("sp", sg[:], I["diff_subln"][layer].partition_broadcast(128), writes=["sg"])
                for q in range(2):
                    P.op("dve", lambda e, q=q: e.tensor_tensor(out=tmp[:], in0=dl[:, q * 96:q * 96 + 48], in1=dl[:, q * 96 + 48:q * 96 + 96],
                                                               op=ALU.mult), ["dl"], ["tmpC"])
                    P.op("dve", lambda e, q=q: e.reduce_sum(out=ss[:, q:q + 1], in_=tmp[:], axis=mybir.AxisListType.X), ["tmpC"], ["ssC"])
                P.op("act", lambda e: e.activation(out=ss[:], in_=ss[:], func=AF.Exp), ["ssC"], ["ssC"])
                P.op("dve", lambda e: e.tensor_tensor(out=nl[:], in0=ss[:, 1:2], in1=ss[:, 0:1], op=ALU.subtract), ["ssC"], ["nlC"])
                P.op("dve", lambda e: e.tensor_scalar_add(out=nl[:], in0=nl[:], scalar1=-lam_init), ["nlC"], ["nlC"])
                P.op("dve", lambda e: e.tensor_scalar(out=sg[:], in0=sg[:], scalar1=1.0 - lam_init, scalar2=None, op0=ALU.mult), ["sg"], ["sg"])
            tiles = []
            qci, uci = 0, 0
            for b in range(cfg.NB):
                qchunks = [] if last else [(0, CTX, CTX // 128)]
                qchunks += [(CTX + j * 512, 512, NT) for j in range(SEQ // 512)]
                for ci, (tok0, Wd, nkc) in enumerate(qchunks):
                    for u in range(nu):
                        for kc in range(nkc):
                            tiles.append(dict(b=b, tok0=tok0, Wd=Wd, nkc=nkc, u=u, kc=kc, qs=qci % 2, po=uci % 2,
                                              first_b=(ci == 0 and u == 0 and kc == 0), first_c=(u == 0 and kc == 0)))
                        uci += 1
                    qci += 1
            for idx, t in enumerate(tiles):
                t["si"] = idx % 3
                t["pi"] = idx % 3

            def emit_qk(t):
                b, qs, Wd, u, kc, si = t["b"], t["qs"], t["Wd"], t["u"], t["kc"], t["si"]
                if t["first_b"]:
                    self.dma("sp", kt[0:d], S[ksrc][b].rearrange("h d t -> d h t"), writes=["ktC"])
                if t["first_c"]:
                    self.dma("sp", qt[qs][0:d, :, 0:Wd], S[qsrc][b].rearrange("h d t -> d h t")[:, :, t["tok0"]:t["tok0"] + Wd],
                             writes=["qtC%d" % qs])
                kd = d if d >= 96 else KP
                P.op("pe", lambda e: e.matmul(pS[si][:, 0:Wd], lhsT=kt[0:kd, u, kc * 128:(kc + 1) * 128], rhs=qt[qs][0:kd, u, 0:Wd],
                                              start=True, stop=True), ["ktC", "qtC%d" % qs], ["pSC%d" % si])

            def emit_rest(t):
                b, qs, Wd, u, kc, si, pi, po, nkc = t["b"], t["qs"], t["Wd"], t["u"], t["kc"], t["si"], t["pi"], t["po"], t["nkc"]
                nqb = Wd // 128
                oxk = "oxsC%d" % qs
                hv = u if fam == "mla" else u // 2
                if t["first_b"]:
                    self.dma("sp", vt[:], S[vsrc][b].rearrange("(n p) c -> p n c", p=128), writes=["vtC"])
                P.op("act", lambda e: e.activation(out=pt[pi][:, 0:Wd], in_=pS[si][:, 0:Wd], func=AF.Exp, scale=scale),
                     ["pSC%d" % si], ["ptC%d" % pi])
                P.op("pe", lambda e: e.matmul(pOT[po][0:dv + 1, 0:Wd], lhsT=vt[:, kc, hv * (dv + 1):(hv + 1) * (dv + 1)], rhs=pt[pi][:, 0:Wd],
                                              start=(kc == 0), stop=(kc == nkc - 1)), ["ptC%d" % pi, "vtC"], ["pOTC%d" % po])
                if kc != nkc - 1:
                    return
                P.op("dve", lambda e: e.tensor_copy(out=oT[po][0:dv + 1, 0:Wd], in_=pOT[po][0:dv + 1, 0:Wd]), ["pOTC%d" % po], ["oTC%d" % po])
                for qb in range(nqb):
                    P.op("pe", lambda e: e.transpose(out=pO[po][:, qb, 0:dv + 1], in_=oT[po][0:dv + 1, qb * 128:(qb + 1) * 128],
                                                     identity=idf[0:dv + 1, 0:dv + 1]), ["oTC%d" % po, "idfC"], ["pOC%d" % po])
                rk = "rinvC%d" % po
                P.op("dve", lambda e: e.reciprocal(out=rinv[po][:, 0:nqb], in_=pO[po][:, 0:nqb, dv]), ["pOC%d" % po], [rk])
                rb = rinv[po][:, 0:nqb].unsqueeze(2).to_broadcast([128, nqb, dv])
                if fam == "mla":
                    P.op("dve", lambda e: e.tensor_tensor(out=oxs[qs][:, 0:nqb, u * 64:(u + 1) * 64], in0=pO[po][:, 0:nqb, 0:dv], in1=rb,
                                                          op=ALU.mult), ["pOC%d" % po, rk], [oxk])
                else:
                    h, m = u // 2, u % 2
                    tt = t1 if m == 0 else t2
                    tk = "t1C" if m == 0 else "t2C"
                    P.op("dve", lambda e: e.tensor_tensor(out=tt[:, 0:nqb, :], in0=pO[po][:, 0:nqb, 0:dv], in1=rb, op=ALU.mult),
                         ["pOC%d" % po, rk], [tk])
                    if m == 1:
                        P.op("dve", lambda e: e.scalar_tensor_tensor(out=oo[:, 0:nqb, :], in0=t2[:, 0:nqb, :], scalar=nl[:, 0:1],
                                                                     in1=t1[:, 0:nqb, :], op0=ALU.mult, op1=ALU.add),
                             ["t1C", "t2C", "nlC"], ["ooC"])
                        for qb in range(nqb):
                            P.op("pool", lambda e: e.tensor_tensor(out=junk[:, qb, :], in0=oo[:, qb, :], in1=oo[:, qb, :], op=ALU.mult),
                                 ["ooC"], ["junkC"])
                        P.op("dve", lambda e: e.reduce_sum(out=sq[:, 0:nqb], in_=junk[:, 0:nqb, :], axis=mybir.AxisListType.X), ["junkC"], ["sqC"])
                        P.defer = []
                        P.op("act", lambda e: e.activation(out=sq[:, 0:nqb], in_=sq[:, 0:nqb], func=AF.Sqrt, scale=1.0 / 96, bias=LN_EPS),
                             ["sqC"], ["sqC"])
                        P.op("dve", lambda e: e.reciprocal(out=sq[:, 0:nqb], in_=sq[:, 0:nqb]), ["sqC"], ["sqC"])
                        P.op("dve", lambda e: e.tensor_tensor(out=oo[:, 0:nqb, :], in0=oo[:, 0:nqb, :],
                                                              in1=sq[:, 0:nqb].unsqueeze(2).to_broadcast([128, nqb, 96]), op=ALU.mult),
                             ["ooC", "sqC"], ["ooC"])
                        P.op("dve", lambda e: e.tensor_tensor(out=oxs[qs][:, 0:nqb, h * 96:(h + 1) * 96], in0=oo[:, 0:nqb, :],
                                                              in1=sg[:].unsqueeze(1).to_broadcast([128, nqb, 96]), op=ALU.mult),
                             ["ooC", "sg"], [oxk])
                if u == nu - 1:
                    self.dma("sp", S["ox"][b, t["tok0"]:t["tok0"] + Wd, col0:col0 + 384].rearrange("(q p) c -> p q c", p=128),
                             oxs[qs][:, 0:nqb, :], reads=[oxk])
                if P.defer is not None:
                    pending.append([3, P.defer])
                    P.defer = None

            pending = []

            def drain(force=False):
                for pe_ in list(pending):
                    pe_[0] -= 1
                    if pe_[0] <= 0 or force:
                        for it_ in pe_[1]:
                            P.add(*it_)
                        pending.remove(pe_)

            LA = 2
            for idx in range(min(LA, len(tiles))):
                emit_qk(tiles[idx])
            for idx in range(len(tiles)):
                if idx + LA < len(tiles):
                    emit_qk(tiles[idx + LA])
                emit_rest(tiles[idx])
                drain()
            drain(force=True)
            P.flush()

    def phase_H(self, layer, last):
        cfg = self.cfg
        for (l, tok0) in ((cfg.SEQ, cfg.CTX), (cfg.CTX, 0)):
            if last and tok0 == 0:
                continue
            with ExitStack() as zes:
                Z = self.sb(zes, "Zh", [128, l // 128, 512], BF16)
                self.hy_conv(layer, l, tok0, Z)
                self.hy_filters(layer, l)
                for o in range(2):
                    with ExitStack() as yes:
                        Y = self.sb(yes, "Yh", [128, l // 128, 2, 512], BF16)
                        self.hy_forward(layer, l, o, Z, Y)
                        self.hy_inverse(layer, l, tok0, o, Z, Y)

    def hy_conv(self, layer, l, tok0, Z):
        cfg, P, I, S = self.cfg, self.P, self.I, self.S
        nA = l // 128
        with ExitStack() as es:
            wc = self.sb(es, "wcH", [128, 3, 768])
            cb = self.sb(es, "cbH", [128, 768])
            self.dma("sp", wc[:].rearrange("p a c -> p (a c)"),
                     I["hy_conv_w"][layer].rearrange("a c -> (a c)").partition_broadcast(128), writes=["wcH"])
            self.dma("sp", cb[:], I["hy_conv_b"][layer].partition_broadcast(128), writes=["cbH"])
            um = [self.sb(es, "umH%d" % s, [128, 768]) for s in range(2)]
            u0 = [self.sb(es, "u0H%d" % s, [128, 768]) for s in range(2)]
            up = [self.sb(es, "upH%d" % s, [128, 768]) for s in range(2)]
            ta = [self.sb(es, "taH%d" % s, [128, 768]) for s in range(2)]
            tb = [self.sb(es, "tbH%d" % s, [128, 768]) for s in range(2)]
            it = 0
            for b in range(cfg.NB):
                for a in range(nA):
                    s = it % 2
                    it += 1
                    K = lambda n: "%sH%d" % (n, s)
                    r0 = tok0 + a * 128
                    src = S["px"][b]
                    self.dma("sp", u0[s][:], src[r0:r0 + 128, 1568:2336], writes=[K("u0")])
                    if a == 0:
                        P.op("pool", lambda e: e.memset(um[s][:], 0.0), [], [K("um")])
                        self.dma("sp", um[s][1:128, :], src[r0:r0 + 127, 1568:2336], writes=[K("um")])
                    else:
                        self.dma("sp", um[s][:], src[r0 - 1:r0 + 127, 1568:2336], writes=[K("um")])
                    if a == nA - 1:
                        P.op("pool", lambda e: e.memset(up[s][:], 0.0), [], [K("up")])
                        self.dma("sp", up[s][0:127, :], src[r0 + 1:r0 + 128, 1568:2336], writes=[K("up")])
                    else:
                        self.dma("sp", up[s][:], src[r0 + 1:r0 + 129, 1568:2336], writes=[K("up")])
                    P.op("pool", lambda e: e.tensor_tensor(out=ta[s][:], in0=um[s][:], in1=wc[:, 0, :], op=ALU.mult), [K("um"), "wcH"], [K("ta")])
                    P.op("dve", lambda e: e.tensor_tensor(out=tb[s][:], in0=u0[s][:], in1=wc[:, 1, :], op=ALU.mult), [K("u0"), "wcH"], [K("tb")])
                    P.op("dve", lambda e: e.tensor_tensor(out=tb[s][:], in0=tb[s][:], in1=ta[s][:], op=ALU.add), [K("ta"), K("tb")], [K("tb")])
                    P.op("pool", lambda e: e.tensor_tensor(out=ta[s][:], in0=up[s][:], in1=wc[:, 2, :], op=ALU.mult), [K("up"), "wcH", K("tb")], [K("ta")])
                    P.op("dve", lambda e: e.tensor_tensor(out=tb[s][:], in0=tb[s][:], in1=ta[s][:], op=ALU.add), [K("ta"), K("tb")], [K("tb")])
                    P.op("dve", lambda e: e.tensor_tensor(out=tb[s][:], in0=tb[s][:], in1=cb[:], op=ALU.add), [K("tb"), "cbH"], [K("tb")])
                    P.op("act", lambda e: e.copy(out=Z[:, a, b * 256:(b + 1) * 256], in_=tb[s][:, 0:256]), [K("tb")], ["Z%d" % a])
                    self.dma("sp", S["hx"][b, r0:r0 + 128, :], tb[s][:, 256:768], reads=[K("tb")])
            P.flush()

    def hy_filters(self, layer, l):
        cfg, P, I, S = self.cfg, self.P, self.I, self.S
        nA = l // 128
        CW = min(512, l)
        ncol = l // CW
        TWO_PI = 2.0 * math.pi
        with ExitStack() as fes:
            fs = self.sb(fes, "fsH", [128, nA, 2, 512], BF16)
            rn = self.sb(fes, "rnH", [128, 2, 256])
            with ExitStack() as es:
                embT = self.sb(es, "embTH", [33, l])
                fw1 = self.sb(es, "fw1H", [33, 64])
                fb1 = self.sb(es, "fb1H", [64, 1])
                fw2 = self.sb(es, "fw2H", [64, 64])
                fb2 = self.sb(es, "fb2H", [64, 1])
                fw3 = self.sb(es, "fw3H", [65, 1024])
                h1T = self.sb(es, "h1TH", [64, l])
                h2T = self.sb(es, "h2TH", [65, l])
                negr = self.sb(es, "negrH", [128, 256])
                tau = self.sb(es, "tauH", [128, nA])
                pre = [self.sb(es, "preH%d" % s, [64, 512]) for s in range(2)]
                kk = [self.sb(es, "kkH%d" % s, [64, 512]) for s in range(2)]
                wt = [self.sb(es, "wtH%d" % s, [128, 256]) for s in range(2)]
                filt2 = [self.sb(es, "filtH%d" % s, [128, 1024], BF16) for s in range(2)]
                absf = [self.sb(es, "absfH%d" % s, [128, 1024], BF16) for s in range(2)]
                onesb = self.sb(es, "onesH", [128, 128], BF16)
                nsum = self.sb(es, "nsumH", [128, 2, 256])
                pm = [self.ps(es, "pmH%d" % s, [128, 512]) for s in range(2)]
                p3 = [self.ps(es, "p3H%d" % s, [128, 512]) for s in range(2)]
                pn = [self.ps(es, "pnH%d" % s, [128, 512]) for s in range(2)]
                self.dma("sp", embT[:], I["embT_%d" % l], writes=["embT"])
                self.dma("sp", fw1[:], I["hy_fw1"][layer], writes=["fw1"])
                self.dma("sp", fb1[:], I["hy_fb1"][layer].rearrange("(p o) -> p o", o=1), writes=["fb1"], allow_slow_non_contiguous=True)
                self.dma("sp", fw2[:], I["hy_fw2"][layer], writes=["fw2"])
                self.dma("sp", fb2[:], I["hy_fb2"][layer].rearrange("(p o) -> p o", o=1), writes=["fb2"], allow_slow_non_contiguous=True)
                self.dma("sp", fw3[0:64, :], I["hy_fw3"][layer], writes=["fw3"])
                self.dma("sp", fw3[64:65, :], I["hy_fb3"][layer].rearrange("(o n) -> o n", o=1), writes=["fw3"])
                self.dma("sp", negr[:], I["negrates"].partition_broadcast(128), writes=["negr"])
                self.dma("sp", tau[:], I["tau_%d" % l], writes=["tau"])
                P.op("pool", lambda e: e.memset(h2T[64:65, :], 1.0), [], ["h2ones"])
                P.op("pool", lambda e: e.memset(onesb[:], 1.0), [], ["onesb"])

                def sin_layer(w, bcol, src, srck, dst, dstk, kdim):
                    for cc in range(ncol):
                        s = cc % 2
                        c0 = cc * CW
                        P.op("pe", lambda e: e.matmul(pm[s][0:64, 0:CW], lhsT=w[0:kdim, :], rhs=src[0:kdim, c0:c0 + CW], start=True, stop=True),
                             [srck, "fw1", "fw2"], ["pmH%d" % s])
                        P.op("dve", lambda e: e.tensor_scalar(out=pre[s][:, 0:CW], in0=pm[s][0:64, 0:CW], scalar1=bcol[:, 0:1], scalar2=None,
                                                              op0=ALU.add), ["pmH%d" % s, "fb1", "fb2"], ["preH%d" % s])
                        P.op("dve", lambda e: e.tensor_scalar(out=kk[s][:, 0:CW], in0=pre[s][:, 0:CW], scalar1=1.0 / TWO_PI, scalar2=MAGIC,
                                                              op0=ALU.mult, op1=ALU.add), ["preH%d" % s], ["kkH%d" % s])
                        P.op("dve", lambda e: e.tensor_scalar(out=kk[s][:, 0:CW], in0=kk[s][:, 0:CW], scalar1=MAGIC, scalar2=None,
                                                              op0=ALU.subtract), ["kkH%d" % s], ["kkH%d" % s])
                        P.op("dve", lambda e: e.scalar_tensor_tensor(out=pre[s][:, 0:CW], in0=kk[s][:, 0:CW], scalar=-TWO_PI, in1=pre[s][:, 0:CW],
                                                                     op0=ALU.mult, op1=ALU.add), ["kkH%d" % s, "preH%d" % s], ["preH%d" % s])
                        P.op("act", lambda e: e.activation(out=dst[0:64, c0:c0 + CW], in_=pre[s][:, 0:CW], func=AF.Sin), ["preH%d" % s], [dstk])

                sin_layer(fw1, fb1, embT, "embT", h1T, "h1T", 33)
                sin_layer(fw2, fb2, h1T, "h1T", h2T, "h2T", 64)
                for a in range(nA):
                    s = a % 2
                    P.op("act", lambda e: e.activation(out=wt[s][:], in_=negr[:], func=AF.Exp, scale=tau[:, a:a + 1]), ["negr", "tau"], ["wtH%d" % s])
                    filt = filt2[s]
                    fk = "filtH%d" % s
                    for o in range(2):
                        P.op("pe", lambda e: e.matmul(p3[o][:], lhsT=h2T[0:65, a * 128:(a + 1) * 128], rhs=fw3[0:65, o * 512:(o + 1) * 512],
                                                      start=True, stop=True), ["h2T", "h2ones", "fw3"], ["p3H%d" % o])
                        P.op("dve", lambda e: e.tensor_tensor(out=filt[:, o * 512:(o + 1) * 512].rearrange("p (d c) -> p d c", d=2),
                                                              in0=p3[o][:].rearrange("p (d c) -> p d c", d=2),
                                                              in1=wt[s][:].unsqueeze(1).to_broadcast([128, 2, 256]), op=ALU.mult),
                             ["p3H%d" % o, "wtH%d" % s], [fk])
                    if a == 0:
                        for o in range(2):
                            P.op("dve", lambda e: e.memset(filt[0:1, o * 512 + 256:o * 512 + 512], 0.0), [fk], [fk])
                    fv = filt[:].rearrange("p (o d c) -> p o d c", o=2, d=2)
                    P.op("pool", lambda e: e.tensor_tensor(out=fs[:, a, 0, :].rearrange("p (o c) -> p o c", o=2), in0=fv[:, :, 0, :], in1=fv[:, :, 1, :],
                                                           op=ALU.add), [fk], ["fs"])
                    P.op("pool", lambda e: e.tensor_tensor(out=fs[:, a, 1, :].rearrange("p (o c) -> p o c", o=2), in0=fv[:, :, 1, :], in1=fv[:, :, 0, :],
                                                           op=ALU.subtract), [fk], ["fs"])
                    P.op("act", lambda e: e.activation(out=absf[s][:], in_=filt[:], func=AF.Abs), [fk], ["absfH%d" % s])
                    for o in range(2):
                        P.op("pe", lambda e: e.matmul(pn[o][:], lhsT=onesb[:], rhs=absf[s][:, o * 512:(o + 1) * 512], start=(a == 0),
                                                      stop=(a == nA - 1)), ["absfH%d" % s, "onesb"], ["pnH%d" % o])
                for o in range(2):
                    P.op("act", lambda e: e.copy(out=nsum[:, o, :], in_=pn[o][:, 0:256]), ["pnH%d" % o], ["nsum"])
                    P.op("dve", lambda e: e.tensor_tensor(out=nsum[:, o, :], in0=nsum[:, o, :], in1=pn[o][:, 256:512], op=ALU.add),
                         ["pnH%d" % o, "nsum"], ["nsum"])
                P.op("dve", lambda e: e.reciprocal(out=rn[:], in_=nsum[:]), ["nsum"], ["rn"])
                P.flush()
            with ExitStack() as es:
                ng = max(1, nA // 2)
                GW = min(256, l)
                nfc = GW // 128
                tc = [self.sb(es, "tcF%d" % s, [128, nA, GW], BF16) for s in range(2)]
                ts = [self.sb(es, "tsF%d" % s, [128, nA, GW], BF16) for s in range(2)]
                kfo = [self.sb(es, "kfoF%d" % s, [128, 2, 2, 256]) for s in range(2)]
                pA = [self.ps(es, "pAF%d" % s, [128, 512]) for s in range(4)]
                ctf = S["ctf_%d" % l].rearrange("(a p) f -> p a f", p=128)
                stf = S["stf_%d" % l].rearrange("(a p) f -> p a f", p=128)
                ci = 0
                for g in range(ng):
                    s = g % 2
                    self.dma("sp", tc[s][:], ctf[:, :, g * GW:(g + 1) * GW], writes=["tcF%d" % s])
                    self.dma("sp", ts[s][:], stf[:, :, g * GW:(g + 1) * GW], writes=["tsF%d" % s])
                    for fc in range(nfc):
                        ks = ci % 2
                        ci += 1
                        for a in range(nA):
                            P.op("pe", lambda e: e.matmul(pA[ks * 2][:], lhsT=tc[s][:, a, fc * 128:(fc + 1) * 128], rhs=fs[:, a, 0, :], start=(a == 0),
                                                          stop=(a == nA - 1)), ["tcF%d" % s, "fs"], ["pAF%d" % (ks * 2)])
                            P.op("pe", lambda e: e.matmul(pA[ks * 2 + 1][:], lhsT=ts[s][:, a, fc * 128:(fc + 1) * 128], rhs=fs[:, a, 1, :], start=(a == 0),
                                                          stop=(a == nA - 1)), ["tsF%d" % s, "fs"], ["pAF%d" % (ks * 2 + 1)])
                        for q in range(2):
                            j = ks * 2 + q
                            P.op("dve", lambda e: e.tensor_tensor(out=kfo[ks][:, q, :, :], in0=pA[j][:].rearrange("p (o c) -> p o c", o=2), in1=rn[:],
                                                                  op=ALU.mult), ["pAF%d" % j, "rn"], ["kfoF%d" % ks])
                        self.dma("sp", S["kf_%d" % l][g * nfc + fc], kfo[ks][:], reads=["kfoF%d" % ks])
                P.flush()

    def hy_forward(self, layer, l, o, Z, Y):
        cfg, P, I, S = self.cfg, self.P, self.I, self.S
        nA = l // 128
        with ExitStack() as es:
            ng = max(1, nA // 2)
            GW = min(256, l)
            nfc = GW // 128
            tc = [self.sb(es, "tcW%d" % s, [128, nA, GW], BF16) for s in range(2)]
            ts = [self.sb(es, "tsW%d" % s, [128, nA, GW], BF16) for s in range(2)]
            sA = [self.sb(es, "sAW%d" % s, [128, 2, 256]) for s in range(4)]
            t1 = [self.sb(es, "t1W%d" % s, [128, 2, 256]) for s in range(2)]
            t2 = [self.sb(es, "t2W%d" % s, [128, 2, 256]) for s in range(2)]
            kfc = [self.sb(es, "kfcW%d" % s, [128, 2, 256]) for s in range(2)]
            pA = [self.ps(es, "pAW%d" % s, [128, 512]) for s in range(4)]
            ctf = S["ctf_%d" % l].rearrange("(a p) f -> p a f", p=128)
            stf = S["stf_%d" % l].rearrange("(a p) f -> p a f", p=128)
            ci = 0
            for g in range(ng):
                s = g % 2
                self.dma("sp", tc[s][:], ctf[:, :, g * GW:(g + 1) * GW], writes=["tcW%d" % s])
                self.dma("sp", ts[s][:], stf[:, :, g * GW:(g + 1) * GW], writes=["tsW%d" % s])
                for fc in range(nfc):
                    ks = ci % 2
                    ci += 1
                    fch = g * nfc + fc
                    self.dma("sp", kfc[ks][:], S["kf_%d" % l][fch][:, :, o, :], writes=["kfcW%d" % ks])
                    jc, js = ks * 2, ks * 2 + 1
                    for a in range(nA):
                        P.op("pe", lambda e: e.matmul(pA[jc][:], lhsT=tc[s][:, a, fc * 128:(fc + 1) * 128], rhs=Z[:, a, :], start=(a == 0),
                                                      stop=(a == nA - 1)), ["tcW%d" % s, "Z%d" % a], ["pAW%d" % jc])
                        P.op("pe", lambda e: e.matmul(pA[js][:], lhsT=ts[s][:, a, fc * 128:(fc + 1) * 128], rhs=Z[:, a, :], start=(a == 0),
                                                      stop=(a == nA - 1)), ["tsW%d" % s, "Z%d" % a], ["pAW%d" % js])
                    P.op("act", lambda e: e.copy(out=sA[jc][:].rearrange("p b c -> p (b c)"), in_=pA[jc][:]), ["pAW%d" % jc], ["sAW%d" % jc])
                    P.op("act", lambda e: e.copy(out=sA[js][:].rearrange("p b c -> p (b c)"), in_=pA[js][:]), ["pAW%d" % js], ["sAW%d" % js])
                    kre = kfc[ks][:, 0, :].unsqueeze(1).to_broadcast([128, 2, 256])
                    kim = kfc[ks][:, 1, :].unsqueeze(1).to_broadcast([128, 2, 256])
                    kk = "kfcW%d" % ks
                    yre = Y[:, fch, 0, :].rearrange("p (b c) -> p b c", b=2)
                    yim = Y[:, fch, 1, :].rearrange("p (b c) -> p b c", b=2)
                    P.op("dve", lambda e: e.tensor_tensor(out=t1[0][:], in0=sA[jc][:], in1=kre, op=ALU.mult), ["sAW%d" % jc, kk], ["t1W0"])
                    P.op("pool", lambda e: e.tensor_tensor(out=t2[0][:], in0=sA[js][:], in1=kim, op=ALU.mult), ["sAW%d" % js, kk], ["t2W0"])
                    P.op("dve", lambda e: e.tensor_tensor(out=yre, in0=t1[0][:], in1=t2[0][:], op=ALU.add), ["t1W0", "t2W0"], ["Y"])
                    P.op("dve", lambda e: e.tensor_tensor(out=t1[1][:], in0=sA[js][:], in1=kre, op=ALU.mult), ["sAW%d" % js, kk], ["t1W1"])
                    P.op("pool", lambda e: e.tensor_tensor(out=t2[1][:], in0=sA[jc][:], in1=kim, op=ALU.mult), ["sAW%d" % jc, kk], ["t2W1"])
                    P.op("dve", lambda e: e.tensor_tensor(out=yim, in0=t1[1][:], in1=t2[1][:], op=ALU.subtract), ["t1W1", "t2W1"], ["Y"])
            P.flush()

    def hy_inverse(self, layer, l, tok0, o, Z, Y):
        cfg, P, I, S = self.cfg, self.P, self.I, self.S
        nA = l // 128
        with ExitStack() as es:
            ng = max(1, nA // 2)
            GW = min(256, l)
            ntc = GW // 128
            tc = [self.sb(es, "tcV%d" % s, [128, nA, GW], BF16) for s in range(2)]
            ts = [self.sb(es, "tsV%d" % s, [128, nA, GW], BF16) for s in range(2)]
            hb = self.sb(es, "hbV", [128, 256])
            tz = [self.sb(es, "tzV%d" % s, [128, 2, 256]) for s in range(2)]
            yv = [self.sb(es, "yvV%d" % s, [128, 2, 256]) for s in range(2)]
            xg = [self.sb(es, "xgV%d" % s, [128, 2, 256]) for s in range(2)]
            og = [self.sb(es, "ogV%d" % s, [128, 2, 256], BF16) for s in range(2)]
            py = [self.ps(es, "pyV%d" % s, [128, 512]) for s in range(2)]
            cft = S["cft_%d" % l].rearrange("(a p) t -> p a t", p=128)
            sft = S["sft_%d" % l].rearrange("(a p) t -> p a t", p=128)
            self.dma("sp", hb[:], I["hy_bias"][layer, o].partition_broadcast(128), writes=["hbV"])
            ci = 0
            for g in range(ng):
                s = g % 2
                self.dma("sp", tc[s][:], cft[:, :, g * GW:(g + 1) * GW], writes=["tcV%d" % s])
                self.dma("sp", ts[s][:], sft[:, :, g * GW:(g + 1) * GW], writes=["tsV%d" % s])
                for tcn in range(ntc):
                    ks = ci % 2
                    ci += 1
                    a = g * ntc + tcn
                    r0 = tok0 + a * 128
                    for b in range(cfg.NB):
                        self.dma("sp", xg[ks][:, b, :], S["hx"][b, r0:r0 + 128, o * 256:(o + 1) * 256], writes=["xgV%d" % ks])
                    for fc in range(nA):
                        P.op("pe", lambda e: e.matmul(py[ks][:], lhsT=tc[s][:, fc, tcn * 128:(tcn + 1) * 128], rhs=Y[:, fc, 0, :], start=(fc == 0),
                                                      stop=False), ["tcV%d" % s, "Y"], ["pyV%d" % ks])
                        P.op("pe", lambda e: e.matmul(py[ks][:], lhsT=ts[s][:, fc, tcn * 128:(tcn + 1) * 128], rhs=Y[:, fc, 1, :], start=False,
                                                      stop=(fc == nA - 1)), ["tsV%d" % s, "Y"], ["pyV%d" % ks])
                    zv = Z[:, a, :].rearrange("p (b c) -> p b c", b=2)
                    hbb = hb[:].unsqueeze(1).to_broadcast([128, 2, 256])
                    P.op("pool", lambda e: e.tensor_tensor(out=tz[ks][:], in0=zv, in1=hbb, op=ALU.mult), ["Z%d" % a, "hbV"], ["tzV%d" % ks])
                    P.op("dve", lambda e: e.scalar_tensor_tensor(out=yv[ks][:], in0=py[ks][:].rearrange("p (b c) -> p b c", b=2), scalar=1.0 / l,
                                                                 in1=tz[ks][:], op0=ALU.mult, op1=ALU.add), ["pyV%d" % ks, "tzV%d" % ks], ["yvV%d" % ks])
                    if o == 0:
                        P.op("dve", lambda e: e.tensor_tensor(out=zv, in0=yv[ks][:], in1=xg[ks][:], op=ALU.mult),
                             ["yvV%d" % ks, "xgV%d" % ks], ["Z%d" % a])
                    else:
                        P.op("dve", lambda e: e.tensor_tensor(out=og[ks][:], in0=yv[ks][:], in1=xg[ks][:], op=ALU.mult),
                             ["yvV%d" % ks, "xgV%d" % ks], ["ogV%d" % ks])
                        for b in range(cfg.NB):
                            self.dma("sp", S["ox"][b, r0:r0 + 128, 768:1024], og[ks][:, b, :], reads=["ogV%d" % ks])
            P.flush()

    def bcast_vecs(self, es, tag, layer, bias_name, gate_slot, lng, lnb):
        P, I, S = self.P, self.I, self.S
        V = dict(tag=tag)
        bt = self.sb(es, "bB" + tag, [128, D])
        self.dma("sp", bt[:], I[bias_name][layer].partition_broadcast(128), writes=["bB" + tag])
        V["g"], V["bg"] = [], []
        for j in range(3):
            g = self.sb(es, "gB%s%d" % (tag, j), [128, D])
            bg = self.sb(es, "bgB%s%d" % (tag, j), [128, D])
            self.dma("sp", g[:], S["mod"][j, gate_slot * D:(gate_slot + 1) * D].partition_broadcast(128), writes=["gB%s%d" % (tag, j)])
            P.op("pool", lambda e: e.tensor_tensor(out=bg[:], in0=bt[:], in1=g[:], op=ALU.mult), ["bB" + tag, "gB%s%d" % (tag, j)],
                 ["bgB%s%d" % (tag, j)])
            V["g"].append(g)
            V["bg"].append(bg)
        V["lng"] = self.sb(es, "lngB" + tag, [128, D])
        V["lnb"] = self.sb(es, "lnbB" + tag, [128, D])
        self.dma("sp", V["lng"][:], I[lng][layer].partition_broadcast(128), writes=["lngB" + tag])
        self.dma("sp", V["lnb"][:], I[lnb][layer].partition_broadcast(128), writes=["lnbB" + tag])
        return V

    def resid_ln(self, acc, acck, xt, xk, V, j, Wk, slot, out, outk, aux="pool"):
        P = self.P
        tag = V["tag"]
        t, st, mv, rs = Wk["t"][slot], Wk["st"][slot], Wk["mv"][slot], Wk["rs"][slot]
        sk = "%s%d" % (Wk["tag"], slot)
        alpha = (2.0 * self.cfg.DEPTH) ** 0.25
        for n in range(2):
            P.op("dve", lambda e: e.tensor_tensor(out=t[:, n * 512:(n + 1) * 512], in0=acc[n], in1=V["g"][j][:, n * 512:(n + 1) * 512], op=ALU.mult),
                 [acck[n], "gB%s%d" % (tag, j)], ["t" + sk])
        P.op(aux, lambda e: e.tensor_tensor(out=t[:], in0=t[:], in1=V["bg"][j][:], op=ALU.add), ["t" + sk, "bgB%s%d" % (tag, j)], ["t" + sk])
        P.op("dve", lambda e: e.scalar_tensor_tensor(out=t[:], in0=xt, scalar=alpha, in1=t[:], op0=ALU.mult, op1=ALU.add), [xk, "t" + sk], ["t" + sk])
        for h in range(2):
            P.op("dve", lambda e: e.bn_stats(out=st[:, h, :], in_=t[:, h * 512:(h + 1) * 512]), ["t" + sk], ["st" + sk + str(h)])
        P.op("dve", lambda e: e.bn_aggr(out=mv[:], in_=st[:].rearrange("p a b -> p (a b)")), ["st" + sk + "0", "st" + sk + "1"], ["mv" + sk])
        P.op("act", lambda e: e.activation(out=rs[:], in_=mv[:, 1:2], func=AF.Sqrt, bias=LN_EPS), ["mv" + sk], ["rs" + sk])
        P.op("dve", lambda e: e.reciprocal(out=rs[:], in_=rs[:]), ["rs" + sk], ["rs" + sk])
        P.op("dve", lambda e: e.tensor_scalar(out=t[:], in0=t[:], scalar1=mv[:, 0:1], scalar2=rs[:], op0=ALU.subtract, op1=ALU.mult),
             ["t" + sk, "mv" + sk, "rs" + sk], ["t" + sk])
        P.op(aux, lambda e: e.tensor_tensor(out=t[:], in0=t[:], in1=V["lng"][:], op=ALU.mult), ["t" + sk, "lngB" + tag], ["t" + sk])
        P.op("dve", lambda e: e.tensor_tensor(out=out, in0=t[:], in1=V["lnb"][:], op=ALU.add), ["t" + sk, "lnbB" + tag], [outk])

    def resid_ws(self, es, tag, nslot=2):
        Wk = dict(tag=tag)
        Wk["t"] = [self.sb(es, "tR%s%d" % (tag, s), [128, D]) for s in range(nslot)]
        Wk["st"] = [self.sb(es, "stR%s%d" % (tag, s), [128, 2, 6]) for s in range(nslot)]
        Wk["mv"] = [self.sb(es, "mvR%s%d" % (tag, s), [128, 2]) for s in range(nslot)]
        Wk["rs"] = [self.sb(es, "rsR%s%d" % (tag, s), [128, 1]) for s in range(nslot)]
        return Wk

    def phase_DE1(self, layer, last):
        cfg, P, I, S = self.cfg, self.P, self.I, self.S
        with ExitStack() as es:
            wout = self.sb(es, "wout", [128, 8, D], BF16)
            wff1 = self.sb(es, "wff1", [128, 8, DFF], BF16)
            self.dma("pool", wout[:], I["w_out"][layer].rearrange("(k p) n -> p k n", p=128), writes=["wout"])
            w1v = I["w_ff1"][layer].rearrange("(k p) n -> p k n", p=128)
            for k in range(8):
                self.dma("pool", wff1[:, k, :], w1v[:, k, :], writes=["wff1"])
            bf1 = self.sb(es, "bf1", [128, 32])
            self.dma("sp", bf1[:], I["b_ff1"][layer].rearrange("(c p) -> p c", p=128), writes=["bf1"], allow_slow_non_contiguous=True)
            V = self.bcast_vecs(es, "D", layer, "b_out", 2, "ln1_g", "ln1_b")
            mod2 = self.load_mod_fm(es, "modE", 3, 4)
            W = self.ln_ws(es, "E", "modE", nslot=4, npslot=1)
            Wk = self.resid_ws(es, "D")
            oxb = [self.sb(es, "oxbD%d" % s, [128, D], BF16) for s in range(2)]
            oxT = [self.sb(es, "oxTD%d" % s, [128, 8, 128], BF16) for s in range(2)]
            xt = [self.sb(es, "xtD%d" % s, [128, D]) for s in range(2)]
            x1t = [self.sb(es, "x1tD%d" % s, [128, D]) for s in range(2)]
            xm2T = [self.sb(es, "xm2TD%d" % s, [128, 8, 256], BF16) for s in range(2)]
            rt = [self.sb(es, "rtD%d" % s, [128, 256]) for s in range(2)]
            hT = self.sb(es, "hTD0", [128, 32, 256], BF16)
            ptr = self.ps(es, "ptrD", [128, 8, 128], BF16)
            pw = [[self.ps(es, "pwD%d_%d" % (ti, nn), [128, 512]) for nn in range(2)] for ti in range(2)]
            pf = [self.ps(es, "pfD%d" % s, [128, 512]) for s in range(2)]
            groups = [(b, g) for b in range(cfg.NB) for g in range(cfg.NG) if not (last and g == 0)]
            fi = [0]

            def SA_pe(q):
                b, g = groups[q]
                for ti in range(2):
                    n = q * 2 + ti
                    s = n % 2
                    i = g * 2 + ti
                    r0 = i * 128
                    K = lambda nm: "%sD%d" % (nm, s)
                    self.dma("sp", oxb[s][:], S["ox"][b, r0:r0 + 128, :], writes=[K("oxb")])
                    self.dma("sp", xt[s][:], self.x_src(layer, b, i), writes=[K("xt")])
                    for k in range(8):
                        P.op("pe", lambda e: e.transpose(out=ptr[:, k, :], in_=oxb[s][:, k * 128:(k + 1) * 128], identity=self.idb[:]),
                             [K("oxb"), "idb"], ["ptrD"])
                    P.op("act", lambda e: e.copy(out=oxT[s][:], in_=ptr[:]), ["ptrD"], [K("oxT")])
                    for nn in range(2):
                        for k in range(8):
                            P.op("pe", lambda e: e.matmul(pw[ti][nn][:], lhsT=oxT[s][:, k, :], rhs=wout[:, k, nn * 512:(nn + 1) * 512], start=(k == 0),
                                                          stop=(k == 7)), [K("oxT"), "wout"], ["pwD%d_%d" % (ti, nn)])

            def chain_ops(q):
                b, g = groups[q]
                j = 2 if g == 0 else b
                P.defer = []
                for ti in range(2):
                    n = q * 2 + ti
                    s, s4 = n % 2, n % 4
                    r0 = (g * 2 + ti) * 128
                    K = lambda nm: "%sD%d" % (nm, s)
                    self.resid_ln([pw[ti][0][:], pw[ti][1][:]], ["pwD%d_0" % ti, "pwD%d_1" % ti], xt[s][:], K("xt"), V, j, Wk, s, x1t[s][:], K("x1t"), aux="dve")
                    self.dma("sp", S["xs"][b, r0:r0 + 128, :], x1t[s][:], reads=[K("x1t")])
                    self.ln_chain(x1t[s][:], K("x1t"), W, s4)
                lst = P.defer
                P.defer = None
                out = []
                for it_ in lst:
                    if it_[0] == "act":
                        out += [None] * 3 + [it_] + [None] * 3
                    else:
                        out.append(it_)
                return out

            def SB(q):
                b, g = groups[q]
                j = 2 if g == 0 else b
                for ti in range(2):
                    n = q * 2 + ti
                    self.ln_T(W, n % 4, 0, mod2, j, xm2T[q % 2], "xm2TD%d" % (q % 2), ti * 128)

            def SF(q, pend):
                b, g = groups[q]
                gs = q % 2
                per = (len(pend) + 31) // 32
                for fc in range(32):
                    fs = fi[0] % 2
                    fi[0] += 1
                    hk = "hTD_%d" % (fc // 8)
                    for k in range(8):
                        P.op("pe", lambda e: e.matmul(pf[fs][:, 0:256], lhsT=wff1[:, k, fc * 128:(fc + 1) * 128], rhs=xm2T[gs][:, k, :], start=(k == 0),
                                                      stop=(k == 7)), ["wff1", "xm2TD%d" % gs], ["pfD%d" % fs])
                    P.op("act", lambda e: e.activation(out=rt[fs][:], in_=pf[fs][:, 0:256], func=AF.Relu, bias=bf1[:, fc:fc + 1]),
                         ["pfD%d" % fs, "bf1"], ["rtD%d" % fs])
                    P.op("pool", lambda e: e.tensor_tensor(out=hT[:, fc, :], in0=rt[fs][:], in1=rt[fs][:], op=ALU.mult), ["rtD%d" % fs], [hk])
                    if fc % 8 == 7:
                        c8 = fc // 8
                        self.dma("sp", S["ht"][b].rearrange("c p t -> p c t")[:, c8 * 8:(c8 + 1) * 8, g * 256:(g + 1) * 256],
                                 hT[:, c8 * 8:(c8 + 1) * 8, :], reads=[hk])
                    for _ in range(per):
                        if pend:
                            it_ = pend.pop(0)
                            if it_ is not None:
                                P.add(*it_)
                while pend:
                    it_ = pend.pop(0)
                    if it_ is not None:
                        P.add(*it_)

            NQ = len(groups)
            for q0 in range(min(2, NQ)):
                SA_pe(q0)
                for it_ in chain_ops(q0):
                    if it_ is not None:
                        P.add(*it_)
            SB(0)
            for q in range(NQ):
                pend = []
                if q + 2 < NQ:
                    SA_pe(q + 2)
                    pend = chain_ops(q + 2)
                if q + 1 < NQ:
                    SB(q + 1)
                SF(q, pend)
            P.flush()

    def phase_E2(self, layer, last):
        cfg, P, I, S = self.cfg, self.P, self.I, self.S
        with ExitStack() as es:
            wff2 = self.sb(es, "wff2", [128, 32, D], BF16)
            w2v = I["w_ff2"][layer].rearrange("(k p) n -> p k n", p=128)
            for k4 in range(8):
                self.dma("pool", wff2[:, k4 * 4:(k4 + 1) * 4, :], w2v[:, k4 * 4:(k4 + 1) * 4, :], writes=["wff2"])
            V = self.bcast_vecs(es, "F", layer, "b_ff2", 5, "ln2_g", "ln2_b")
            Wk = self.resid_ws(es, "F")
            hT = [self.sb(es, "hTF%d" % s, [128, 32, 256], BF16) for s in range(2)]
            xt = [self.sb(es, "xtF%d" % s, [128, D]) for s in range(2)]
            x2t = [self.sb(es, "x2tF%d" % s, [128, D]) for s in range(2)]
            po = [self.ps(es, "poF%d" % s, [128, 512]) for s in range(4)]
            it, gi = 0, 0
            nct = cfg.CTX // 128
            for b in range(cfg.NB):
                for g in range(cfg.NG):
                    if last and g == 0:
                        continue
                    j = 2 if g == 0 else b
                    gs = gi % 2
                    gi += 1
                    self.dma("sp", hT[gs][:], S["ht"][b].rearrange("c p t -> p c t")[:, :, g * 256:(g + 1) * 256], writes=["hTF%d" % gs])
                    for ti in range(2):
                        s = it % 2
                        it += 1
                        i = g * 2 + ti
                        r0 = i * 128
                        K = lambda n: "%sF%d" % (n, s)
                        self.dma("sp", xt[s][:], S["xs"][b, r0:r0 + 128, :], reads=["xs%d_%d" % (b, i)], writes=[K("xt")])
                        for n in range(2):
                            pb = s * 2 + n
                            for k in range(32):
                                P.op("pe", lambda e: e.matmul(po[pb][:], lhsT=hT[gs][:, k, ti * 128:(ti + 1) * 128], rhs=wff2[:, k, n * 512:(n + 1) * 512],
                                                              start=(k == 0), stop=(k == 31)), ["hTF%d" % gs, "wff2"], ["poF%d" % pb])
                        self.resid_ln([po[s * 2][:], po[s * 2 + 1][:]], ["poF%d" % (s * 2), "poF%d" % (s * 2 + 1)], xt[s][:], K("xt"), V, j, Wk, s,
                                      x2t[s][:], K("x2t"))
                        if last:
                            dst = self.out[b, (i - nct) * 128:(i - nct + 1) * 128, :]
                        else:
                            dst = S["xs"][b, r0:r0 + 128, :]
                        self.dma("sp", dst, x2t[s][:], reads=[K("x2t")], writes=["xs%d_%d" % (b, i)])
            P.flush()


def build_program(cfg):
    bld = Builder(cfg)
    nc = bld.build()
    return bld, nc


_CACHE = {}


def kernel(**inputs):
    cfg = Cfg(SEQ=4096, CTX=256, DEPTH=4, NB=2)
    n_cores = 8
    if "prog" not in _CACHE:
        _CACHE["prog"] = build_program(cfg)
    bld, nc = _CACHE["prog"]
    consts = host_consts(cfg)
    in_maps = []
    for c in range(n_cores):
        m = {}
        sl = slice(c * cfg.NB, (c + 1) * cfg.NB)
        m["x"] = np.ascontiguousarray(inputs["x"][sl], dtype=np.float32)
        m["ctx"] = np.ascontiguousarray(inputs["ctx"][sl], dtype=np.float32)
        m["c"] = np.ascontiguousarray(inputs["c"][sl], dtype=np.float32)
        m["c_ctx"] = np.ascontiguousarray(inputs["c_ctx"], dtype=np.float32)
        for n, _ in WSPEC:
            m[n] = np.ascontiguousarray(inputs[n], dtype=np.float32)
        m.update(consts)
        in_maps.append(m)
    res = run_bass_kernel_spmd(nc, in_maps, core_ids=list(range(n_cores)))
    return np.concatenate([np.asarray(r["out"], dtype=np.float32) for r in res.results], axis=0)
```
